# Optimizing a Trainium2 kernel written in Bass

```python
import math
import jax, jax.numpy as jnp
from jax import lax
import numpy as np

D_MODEL = 2048
BATCH = 4
SEQ = 2048
DEPTH = 4
DEC_BATCH = 8
DEC_SEQ = 8
PAST_LEN = 16384
PAGE_SIZE = 128

N_EVEN = (DEPTH + 1) // 2
N_ODD = DEPTH // 2
EPS = 1e-6
FOX_HEADS = 8
FOX_HD = 128
FOX_W = FOX_HEADS * FOX_HD
Q_BLOCK = 128
SSD_D_INNER = D_MODEL
SSD_HD = 64
SSD_HEADS = SSD_D_INNER // SSD_HD
SSD_GROUPS = 4
SSD_STATE = 128
SSD_CONV = 4
SSD_CHUNK = 128
SSD_CONV_DIM = SSD_D_INNER + 2 * SSD_GROUPS * SSD_STATE
SPLIT_EVEN = (FOX_W, 2 * FOX_W, 3 * FOX_W, 3 * FOX_W + FOX_HEADS,
              3 * FOX_W + FOX_HEADS + SSD_D_INNER,
              3 * FOX_W + FOX_HEADS + SSD_D_INNER + SSD_CONV_DIM)
EVEN_IN = 3 * FOX_W + FOX_HEADS + SSD_D_INNER + SSD_CONV_DIM + SSD_HEADS
EVEN_MIX = FOX_W + SSD_D_INNER
S5_W = D_MODEL
S5_GROUP = 16
S5_GROUPS = S5_W // S5_GROUP
S5_STATE = 64
MEM_TOKENS = 256
MEM_HEADS = 4
MEM_HD = 128
MEM_W = MEM_HEADS * MEM_HD
FFN_HIDDEN = -(-8 * D_MODEL // (3 * 256)) * 256

kernel_name = 'fox_ssd_s5_hybrid_decode_step'


def rms_norm(x, g):
    xf = x.astype(jnp.float32)
    y = xf * lax.rsqrt(jnp.mean(xf * xf, axis=-1, keepdims=True) + EPS)
    return (y * g.astype(jnp.float32)).astype(x.dtype)


def swiglu_ffn(x, w_up, w_down):
    g, u = jnp.split(x @ w_up, 2, axis=-1)
    return (jax.nn.silu(g) * u) @ w_down


def memory_kv(mem, g_mem, w_mkv, k_norm_g):
    b, m, _ = mem.shape
    k, v = jnp.split(rms_norm(mem, g_mem) @ w_mkv, 2, axis=-1)
    k = rms_norm(k.reshape(b, m, MEM_HEADS, MEM_HD), k_norm_g)
    return k, v.reshape(b, m, MEM_HEADS, MEM_HD)


def memory_cross_attention(xn, w_mq, q_norm_g, mem_k, mem_v, w_mo):
    b, l, _ = xn.shape
    q = rms_norm((xn @ w_mq).reshape(b, l, MEM_HEADS, MEM_HD), q_norm_g)
    s = jnp.einsum('bqhd,bkhd->bhqk', q, mem_k).astype(jnp.float32) * MEM_HD ** -0.5
    p = jax.nn.softmax(s, axis=-1).astype(mem_v.dtype)
    o = jnp.einsum('bhqk,bkhd->bqhd', p, mem_v).reshape(b, l, MEM_W)
    return o @ w_mo


def fox_attention_prompt(q, k, v, log_f):
    b, l, h, d = q.shape
    n_blk = l // Q_BLOCK
    c = jnp.cumsum(log_f, axis=1).transpose(0, 2, 1)
    q_blocks = q.reshape(b, n_blk, Q_BLOCK, h, d).transpose(1, 0, 2, 3, 4)
    c_blocks = c.reshape(b, h, n_blk, Q_BLOCK).transpose(2, 0, 1, 3)
    k_pos = jnp.arange(l)

    def block(args):
        i, q_i, c_i = args
        q_pos = i * Q_BLOCK + jnp.arange(Q_BLOCK)
        s = jnp.einsum('bqhd,bkhd->bhqk', q_i, k).astype(jnp.float32) * d ** -0.5
        s = s + (c_i[..., :, None] - c[..., None, :])
        s = jnp.where(k_pos[None, :] <= q_pos[:, None], s, -jnp.inf)
        p = jax.nn.softmax(s, axis=-1).astype(v.dtype)
        return jnp.einsum('bhqk,bkhd->bqhd', p, v)

    o = lax.map(block, (jnp.arange(n_blk), q_blocks, c_blocks))
    return o.transpose(1, 0, 2, 3, 4).reshape(b, l, h, d)


def fox_attention_sample(q, k, v, log_f, k_past, v_past, log_f_past):
    b, t, h, d = q.shape
    n_past = k_past.shape[1]
    scale = d ** -0.5
    c_new = jnp.cumsum(log_f, axis=1).transpose(0, 2, 1)
    lp = log_f_past.astype(jnp.float32)
    tail = (lax.cumsum(lp, axis=1, reverse=True) - lp).transpose(0, 2, 1)
    s_past = jnp.einsum('bthd,bshd->bhts', q, k_past).astype(jnp.float32) * scale
    s_past = s_past + c_new[..., :, None] + tail[..., None, :]
    s_new = jnp.einsum('bthd,bshd->bhts', q, k).astype(jnp.float32) * scale
    s_new = s_new + (c_new[..., :, None] - c_new[..., None, :])
    s_new = jnp.where(jnp.tril(jnp.ones((t, t), dtype=bool)), s_new, -jnp.inf)
    p = jax.nn.softmax(jnp.concatenate([s_past, s_new], axis=-1), axis=-1).astype(v.dtype)
    return (jnp.einsum('bhts,bshd->bthd', p[..., :n_past], v_past)
            + jnp.einsum('bhts,bshd->bthd', p[..., n_past:], v))


def causal_depthwise_conv(u, buf, w, bias):
    l = u.shape[1]
    full = jnp.concatenate([buf.astype(u.dtype), u], axis=1)
    y = bias + sum(full[:, k:k + l] * w[k] for k in range(SSD_CONV))
    return y, full[:, l:]


def ssd_chunked(x, dt, A, Bm, Cm, h0):
    f32 = jnp.float32
    b, l, H, P = x.shape
    G, N = SSD_GROUPS, SSD_STATE
    R = H // G
    q = SSD_CHUNK if l % SSD_CHUNK == 0 else l
    c = l // q
    a = (dt * A).reshape(b, c, q, G, R)
    xdt = (x.astype(f32) * dt[..., None]).reshape(b, c, q, G, R, P)
    Bc = Bm.astype(f32).reshape(b, c, q, G, N)
    Cc = Cm.astype(f32).reshape(b, c, q, G, N)
    a_cum = jnp.cumsum(a, axis=2)
    causal = jnp.tril(jnp.ones((q, q), dtype=bool))[None, None, :, :, None, None]
    seg = a_cum[:, :, :, None] - a_cum[:, :, None, :]
    decay_in = jnp.exp(jnp.where(causal, seg, -jnp.inf))
    cb = jnp.einsum('bcqgn,bcsgn->bcqsg', Cc, Bc)
    y_diag = jnp.einsum('bcqsg,bcqsgr,bcsgrp->bcqgrp', cb, decay_in, xdt)
    decay_out = jnp.exp(a_cum[:, :, -1:] - a_cum)
    states = jnp.einsum('bcsgn,bcsgr,bcsgrp->bcgrpn', Bc, decay_out, xdt)
    chunk_decay = jnp.exp(a_cum[:, :, -1])

    def step(h, inp):
        st, dec = inp
        return h * dec[..., None, None] + st, h

    h_last, h_start = lax.scan(step, h0.astype(f32).reshape(b, G, R, P, N),
                               (jnp.moveaxis(states, 1, 0), jnp.moveaxis(chunk_decay, 1, 0)))
    h_start = jnp.moveaxis(h_start, 0, 1)
    y_off = jnp.einsum('bcqgn,bcgrpn,bcqgr->bcqgrp', Cc, h_start, jnp.exp(a_cum))
    return (y_diag + y_off).reshape(b, l, H, P), h_last.reshape(b, H, P, N)


def _complex_affine_combine(e1, e2):
    a1r, a1i, b1r, b1i = e1
    a2r, a2i, b2r, b2i = e2
    return (a2r * a1r - a2i * a1i, a2r * a1i + a2i * a1r,
            a2r * b1r - a2i * b1i + b2r, a2r * b1i + a2i * b1r + b2i)


def s5_scan(u, A_re, A_im, B_re, B_im, C_re, C_im, D_skip, log_dt, x0_re, x0_im):
    f32 = jnp.float32
    b, l, w = u.shape
    uf = u.astype(f32).reshape(b, l, S5_GROUPS, S5_GROUP)
    lam_re = jnp.minimum(A_re.astype(f32), -1e-4)
    lam_im = A_im.astype(f32)
    dt = jnp.exp(log_dt.astype(f32))[:, None]
    mag = jnp.exp(lam_re * dt)
    ang = lam_im * dt
    lb_re, lb_im = mag * jnp.cos(ang), mag * jnp.sin(ang)
    nr, ni = lb_re - 1.0, lb_im
    den = lam_re * lam_re + lam_im * lam_im
    coef_re = (nr * lam_re + ni * lam_im) / den
    coef_im = (ni * lam_re - nr * lam_im) / den
    Br, Bi = B_re.astype(f32), B_im.astype(f32)
    bb_re = coef_re[..., None] * Br - coef_im[..., None] * Bi
    bb_im = coef_re[..., None] * Bi + coef_im[..., None] * Br
    bu_re = jnp.einsum('blgk,gnk->blgn', uf, bb_re)
    bu_im = jnp.einsum('blgk,gnk->blgn', uf, bb_im)
    a_re = jnp.broadcast_to(lb_re, (1, l) + lb_re.shape)
    a_im = jnp.broadcast_to(lb_im, (1, l) + lb_im.shape)
    ac_re, ac_im, s_re, s_im = lax.associative_scan(
        _complex_affine_combine, (a_re, a_im, bu_re, bu_im), axis=1)
    x0r = x0_re.astype(f32)[:, None]
    x0i = x0_im.astype(f32)[:, None]
    s_re, s_im = (s_re + ac_re * x0r - ac_im * x0i, s_im + ac_re * x0i + ac_im * x0r)
    y = (jnp.einsum('blgn,gkn->blgk', s_re, C_re.astype(f32))
         - jnp.einsum('blgn,gkn->blgk', s_im, C_im.astype(f32)))
    y = y.reshape(b, l, w) + D_skip.astype(f32) * u.astype(f32)
    return y.astype(u.dtype), s_re[:, -1], s_im[:, -1]


def even_layer_mixer(xn, W, j, conv_buf, ssm0, fox_past):
    b, l, _ = xn.shape
    proj = xn @ W['w_in_even'][j]
    q, k, v, f_logit, z, xbc, dt_raw = jnp.split(proj, SPLIT_EVEN, axis=-1)
    q = rms_norm(q.reshape(b, l, FOX_HEADS, FOX_HD), W['fox_q_norm'][j])
    k = rms_norm(k.reshape(b, l, FOX_HEADS, FOX_HD), W['fox_k_norm'][j])
    v = v.reshape(b, l, FOX_HEADS, FOX_HD)
    log_f = jax.nn.log_sigmoid(f_logit.astype(jnp.float32) + W['fox_b_forget'][j].astype(jnp.float32))
    if fox_past is None:
        o_fox = fox_attention_prompt(q, k, v, log_f)
    else:
        o_fox = fox_attention_sample(q, k, v, log_f, *fox_past)
    xbc, new_buf = causal_depthwise_conv(xbc, conv_buf, W['ssd_conv_w'][j], W['ssd_conv_b'][j])
    xbc = jax.nn.silu(xbc)
    xs, Bm, Cm = jnp.split(xbc, [SSD_D_INNER, SSD_D_INNER + SSD_GROUPS * SSD_STATE], axis=-1)
    xs = xs.reshape(b, l, SSD_HEADS, SSD_HD)
    dt = jax.nn.softplus(dt_raw.astype(jnp.float32) + W['ssd_dt_bias'][j].astype(jnp.float32))
    A = -jnp.exp(W['ssd_A_log'][j].astype(jnp.float32))
    y, h = ssd_chunked(xs, dt, A, Bm.reshape(b, l, SSD_GROUPS, SSD_STATE),
                       Cm.reshape(b, l, SSD_GROUPS, SSD_STATE), ssm0)
    y = (y.astype(xs.dtype) + W['ssd_D'][j][:, None] * xs).reshape(b, l, SSD_D_INNER)
    y = rms_norm(y * jax.nn.silu(z), W['ssd_norm'][j])
    out = jnp.concatenate([o_fox.reshape(b, l, FOX_W), y], axis=-1) @ W['w_out_even'][j]
    return out, k, v, log_f, h, new_buf


def odd_layer_mixer(xn, W, j, re0, im0):
    u = xn @ W['w_in_odd'][j]
    y, s_re, s_im = s5_scan(u, W['s5_A_re'][j], W['s5_A_im'][j], W['s5_B_re'][j], W['s5_B_im'][j],
                            W['s5_C_re'][j], W['s5_C_im'][j], W['s5_D'][j], W['s5_log_dt'][j], re0, im0)
    g = jax.nn.gelu(y)
    a, gate = jnp.split(g @ W['s5_w_glu'][j], 2, axis=-1)
    return a * jax.nn.sigmoid(gate), s_re, s_im


def gather_pages(pool, j, page_table):
    g = pool[j, page_table]
    return g.reshape((g.shape[0], g.shape[1] * g.shape[2]) + g.shape[3:])


def trunk(x, W, mem_k, mem_v, conv0, ssm0, s5_re0, s5_im0, fox_cache):
    fk, fv, fl, hs, bufs, srs, sis = [], [], [], [], [], [], []
    for i in range(DEPTH):
        j = i // 2
        xn = rms_norm(x, W['norm_mix'][i])
        if i % 2 == 0:
            past = None
            if fox_cache is not None:
                k_pool, v_pool, lf_pool, page_table = fox_cache
                past = (gather_pages(k_pool, j, page_table), gather_pages(v_pool, j, page_table),
                        gather_pages(lf_pool, j, page_table))
            out, k, v, lf, h, buf = even_layer_mixer(xn, W, j, conv0[j], ssm0[j], past)
            fk.append(k); fv.append(v); fl.append(lf); hs.append(h); bufs.append(buf)
        else:
            out, sr, si = odd_layer_mixer(xn, W, j, s5_re0[j], s5_im0[j])
            srs.append(sr); sis.append(si)
        x = x + out
        x = x + memory_cross_attention(rms_norm(x, W['norm_cross'][i]), W['w_mq'][i], W['mem_q_norm'][i],
                                       mem_k[i], mem_v[i], W['w_mo'][i])
        x = x + swiglu_ffn(rms_norm(x, W['norm_ffn'][i]), W['w_ffn_up'][i], W['w_ffn_down'][i])
    return (x, jnp.stack(fk), jnp.stack(fv), jnp.stack(fl), jnp.stack(hs), jnp.stack(bufs),
            jnp.stack(srs), jnp.stack(sis))


def setup_inputs(seed: int = 0) -> dict:
    key = jax.random.key(seed)
    keys = jax.random.split(key, 64)
    counter = [0]

    def nk():
        counter[0] += 1
        return keys[counter[0] - 1]

    def nrm(shape, scale=1.0):
        return jax.random.normal(nk(), shape, jnp.float32) * scale

    def gain(shape):
        return 1.0 + nrm(shape, 0.02)

    n_pages = PAST_LEN // PAGE_SIZE
    n_used = DEC_BATCH * n_pages
    n_pool = n_used + (n_used + 3) // 4
    page_table = jax.random.permutation(nk(), n_pool)[:n_used].reshape(DEC_BATCH, n_pages).astype(jnp.int32)
    ssd_dt0 = jnp.exp(jax.random.uniform(nk(), (N_EVEN, SSD_HEADS), jnp.float32,
                                         math.log(1e-3), math.log(1e-1)))
    return {
        'x_prompt': nrm((BATCH, SEQ, D_MODEL)),
        'x_sample': nrm((DEC_BATCH, DEC_SEQ, D_MODEL)),
        'mem_prompt': nrm((BATCH, MEM_TOKENS, D_MODEL)),
        'cache_fox_k': nrm((N_EVEN, n_pool, PAGE_SIZE, FOX_HEADS, FOX_HD)),
        'cache_fox_v': nrm((N_EVEN, n_pool, PAGE_SIZE, FOX_HEADS, FOX_HD)),
        'cache_fox_logf': jax.nn.log_sigmoid(4.0 + nrm((N_EVEN, n_pool, PAGE_SIZE, FOX_HEADS))),
        'cache_mem_k': nrm((DEPTH, DEC_BATCH, MEM_TOKENS, MEM_HEADS, MEM_HD)),
        'cache_mem_v': nrm((DEPTH, DEC_BATCH, MEM_TOKENS, MEM_HEADS, MEM_HD)),
        'state_ssd': nrm((N_EVEN, DEC_BATCH, SSD_HEADS, SSD_HD, SSD_STATE), 0.1),
        'state_conv': nrm((N_EVEN, DEC_BATCH, SSD_CONV - 1, SSD_CONV_DIM)),
        'state_s5_re': nrm((N_ODD, DEC_BATCH, S5_GROUPS, S5_STATE), 0.1),
        'state_s5_im': nrm((N_ODD, DEC_BATCH, S5_GROUPS, S5_STATE), 0.1),
        'page_table': page_table,
        'norm_mix': gain((DEPTH, D_MODEL)),
        'norm_cross': gain((DEPTH, D_MODEL)),
        'norm_mem': gain((DEPTH, D_MODEL)),
        'norm_ffn': gain((DEPTH, D_MODEL)),
        'w_in_even': nrm((N_EVEN, D_MODEL, EVEN_IN), D_MODEL ** -0.5),
        'fox_b_forget': 4.0 + nrm((N_EVEN, FOX_HEADS), 0.5),
        'fox_q_norm': gain((N_EVEN, FOX_HD)),
        'fox_k_norm': gain((N_EVEN, FOX_HD)),
        'ssd_conv_w': nrm((N_EVEN, SSD_CONV, SSD_CONV_DIM), 0.5),
        'ssd_conv_b': nrm((N_EVEN, SSD_CONV_DIM), 0.02),
        'ssd_dt_bias': ssd_dt0 + jnp.log(-jnp.expm1(-ssd_dt0)),
        'ssd_A_log': jnp.log(jax.random.uniform(nk(), (N_EVEN, SSD_HEADS), jnp.float32, 1.0, 16.0)),
        'ssd_D': 1.0 + nrm((N_EVEN, SSD_HEADS), 0.1),
        'ssd_norm': gain((N_EVEN, SSD_D_INNER)),
        'w_out_even': nrm((N_EVEN, EVEN_MIX, D_MODEL), EVEN_MIX ** -0.5),
        'w_in_odd': nrm((N_ODD, D_MODEL, S5_W), D_MODEL ** -0.5),
        's5_A_re': -0.5 + nrm((N_ODD, S5_GROUPS, S5_STATE), 0.01),
        's5_A_im': jnp.pi * jnp.arange(S5_STATE, dtype=jnp.float32) + nrm((N_ODD, S5_GROUPS, S5_STATE), 0.01),
        's5_B_re': nrm((N_ODD, S5_GROUPS, S5_STATE, S5_GROUP), (2 * S5_GROUP) ** -0.5),
        's5_B_im': nrm((N_ODD, S5_GROUPS, S5_STATE, S5_GROUP), (2 * S5_GROUP) ** -0.5),
        's5_C_re': nrm((N_ODD, S5_GROUPS, S5_GROUP, S5_STATE), S5_STATE ** -0.5),
        's5_C_im': nrm((N_ODD, S5_GROUPS, S5_GROUP, S5_STATE), S5_STATE ** -0.5),
        's5_D': 1.0 + nrm((N_ODD, S5_W), 0.1),
        's5_log_dt': jax.random.uniform(nk(), (N_ODD, S5_GROUPS), jnp.float32, math.log(1e-3), math.log(1e-1)),
        's5_w_glu': nrm((N_ODD, S5_W, 2 * D_MODEL), S5_W ** -0.5),
        'w_mq': nrm((DEPTH, D_MODEL, MEM_W), D_MODEL ** -0.5),
        'w_mkv': nrm((DEPTH, D_MODEL, 2 * MEM_W), D_MODEL ** -0.5),
        'mem_q_norm': gain((DEPTH, MEM_HD)),
        'mem_k_norm': gain((DEPTH, MEM_HD)),
        'w_mo': nrm((DEPTH, MEM_W, D_MODEL), MEM_W ** -0.5),
        'w_ffn_up': nrm((DEPTH, D_MODEL, 2 * FFN_HIDDEN), D_MODEL ** -0.5),
        'w_ffn_down': nrm((DEPTH, FFN_HIDDEN, D_MODEL), FFN_HIDDEN ** -0.5),
    }


def reference(x_prompt, x_sample, mem_prompt, cache_fox_k, cache_fox_v, cache_fox_logf, cache_mem_k, cache_mem_v,
              state_ssd, state_conv, state_s5_re, state_s5_im, page_table,
              norm_mix, norm_cross, norm_mem, norm_ffn,
              w_in_even, fox_b_forget, fox_q_norm, fox_k_norm, ssd_conv_w, ssd_conv_b, ssd_dt_bias, ssd_A_log,
              ssd_D, ssd_norm, w_out_even,
              w_in_odd, s5_A_re, s5_A_im, s5_B_re, s5_B_im, s5_C_re, s5_C_im, s5_D, s5_log_dt, s5_w_glu,
              w_mq, w_mkv, mem_q_norm, mem_k_norm, w_mo, w_ffn_up, w_ffn_down):
    W = {
        'norm_mix': norm_mix, 'norm_cross': norm_cross, 'norm_ffn': norm_ffn,
        'w_in_even': w_in_even, 'fox_b_forget': fox_b_forget, 'fox_q_norm': fox_q_norm, 'fox_k_norm': fox_k_norm,
        'ssd_conv_w': ssd_conv_w, 'ssd_conv_b': ssd_conv_b, 'ssd_dt_bias': ssd_dt_bias, 'ssd_A_log': ssd_A_log,
        'ssd_D': ssd_D, 'ssd_norm': ssd_norm, 'w_out_even': w_out_even,
        'w_in_odd': w_in_odd, 's5_A_re': s5_A_re, 's5_A_im': s5_A_im, 's5_B_re': s5_B_re, 's5_B_im': s5_B_im,
        's5_C_re': s5_C_re, 's5_C_im': s5_C_im, 's5_D': s5_D, 's5_log_dt': s5_log_dt, 's5_w_glu': s5_w_glu,
        'w_mq': w_mq, 'mem_q_norm': mem_q_norm, 'w_mo': w_mo, 'w_ffn_up': w_ffn_up, 'w_ffn_down': w_ffn_down,
    }
    b = x_prompt.shape[0]
    mk, mv = [], []
    for i in range(DEPTH):
        k_i, v_i = memory_kv(mem_prompt, norm_mem[i], w_mkv[i], mem_k_norm[i])
        mk.append(k_i); mv.append(v_i)
    mem_k_p = jnp.stack(mk)
    mem_v_p = jnp.stack(mv)
    (y_prompt, fox_k_p, fox_v_p, fox_logf_p, ssd_p, conv_p, s5_re_p, s5_im_p) = trunk(
        x_prompt, W, mem_k_p, mem_v_p,
        jnp.zeros((N_EVEN, b, SSD_CONV - 1, SSD_CONV_DIM), x_prompt.dtype),
        jnp.zeros((N_EVEN, b, SSD_HEADS, SSD_HD, SSD_STATE), jnp.float32),
        jnp.zeros((N_ODD, b, S5_GROUPS, S5_STATE), jnp.float32),
        jnp.zeros((N_ODD, b, S5_GROUPS, S5_STATE), jnp.float32),
        None)
    (y_sample, fox_k_s, fox_v_s, fox_logf_s, ssd_s, conv_s, s5_re_s, s5_im_s) = trunk(
        x_sample, W, cache_mem_k, cache_mem_v, state_conv, state_ssd, state_s5_re, state_s5_im,
        (cache_fox_k, cache_fox_v, cache_fox_logf, page_table))
    return (y_prompt, y_sample,
            fox_k_p, fox_v_p, fox_logf_p, mem_k_p, mem_v_p, ssd_p, conv_p, s5_re_p, s5_im_p,
            fox_k_s, fox_v_s, fox_logf_s, ssd_s, conv_s, s5_re_s, s5_im_s)
```

```python
import contextlib
import math
import numpy as np
import concourse.bass as bass
import concourse.mybir as mybir
from concourse.bass_utils import run_bass_kernel_spmd

F32 = mybir.dt.float32
BF16 = mybir.dt.bfloat16
I32 = mybir.dt.int32
AF = mybir.ActivationFunctionType
ALU = mybir.AluOpType
AX = mybir.AxisListType

D = 2048
KC = 16
EPS = 1e-6
SEM_LIMIT = 30000
EMBED_WAIT = True
NSLOT = 6
FFN_H = 5632
EVEN_IN = 8232
TWO_PI = 2.0 * math.pi
MAGIC = 12582912.0


class TA:
    def __init__(self, ap, key):
        self.ap = ap
        self.key = key


def _isap(a):
    return hasattr(a, "tensor") and hasattr(a, "ap") and hasattr(a, "offset")


class KB:
    def __init__(self):
        self.nc = bass.Bass("TRN2", target_bir_lowering=False)
        self.es = contextlib.ExitStack()
        self.ses = contextlib.ExitStack()
        nc = self.nc
        self.engs = {"pe": nc.tensor, "act": nc.scalar, "dve": nc.vector, "pool": nc.gpsimd, "sp": nc.sync}
        self.semh = {}
        self.nsem = 0
        self.sem = {}
        self.cnt = {}
        self.seen = {e: {} for e in self.engs}
        self.retired = set()
        for e in self.engs:
            self.sem[e] = self._sem()
            self.cnt[e] = 0
        self.dq = {q: {"slots": [[self._sem(), 0] for _ in range(NSLOT)], "i": 0} for q in ("sp", "pool", "act")}
        self.bufs = {}
        self.rot = {}
        self.ninst = 0

    def _sem(self):
        h = self.ses.enter_context(self.nc.semaphore("s%d" % self.nsem))
        uid = self.nsem
        self.nsem += 1
        self.semh[uid] = h
        return uid

    def sb(self, name, shape, dt):
        self.nalloc = getattr(self, "nalloc", 0) + 1
        return self.es.enter_context(self.nc.sbuf_tensor("%s_%d" % (name, self.nalloc), list(shape), dt))

    def psum(self, name, shape, dt):
        return self.es.enter_context(self.nc.psum_tensor(name, list(shape), dt))

    def dram(self, name, shape, dt, kind):
        return self.nc.dram_tensor(name, list(shape), dt, kind=kind).ap()

    def rotating(self, name, n, shape, dt):
        self.rot[name] = [[self.sb("%s_%d" % (name, i), shape, dt) for i in range(n)], 0]

    def nxt(self, name):
        r = self.rot[name]
        t = r[0][r[1] % len(r[0])]
        r[1] += 1
        return t

    def _retire(self, uid, final):
        for e2 in self.engs:
            if self.seen[e2].get(uid, 0) < final:
                self.engs[e2].wait_ge(self.semh[uid], final)
        for e2 in self.engs:
            self.seen[e2].pop(uid, None)
        self.retired.add(uid)

    def _emit_waits(self, e, toks, embed=False):
        need = {}
        for uid, v in toks:
            if uid in self.retired:
                continue
            if need.get(uid, 0) < v:
                need[uid] = v
        todo = []
        for uid, v in need.items():
            if self.seen[e].get(uid, 0) >= v:
                continue
            if e == "pe" and uid == self.sem["pe"]:
                continue
            todo.append((uid, v))
        last = None
        if embed and EMBED_WAIT and todo:
            last = todo.pop()
        for uid, v in todo:
            self.engs[e].wait_ge(self.semh[uid], v)
            self.seen[e][uid] = v
        if last is not None:
            self.seen[e][last[0]] = last[1]
        return last

    def _collect(self, args, kw):
        reads, writes, nargs, nkw = [], [], [], {}

        def key_of(a):
            if isinstance(a, TA):
                return (a.ap.tensor.name, a.key), a.ap
            return (a.tensor.name, None), a

        for i, a in enumerate(args):
            if isinstance(a, TA) or _isap(a):
                k, ap = key_of(a)
                (writes if i == 0 else reads).append(k)
                nargs.append(ap)
            else:
                nargs.append(a)
        for kk, a in kw.items():
            if isinstance(a, TA) or _isap(a):
                k, ap = key_of(a)
                (writes if kk in ("out", "accum_out") else reads).append(k)
                nkw[kk] = ap
            else:
                nkw[kk] = a
        return reads, writes, nargs, nkw

    def _deps(self, reads, writes):
        toks = []
        for k in reads:
            b = self.bufs.get(k)
            if b:
                toks += b["w"]
        for k in writes:
            b = self.bufs.get(k)
            if b:
                toks += b["w"]
                toks += list(b["r"].items())
        return toks

    def _record(self, reads, writes, tok):
        for k in reads:
            b = self.bufs.setdefault(k, {"w": [], "r": {}})
            if b["r"].get(tok[0], 0) < tok[1]:
                b["r"][tok[0]] = tok[1]
        for k in writes:
            self.bufs[k] = {"w": [tok], "r": {}}

    def I(self, e, meth, *args, **kw):
        xr = kw.pop("_R", ())
        xw = kw.pop("_W", ())
        reads, writes, nargs, nkw = self._collect(args, kw)
        reads += list(xr)
        writes += list(xw)
        if self.cnt[e] >= SEM_LIMIT:
            self._retire(self.sem[e], self.cnt[e])
            self.sem[e] = self._sem()
            self.cnt[e] = 0
        last = self._emit_waits(e, self._deps(reads, writes), embed=True)
        ins = getattr(self.engs[e], meth)(*nargs, **nkw)
        if last is not None:
            ins._wait_ge(self.semh[last[0]], last[1])
        self.cnt[e] += 1
        ins.then_inc(self.semh[self.sem[e]], 1)
        self._record(reads, writes, (self.sem[e], self.cnt[e]))
        self.ninst += 1
        return ins

    def dma(self, q, out, in_, meth="dma_start", xr=(), **kw):
        reads, writes, _, nkw = self._collect((), dict(out=out, in_=in_, **kw))
        reads += list(xr)
        dq = self.dq[q]
        slot = dq["slots"][dq["i"] % NSLOT]
        dq["i"] += 1
        if slot[1] + 16 > SEM_LIMIT:
            self._retire(slot[0], slot[1])
            slot[0] = self._sem()
            slot[1] = 0
        toks = self._deps(reads, writes)
        if slot[1] > 0:
            toks.append((slot[0], slot[1]))
        last = self._emit_waits(q, toks, embed=True)
        ins = getattr(self.engs[q], meth)(**nkw)
        if last is not None:
            ins._wait_ge(self.semh[last[0]], last[1])
        slot[1] += 16
        ins.then_inc(self.semh[slot[0]], 16)
        self._record(reads, writes, (slot[0], slot[1]))
        self.ninst += 1
        return ins

    def barrier(self):
        toks = []
        for e in self.engs:
            if self.cnt[e] > 0:
                toks.append((self.sem[e], self.cnt[e]))
        for q, dq in self.dq.items():
            for uid, tgt in dq["slots"]:
                if tgt > 0:
                    toks.append((uid, tgt))
        for e in self.engs:
            self._emit_waits(e, toks)
        self.bufs = {}

    @contextlib.contextmanager
    def scope(self):
        outer = self.es
        outer_rot = dict(self.rot)
        self.es = contextlib.ExitStack()
        try:
            yield
        finally:
            self.barrier()
            self.es.close()
            self.es = outer
            self.rot = outer_rot

    def finish(self):
        for q, dq in self.dq.items():
            for uid, tgt in dq["slots"]:
                if tgt > 0 and uid not in self.retired and self.seen["sp"].get(uid, 0) < tgt:
                    self.engs["sp"].wait_ge(self.semh[uid], tgt)
        for e in self.engs:
            if self.cnt[e] > 0 and e != "sp":
                self.engs["sp"].wait_ge(self.semh[self.sem[e]], self.cnt[e])


class Prog:
    def __init__(self, cfg):
        self.cfg = cfg
        self.k = KB()
        k = self.k
        self.T = cfg.get("T", 256)
        T = self.T
        self.SEQ = cfg.get("SEQ", 2048)
        self.depth = cfg.get("depth", 4)
        self.parts = cfg.get("parts", ("memkv", "mixer", "cross", "ffn"))
        self.do_sample = cfg.get("sample", True)
        nev = (self.depth + 1) // 2
        nod = self.depth // 2
        self.nev, self.nod = nev, nod
        dep = self.depth
        dr = k.dram
        SEQ = self.SEQ
        self.xp = dr("xp", [SEQ, D], F32, "ExternalInput")
        self.mem = dr("mem", [256, D], F32, "ExternalInput")
        self.consts = dr("consts", [128, 1024], F32, "ExternalInput")
        self.norm_mix = dr("norm_mix", [dep, D], F32, "ExternalInput")
        self.norm_cross = dr("norm_cross", [dep, D], F32, "ExternalInput")
        self.norm_mem = dr("norm_mem", [dep, D], F32, "ExternalInput")
        self.norm_ffn = dr("norm_ffn", [dep, D], F32, "ExternalInput")
        self.w_mq = dr("w_mq", [dep, D, 512], F32, "ExternalInput")
        self.w_mkv = dr("w_mkv", [dep, D, 1024], F32, "ExternalInput")
        self.mem_q_norm = dr("mem_q_norm", [dep, 128], F32, "ExternalInput")
        self.mem_k_norm = dr("mem_k_norm", [dep, 128], F32, "ExternalInput")
        self.w_mo = dr("w_mo", [dep, 512, D], F32, "ExternalInput")
        self.w_up = dr("w_ffn_up", [dep, D, 2 * FFN_H], F32, "ExternalInput")
        self.w_down = dr("w_ffn_down", [dep, FFN_H, D], F32, "ExternalInput")
        self.w_in_even = dr("w_in_even", [nev, D, EVEN_IN], F32, "ExternalInput")
        self.w_out_even = dr("w_out_even", [nev, 3072, D], F32, "ExternalInput")
        self.fox_qn = dr("fox_q_norm", [nev, 128], F32, "ExternalInput")
        self.fox_kn = dr("fox_k_norm", [nev, 128], F32, "ExternalInput")
        self.ev_small = dr("ev_small", [nev, 136], F32, "ExternalInput")
        self.ssd_norm = dr("ssd_norm", [nev, D], F32, "ExternalInput")
        self.conv_wT = dr("conv_wT", [nev, 128, 24 * 5], F32, "ExternalInput")
        if nod:
            self.w_in_odd = dr("w_in_odd", [nod, D, D], F32, "ExternalInput")
            self.s5_w_glu = dr("s5_w_glu", [nod, D, 2 * D], F32, "ExternalInput")
            self.s5_par = dr("s5_par", [nod, 128, 3 * 64], F32, "ExternalInput")
            self.s5_Bp = dr("s5_Bp", [nod, 2, 64, 128, 128], F32, "ExternalInput")
            self.s5_Cp = dr("s5_Cp", [nod, 2, 64, 128, 128], F32, "ExternalInput")
            self.s5_D = dr("s5_D", [nod, 128, 16], F32, "ExternalInput")
            self.s5_bbT = dr("s5_bbT", [nod, 64, 128, 2, 128], BF16, "Internal")
        self.kt_scr = dr("kt_scr", [nev, 128, 8, SEQ], BF16, "Internal")
        self.v_scr = dr("v_scr", [nev, SEQ, 1024], BF16, "Internal")
        self.y_p = dr("y_p", [SEQ, D], F32, "ExternalOutput")
        self.mem_k_p = dr("mem_k_p", [dep, 256, 512], F32, "ExternalOutput")
        self.mem_v_p = dr("mem_v_p", [dep, 256, 512], F32, "ExternalOutput")
        self.fox_k_p = dr("fox_k_p", [nev, SEQ, 1024], F32, "ExternalOutput")
        self.fox_v_p = dr("fox_v_p", [nev, SEQ, 1024], F32, "ExternalOutput")
        self.fox_logf_p = dr("fox_logf_p", [nev, SEQ, 8], F32, "ExternalOutput")
        self.ssd_p = dr("ssd_p", [nev, 2048, 128], F32, "ExternalOutput")
        self.conv_p = dr("conv_p", [nev, 3, 3072], F32, "ExternalOutput")
        if nod:
            self.s5_re_p = dr("s5_re_p", [nod, 64, 128], F32, "ExternalOutput")
            self.s5_im_p = dr("s5_im_p", [nod, 64, 128], F32, "ExternalOutput")
        if self.do_sample:
            NP = cfg.get("n_pages", 128)
            NROWS = cfg.get("n_pool", 1280) * 128
            self.NP = NP
            self.xs_in = dr("xs", [8, D], F32, "ExternalInput")
            self.cfk = [dr("cfk%d" % jj, [NROWS, 1024], F32, "ExternalInput") for jj in range(nev)]
            self.cfv = [dr("cfv%d" % jj, [NROWS, 1024], F32, "ExternalInput") for jj in range(nev)]
            self.cfl = [dr("cfl%d" % jj, [NROWS, 8], F32, "ExternalInput") for jj in range(nev)]
            self.cmk = dr("cmk", [dep, 256, 512], F32, "ExternalInput")
            self.cmv = dr("cmv", [dep, 256, 512], F32, "ExternalInput")
            self.sssd = dr("sssd", [nev, 2048, 128], F32, "ExternalInput")
            self.sconv = dr("sconv", [nev, 3, 3072], F32, "ExternalInput")
            self.ptab = dr("ptab", [1, NP], I32, "ExternalInput")
            self.y_s = dr("y_s", [8, D], F32, "ExternalOutput")
            self.fox_k_s = dr("fox_k_s", [nev, 8, 1024], F32, "ExternalOutput")
            self.fox_v_s = dr("fox_v_s", [nev, 8, 1024], F32, "ExternalOutput")
            self.fox_logf_s = dr("fox_logf_s", [nev, 8, 8], F32, "ExternalOutput")
            self.ssd_s = dr("ssd_s", [nev, 2048, 128], F32, "ExternalOutput")
            self.conv_s = dr("conv_s", [nev, 3, 3072], F32, "ExternalOutput")
            self.kt_scr_s = dr("kt_scr_s", [nev, 128, 8, 8], BF16, "Internal")
            self.v_scr_s = dr("v_scr_s", [nev, 8, 1024], BF16, "Internal")
            if nod:
                self.s5re_in = dr("s5re_in", [nod, 64, 128], F32, "ExternalInput")
                self.s5im_in = dr("s5im_in", [nod, 64, 128], F32, "ExternalInput")
                self.s5_re_s = dr("s5_re_s", [nod, 64, 128], F32, "ExternalOutput")
                self.s5_im_s = dr("s5_im_s", [nod, 64, 128], F32, "ExternalOutput")
        if cfg.get("dbg"):
            self.dbg_of = dr("dbg_of", [SEQ // T, 128, 8, T], BF16, "ExternalOutput")
            self.dbg_yn = dr("dbg_yn", [SEQ // T, 128, 16, T], BF16, "ExternalOutput")
        NBK = max(1, T // 128)
        self.NBK = NBK
        sb = k.sb
        self.cst = sb("cst", [128, 512], F32)
        self.cbf = sb("cbf", [128, 384], BF16)
        self.onesf = sb("onesf", [128, 128], F32)
        self.maskneg = sb("maskneg", [128, 128], F32)
        self.iota = sb("iota", [128, 512], F32)
        self.x = sb("x", [128, NBK, D], F32)
        self.xnT = sb("xnT", [128, KC, T], BF16)
        k.rotating("wt", 2, [128, 16, 512], BF16)
        k.rotating("gain", 1, [128, D], F32)
        k.rotating("g128", 2, [128, 128], F32)
        k.rotating("stat", 6, [128, 16], F32)
        k.rotating("xnb", 2, [128, D], BF16)
        k.rotating("tmpf", 3, [128, 512], F32)
        k.rotating("tmpb", 3, [128, 512], BF16)
        k.rotating("pT", 3, [128, 512], BF16)
        k.rotating("stage", 3, [128, 512], F32)
        self.mkT = sb("mkT", [128, 4, 256], BF16)
        self.mv = sb("mv", [128, 2, 512], BF16)
        self.cqT = sb("cqT", [128, 4, T], BF16)
        self.coT = sb("coT", [128, 4, T], BF16)
        if nod:
            self.s5r = sb("s5r", [128, nod, 64], F32)
            self.s5th = sb("s5th", [128, nod, 64], F32)
            self.s5st = sb("s5st", [128, nod, 2, 64], F32)
            self.s5vi = sb("s5vi", [128, nod, 2, 64], F32)
            self.s5Dt = sb("s5Dt", [128, nod, 16], F32)
        self.ssdH = sb("ssdH", [128, nev, 2048], F32)
        self.convst = sb("convst", [128, nev, 24, 3], F32)
        self.foxtot = sb("foxtot", [128, nev, 8], F32)
        self.negc = sb("negc", [128, nev, SEQ // 128 if SEQ >= 128 else 1, 8], F32)
        self.evs = sb("evs", [128, nev, 136], F32)
        self.evA = sb("evA", [128, nev, 32], F32)
        self.cwT = sb("cwT", [128, nev, 24, 5], F32)
        self.psA = [k.psum("psA%d" % i, [128, 512], F32) for i in range(4)]
        self.psB = [k.psum("psB%d" % i, [128, 512], F32) for i in range(2)]
        self.psT = [k.psum("psT%d" % i, [128, 1024], BF16) for i in range(2)]
        self.iA = self.iB = self.iT = 0
        self.cpy = 0
        k.dma("sp", self.cst[:], self.consts[:, 0:512])
        k.I("dve", "tensor_copy", self.cbf[:, 0:128], self.cst[:, 0:128])
        k.I("dve", "memset", self.cbf[:, 128:256], 1.0)
        k.I("dve", "tensor_copy", self.cbf[:, 256:384], self.cst[:, 128:256])
        k.I("dve", "memset", self.onesf[:], 1.0)
        k.I("dve", "tensor_scalar", self.maskneg[:], self.cst[:, 128:256], -1.0, 30000.0, ALU.add, ALU.mult)
        k.dma("sp", self.iota[:], self.consts[0:1, 512:1024].partition_broadcast(128))
        for j in range(nev):
            k.dma("sp", self.evs[:, j, :], self.ev_small[j:j + 1, :].partition_broadcast(128))
            k.dma("sp", self.cwT[:, j, :, :], self.conv_wT[j].rearrange("p (m c) -> p m c", c=5))
            k.I("act", "activation", out=self.evA[:, j, :], in_=self.evs[:, j, 40:72], func=AF.Exp)
            k.I("dve", "tensor_scalar", self.evA[:, j, :], self.evA[:, j, :], -1.0, None, ALU.mult)

    @property
    def ident_f(self):
        return self.cst[:, 0:128]

    @property
    def U_f(self):
        return self.cst[:, 128:256]

    @property
    def ident_b(self):
        return self.cbf[:, 0:128]

    @property
    def ones_b(self):
        return self.cbf[:, 128:256]

    @property
    def maskU_b(self):
        return self.cbf[:, 256:384]

    def bankA(self):
        self.iA += 1
        return self.psA[self.iA % 4]

    def bankB(self):
        self.iB += 1
        return self.psB[self.iB % 2]

    def bankT(self):
        self.iT += 1
        return self.psT[self.iT % 2]

    def copy(self, out, in_):
        self.cpy += 1
        if self.cpy % 2:
            return self.k.I("act", "activation", out=out, in_=in_, func=AF.Copy)
        return self.k.I("dve", "tensor_copy", out, in_)

    def load_w(self, W, k0, kn, c0, w):
        wt = self.k.nxt("wt")
        src = W[k0 * 128:(k0 + kn) * 128, c0:c0 + w].rearrange("(kc p) n -> p kc n", p=128)
        self.k.dma("pool", wt[:, 0:kn, 0:w], src)
        return wt

    def linear_tm(self, xT, kcn, W, c0, ncols, T, consume):
        k = self.k
        TB = min(T, 128)
        nb = T // TB
        ncb = (ncols + 511) // 512
        for cb in range(ncb):
            w = min(512, ncols - cb * 512)
            kgs = [(k0, min(16, kcn - k0)) for k0 in range(0, kcn, 16)]
            if len(kgs) == 1:
                wt = self.load_w(W, 0, kcn, c0 + cb * 512, w)
                for tb in range(nb):
                    ps = self.bankA()
                    for kc in range(kcn):
                        k.I("pe", "matmul", ps[:TB, :w], xT(kc, tb * TB, (tb + 1) * TB), wt[:, kc, :w],
                            start=(kc == 0), stop=(kc == kcn - 1))
                    consume(tb, cb, w, ps[:TB, :w])
            else:
                assert nb <= 4
                pss = [self.bankA() for _ in range(nb)]
                for (k0, kn) in kgs:
                    wt = self.load_w(W, k0, kn, c0 + cb * 512, w)
                    for tb in range(nb):
                        for kc in range(kn):
                            k.I("pe", "matmul", pss[tb][:TB, :w], xT(k0 + kc, tb * TB, (tb + 1) * TB), wt[:, kc, :w],
                                start=(k0 + kc == 0), stop=(k0 + kc == kcn - 1))
                for tb in range(nb):
                    consume(tb, cb, w, pss[tb][:TB, :w])

    def linear_fm(self, xT, kcn, W, c0, ncols, T, consume):
        k = self.k
        assert kcn <= 16
        ncb = (ncols + 511) // 512
        for cb in range(ncb):
            w = min(512, ncols - cb * 512)
            wt = self.load_w(W, 0, kcn, c0 + cb * 512, w)
            for mi in range((w + 127) // 128):
                mw = min(128, w - mi * 128)
                ps = self.bankA()
                for kc in range(kcn):
                    k.I("pe", "matmul", ps[:mw, :T], wt[:, kc, mi * 128:mi * 128 + mw], xT(kc, 0, T),
                        start=(kc == 0), stop=(kc == kcn - 1))
                consume(cb * 4 + mi, mw, ps[:mw, :T])

    def load_gain(self, row):
        g = self.k.nxt("gain")
        self.k.dma("sp", g[:], row.partition_broadcast(128))
        return g

    def load_g128(self, row):
        g = self.k.nxt("g128")
        self.k.dma("sp", g[:], row.partition_broadcast(128))
        return g

    def rstd(self, st, n, TB, c_in, c_out, width, extra=None):
        k = self.k
        k.I("act", "activation", out=st[:TB, 12:12 + width], in_=st[:TB, c_in:c_in + width], func=AF.Sqrt,
            scale=1.0 / n, bias=EPS)
        k.I("dve", "reciprocal", st[:TB, c_out:c_out + width], st[:TB, 12:12 + width])
        if extra is not None:
            k.I("dve", "tensor_scalar", st[:TB, c_out:c_out + width], st[:TB, c_out:c_out + width], extra, None, ALU.mult)

    def norm_rows(self, xb, g, TB, xnb):
        k = self.k
        st = k.nxt("stat")
        k.I("dve", "memset", st[:TB, 0:1], 0.0)
        k.I("act", "activation", out=xnb[:TB, :], in_=xb, func=AF.Square, accum_out=st[:TB, 0:1])
        self.rstd(st, D, TB, 0, 1, 1)
        k.I("dve", "scalar_tensor_tensor", xnb[:TB, :], xb, st[:TB, 1:2], g[:TB, :], ALU.mult, ALU.mult)

    def transpose16(self, xnb, TB, dst, tb, nchunk=16):
        k = self.k
        for c0 in range(0, nchunk, 8):
            n8 = min(8, nchunk - c0)
            pt = self.bankT()
            for c8 in range(n8):
                kc = c0 + c8
                k.I("pe", "transpose", pt[:, c8 * TB:(c8 + 1) * TB], xnb[:TB, kc * 128:(kc + 1) * 128],
                    self.ident_b[:TB, :TB])
            self.copy(dst[:, c0:c0 + n8, tb * TB:(tb + 1) * TB], pt[:, 0:n8 * TB].rearrange("p (c t) -> p c t", t=TB))

    def rmsnorm_T(self, xblk, gain_row, T):
        k = self.k
        TB = min(T, 128)
        nb = T // TB
        g = self.load_gain(gain_row)
        for tb in range(nb):
            xnb = k.nxt("xnb")
            self.norm_rows(xblk(tb), g, TB, xnb)
            self.transpose16(xnb, TB, self.xnT, tb)

    def xblk(self, TB):
        return lambda tb: TA(self.x[:TB, tb, :], tb)

    def resid_add(self, TB):
        def f(tb, cb, w, ps):
            xa = TA(self.x[:TB, tb, cb * 512:cb * 512 + w], tb)
            self.k.I("dve", "tensor_tensor", xa, xa, ps, ALU.add)
        return f

    def headnorm(self, ps, TB, nh, gain128, out_fn, extra=None):
        k = self.k
        st = k.nxt("stat")
        tf = k.nxt("tmpf")
        k.I("act", "activation", out=tf[:TB, :nh * 128], in_=ps, func=AF.Square)
        k.I("dve", "tensor_reduce", st[:TB, 0:nh], tf[:TB, :nh * 128].rearrange("p (h d) -> p h d", d=128), AX.X, ALU.add)
        self.rstd(st, 128, TB, 0, 4, nh, extra)
        for h in range(nh):
            k.I("dve", "scalar_tensor_tensor", out_fn(h), ps[:, h * 128:(h + 1) * 128], st[:TB, 4 + h:5 + h],
                gain128[:TB, :], ALU.mult, ALU.mult)

    def ffn(self, i, T):
        k = self.k
        TB = min(T, 128)
        with k.scope():
            hT = k.sb("hT", [128, 44, T], BF16)
            gbuf = k.sb("gbuf", [128, self.NBK, 512], BF16)
            self.rmsnorm_T(self.xblk(TB), self.norm_ffn[i:i + 1, :], T)
            xT = lambda kc, a, b: self.xnT[:, kc, a:b]
            for j in range(11):
                def c_gate(tb, cb, w, ps, j=j):
                    k.I("act", "activation", out=TA(gbuf[:TB, tb, :], tb), in_=ps, func=AF.Silu)

                def c_up(tb, cb, w, ps, j=j):
                    hb = k.nxt("tmpb")
                    k.I("dve", "tensor_tensor", hb[:TB, :], TA(gbuf[:TB, tb, :], tb), ps, ALU.mult)
                    pt = self.bankT()
                    for c in range(4):
                        k.I("pe", "transpose", pt[:, c * TB:(c + 1) * TB], hb[:TB, c * 128:(c + 1) * 128], self.ident_b[:TB, :TB])
                    self.copy(TA(hT[:, j * 4:j * 4 + 4, tb * TB:(tb + 1) * TB], (j, tb)),
                              pt[:, 0:4 * TB].rearrange("p (c t) -> p c t", t=TB))
                self.linear_tm(xT, KC, self.w_up[i], j * 512, 512, T, c_gate)
                self.linear_tm(xT, KC, self.w_up[i], FFN_H + j * 512, 512, T, c_up)
            hTf = lambda kc, a, b: TA(hT[:, kc, a:b], (kc // 4, a // TB))
            self.linear_tm(hTf, 44, self.w_down[i], 0, D, T, self.resid_add(TB))

    def mem_kv_prompt(self):
        k = self.k
        for tb in range(2):
            k.dma("sp", TA(self.x[:, 0, :], 0), self.mem[tb * 128:(tb + 1) * 128, :])
            for i in range(self.depth):
                g = self.load_gain(self.norm_mem[i:i + 1, :])
                gk = self.load_g128(self.mem_k_norm[i:i + 1, :])
                xnb = k.nxt("xnb")
                self.norm_rows(TA(self.x[:, 0, :], 0), g, 128, xnb)
                self.transpose16(xnb, 128, self.xnT, 0)
                xT = lambda kc, a, b: self.xnT[:, kc, a:b]

                def cons(tb_, cb, w, ps, i=i, gk=gk, tb=tb):
                    stg = k.nxt("stage")
                    if cb == 0:
                        self.headnorm(ps, 128, 4, gk, lambda h: stg[:, h * 128:(h + 1) * 128])
                        k.dma("sp", self.mem_k_p[i, tb * 128:(tb + 1) * 128, :], stg[:, :])
                    else:
                        k.I("act", "activation", out=stg[:, :], in_=ps, func=AF.Copy)
                        k.dma("sp", self.mem_v_p[i, tb * 128:(tb + 1) * 128, :], stg[:, :])
                self.linear_tm(xT, KC, self.w_mkv[i], 0, 1024, 128, cons)

    def load_mem_kv(self, kd, vd):
        k = self.k
        for mb in range(2):
            kb = k.nxt("tmpb")
            k.dma("pool", kb[:, :], kd[mb * 128:(mb + 1) * 128, :])
            k.dma("pool", self.mv[:, mb, :], vd[mb * 128:(mb + 1) * 128, :])
            pt = self.bankT()
            for h in range(4):
                k.I("pe", "transpose", pt[:, h * 128:(h + 1) * 128], kb[:, h * 128:(h + 1) * 128], self.ident_b)
            self.copy(self.mkT[:, :, mb * 128:(mb + 1) * 128], pt[:, 0:512].rearrange("p (h t) -> p h t", t=128))

    def cross(self, i, T, kd, vd):
        k = self.k
        TB = min(T, 128)
        self.load_mem_kv(kd, vd)
        self.rmsnorm_T(self.xblk(TB), self.norm_cross[i:i + 1, :], T)
        gq = self.load_g128(self.mem_q_norm[i:i + 1, :])
        xT = lambda kc, a, b: self.xnT[:, kc, a:b]

        def qcons(tb, cb, w, ps):
            qb = k.nxt("tmpb")
            self.headnorm(ps, TB, 4, gq, lambda h: qb[:TB, h * 128:(h + 1) * 128])
            pt = self.bankT()
            for h in range(4):
                k.I("pe", "transpose", pt[:, h * TB:(h + 1) * TB], qb[:TB, h * 128:(h + 1) * 128], self.ident_b[:TB, :TB])
            self.copy(self.cqT[:, :, tb * TB:(tb + 1) * TB], pt[:, 0:4 * TB].rearrange("p (h t) -> p h t", t=TB))
        self.linear_tm(xT, KC, self.w_mq[i], 0, 512, T, qcons)
        scale = 128.0 ** -0.5
        for h in range(4):
            pso = self.psA[2 + h % 2]
            psd = self.psB[h % 2]
            for mb in range(2):
                pss = self.psA[mb]
                k.I("pe", "matmul", pss[:, :T], self.mkT[:, h, mb * 128:(mb + 1) * 128], self.cqT[:, h, :T],
                    start=True, stop=True)
                pT = k.nxt("pT")
                k.I("act", "activation", out=pT[:, :T], in_=pss[:, :T], func=AF.Exp, scale=scale)
                k.I("pe", "matmul", pso[:, :T], self.mv[:, mb, h * 128:(h + 1) * 128], pT[:, :T],
                    start=(mb == 0), stop=(mb == 1))
                k.I("pe", "matmul", psd[:, :T], self.ones_b, pT[:, :T], start=(mb == 0), stop=(mb == 1))
            rd = k.nxt("tmpf")
            k.I("dve", "reciprocal", rd[:, :T], psd[:, :T])
            k.I("dve", "tensor_tensor", self.coT[:, h, :T], pso[:, :T], rd[:, :T], ALU.mult)
        cT = lambda kc, a, b: self.coT[:, kc, a:b]
        self.linear_tm(cT, 4, self.w_mo[i], 0, D, T, self.resid_add(TB))

    def s5_setup(self, j):
        k = self.k
        with k.scope():
            W = k.sb("s5w", [128, 18, 64], F32)
            k.dma("sp", W[:, 0:3, :], self.s5_par[j].rearrange("p (a i) -> p a i", i=64))
            k.dma("sp", self.s5Dt[:, j, :], self.s5_D[j])
            A_re, A_im, ldt = W[:, 0, :], W[:, 1, :], W[:, 2, :]
            lam_re, dt, r, th = W[:, 3, :], W[:, 4, :], self.s5r[:, j, :], self.s5th[:, j, :]
            k.I("dve", "tensor_scalar", lam_re, A_re, -1e-4, None, ALU.min)
            k.I("act", "activation", out=dt, in_=ldt, func=AF.Exp)
            k.I("dve", "tensor_tensor", W[:, 5, :], lam_re, dt, ALU.mult)
            k.I("act", "activation", out=r, in_=W[:, 5, :], func=AF.Exp)
            k.I("dve", "tensor_tensor", th, A_im, dt, ALU.mult)
            k.I("dve", "tensor_scalar", th, th, 1.0 / TWO_PI, None, ALU.mult)
            k.I("dve", "tensor_scalar", W[:, 6, :], th, MAGIC, MAGIC, ALU.add, ALU.subtract)
            k.I("dve", "tensor_tensor", W[:, 6, :], th, W[:, 6, :], ALU.subtract)
            k.I("act", "activation", out=W[:, 7, :], in_=W[:, 6, :], func=AF.Abs)
            k.I("act", "activation", out=W[:, 8, :], in_=W[:, 6, :], func=AF.Sin, scale=TWO_PI)
            k.I("act", "activation", out=W[:, 9, :], in_=W[:, 7, :], func=AF.Sin, scale=-TWO_PI, bias=0.5 * math.pi)
            ni, nr = W[:, 10, :], W[:, 11, :]
            k.I("dve", "tensor_tensor", ni, W[:, 8, :], r, ALU.mult)
            k.I("dve", "tensor_tensor", nr, W[:, 9, :], r, ALU.mult)
            k.I("dve", "tensor_scalar", nr, nr, -1.0, None, ALU.add)
            den, t1, t2 = W[:, 12, :], W[:, 13, :], W[:, 14, :]
            k.I("dve", "tensor_tensor", den, lam_re, lam_re, ALU.mult)
            k.I("dve", "tensor_tensor", t1, A_im, A_im, ALU.mult)
            k.I("dve", "tensor_tensor", den, den, t1, ALU.add)
            k.I("dve", "reciprocal", den, den)
            cre, cim = W[:, 15, :], W[:, 16, :]
            k.I("dve", "tensor_tensor", t1, nr, lam_re, ALU.mult)
            k.I("dve", "tensor_tensor", t2, ni, A_im, ALU.mult)
            k.I("dve", "tensor_tensor", t1, t1, t2, ALU.add)
            k.I("dve", "tensor_tensor", cre, t1, den, ALU.mult)
            k.I("dve", "tensor_tensor", t1, ni, lam_re, ALU.mult)
            k.I("dve", "tensor_tensor", t2, nr, A_im, ALU.mult)
            k.I("dve", "tensor_tensor", t1, t1, t2, ALU.subtract)
            k.I("dve", "tensor_tensor", cim, t1, den, ALU.mult)
            k.I("dve", "tensor_scalar", W[:, 17, :], cim, -1.0, None, ALU.mult)
            ncim = W[:, 17, :]
            for i in range(64):
                br = k.nxt("stage")
                bi = k.nxt("stage")
                k.dma("sp", br[:, 0:128], self.s5_Bp[j, 0, i])
                k.dma("sp", bi[:, 0:128], self.s5_Bp[j, 1, i])
                o = k.nxt("tmpb")
                t = k.nxt("tmpf")
                k.I("dve", "tensor_scalar", t[:, 0:128], bi[:, 0:128], ncim[:, i:i + 1], None, ALU.mult)
                k.I("dve", "scalar_tensor_tensor", o[:, 0:128], br[:, 0:128], cre[:, i:i + 1], t[:, 0:128], ALU.mult, ALU.add)
                k.I("dve", "tensor_scalar", t[:, 128:256], br[:, 0:128], cim[:, i:i + 1], None, ALU.mult)
                k.I("dve", "scalar_tensor_tensor", o[:, 128:256], bi[:, 0:128], cre[:, i:i + 1], t[:, 128:256], ALU.mult, ALU.add)
                pt = self.bankT()
                k.I("pe", "transpose", pt[:, 0:128], o[:, 0:128], self.ident_b)
                k.I("pe", "transpose", pt[:, 128:256], o[:, 128:256], self.ident_b)
                o2 = k.nxt("tmpb")
                self.copy(o2[:, 0:256], pt[:, 0:256])
                k.dma("sp", self.s5_bbT[j, i], o2[:, 0:256].rearrange("p (a c) -> p a c", c=128))

    def s5_init_state(self, j):
        k = self.k
        k.I("dve", "memset", self.s5st[:, j, :, :], 0.0)
        k.I("dve", "memset", self.s5vi[:, j, :, :], 0.0)

    def s5_mixer(self, i, T):
        k = self.k
        j = i // 2
        TB = min(T, 128)
        with k.scope():
            uT = k.sb("uT", [128, KC, T], BF16)
            gT = k.sb("gT", [128, KC, T], BF16)
            abuf = k.sb("abuf", [128, self.NBK, 512], F32)
            k.rotating("s5lw", 2, [128, 4, 4, 128], BF16)
            k.rotating("s5f", 14, [128, T], F32)
            k.rotating("s5b", 4, [128, T], BF16)
            self.rmsnorm_T(self.xblk(TB), self.norm_mix[i:i + 1, :], T)
            xT = lambda kc, a, b: self.xnT[:, kc, a:b]

            def ucons(m, mw, ps):
                self.copy(TA(uT[:, m, :T], m), ps)
            self.linear_fm(xT, KC, self.w_in_odd[j], 0, D, T, ucons)
            nb2 = 0
            for cc in range(16):
                lw = k.nxt("s5lw")
                k.dma("sp", lw[:, :, 0:2, :], self.s5_bbT[j, cc * 4:(cc + 1) * 4].rearrange("i p a c -> p i a c"))
                for a in range(2):
                    k.dma("pool", lw[:, :, 2 + a, :], self.s5_Cp[j, a, cc * 4:(cc + 1) * 4].rearrange("i p c -> p i c"))
                psy = self.psA[cc % 2]
                for i4 in range(4):
                    ti = cc * 4 + i4
                    nb2 += 1
                    pbr = (self.psA[2], self.psB[0])[nb2 % 2]
                    pbi = (self.psA[3], self.psB[1])[nb2 % 2]
                    k.I("pe", "matmul", pbr[:, :T], lw[:, i4, 0, :], TA(uT[:, cc, :T], cc), start=True, stop=True)
                    k.I("pe", "matmul", pbi[:, :T], lw[:, i4, 1, :], TA(uT[:, cc, :T], cc), start=True, stop=True)
                    f = lambda: k.nxt("s5f")
                    ph, ph2, S, C = f(), f(), f(), f()
                    k.I("dve", "tensor_scalar", ph[:, :T], self.iota[:, :T], self.s5th[:, j, ti:ti + 1], None, ALU.mult)
                    k.I("dve", "tensor_scalar", ph2[:, :T], ph[:, :T], MAGIC, MAGIC, ALU.add, ALU.subtract)
                    k.I("dve", "tensor_tensor", ph[:, :T], ph[:, :T], ph2[:, :T], ALU.subtract)
                    k.I("act", "activation", out=ph2[:, :T], in_=ph[:, :T], func=AF.Abs)
                    k.I("act", "activation", out=S[:, :T], in_=ph[:, :T], func=AF.Sin, scale=TWO_PI)
                    k.I("act", "activation", out=C[:, :T], in_=ph2[:, :T], func=AF.Sin, scale=-TWO_PI, bias=0.5 * math.pi)
                    t1, t2, cre, cim = f(), f(), f(), f()
                    k.I("dve", "tensor_tensor", t1[:, :T], C[:, :T], pbr[:, :T], ALU.mult)
                    k.I("dve", "tensor_tensor", t2[:, :T], S[:, :T], pbi[:, :T], ALU.mult)
                    k.I("dve", "tensor_tensor", cre[:, :T], t1[:, :T], t2[:, :T], ALU.add)
                    k.I("dve", "tensor_tensor", t1[:, :T], C[:, :T], pbi[:, :T], ALU.mult)
                    k.I("dve", "tensor_tensor", t2[:, :T], S[:, :T], pbr[:, :T], ALU.mult)
                    k.I("dve", "tensor_tensor", cim[:, :T], t1[:, :T], t2[:, :T], ALU.subtract)
                    vre, vim = f(), f()
                    rb = self.s5r[:, j, ti:ti + 1].to_broadcast([128, T])
                    k.I("dve", "tensor_tensor_scan", vre[:, :T], rb, cre[:, :T], self.s5vi[:, j, 0, ti:ti + 1], ALU.mult, ALU.add)
                    k.I("dve", "tensor_tensor_scan", vim[:, :T], rb, cim[:, :T], self.s5vi[:, j, 1, ti:ti + 1], ALU.mult, ALU.add)
                    sre, simn = f(), f()
                    k.I("dve", "tensor_tensor", t1[:, :T], C[:, :T], vre[:, :T], ALU.mult)
                    k.I("dve", "tensor_tensor", t2[:, :T], S[:, :T], vim[:, :T], ALU.mult)
                    k.I("dve", "tensor_tensor", sre[:, :T], t1[:, :T], t2[:, :T], ALU.subtract)
                    k.I("dve", "tensor_tensor", t1[:, :T], C[:, :T], vim[:, :T], ALU.mult)
                    k.I("dve", "tensor_tensor", t2[:, :T], S[:, :T], vre[:, :T], ALU.mult)
                    k.I("dve", "scalar_tensor_tensor", simn[:, :T], t1[:, :T], -1.0, t2[:, :T], ALU.mult, ALU.subtract)
                    sb1, sb2 = k.nxt("s5b"), k.nxt("s5b")
                    k.I("act", "activation", out=sb1[:, :T], in_=sre[:, :T], func=AF.Copy)
                    k.I("act", "activation", out=sb2[:, :T], in_=simn[:, :T], func=AF.Copy)
                    k.I("act", "activation", out=self.s5st[:, j, 0, ti:ti + 1], in_=sre[:, T - 1:T], func=AF.Copy)
                    k.I("act", "activation", out=self.s5st[:, j, 1, ti:ti + 1], in_=simn[:, T - 1:T], func=AF.Copy)
                    k.I("act", "activation", out=self.s5vi[:, j, 0, ti:ti + 1], in_=sre[:, T - 1:T], func=AF.Copy)
                    k.I("act", "activation", out=self.s5vi[:, j, 1, ti:ti + 1], in_=simn[:, T - 1:T], func=AF.Copy, scale=-1.0)
                    k.I("pe", "matmul", psy[:, :T], lw[:, i4, 2, :], sb1[:, :T], start=(i4 == 0), stop=False)
                    k.I("pe", "matmul", psy[:, :T], lw[:, i4, 3, :], sb2[:, :T], start=False, stop=(i4 == 3))
                yf = k.nxt("tmpf")
                k.I("dve", "scalar_tensor_tensor", yf[:, :T], TA(uT[:, cc, :T], cc), self.s5Dt[:, j, cc:cc + 1], psy[:, :T],
                    ALU.mult, ALU.add)
                k.I("act", "activation", out=TA(gT[:, cc, :T], cc), in_=yf[:, :T], func=AF.Gelu)
            gTf = lambda kc, a, b: TA(gT[:, kc, a:b], kc)
            for cb in range(4):
                def acons(tb, cb_, w, ps, cb=cb):
                    self.copy(TA(abuf[:TB, tb, :], tb), ps)

                def gcons(tb, cb_, w, ps, cb=cb):
                    sg = k.nxt("tmpf")
                    k.I("act", "activation", out=sg[:TB, :], in_=ps, func=AF.Sigmoid)
                    k.I("dve", "tensor_tensor", sg[:TB, :], sg[:TB, :], TA(abuf[:TB, tb, :], tb), ALU.mult)
                    xa = TA(self.x[:TB, tb, cb * 512:(cb + 1) * 512], tb)
                    k.I("dve", "tensor_tensor", xa, xa, sg[:TB, :], ALU.add)
                self.linear_tm(gTf, KC, self.s5_w_glu[j], cb * 512, 512, T, acons)
                self.linear_tm(gTf, KC, self.s5_w_glu[j], D + cb * 512, 512, T, gcons)

    def s5_store_state(self, j, re_out, im_out):
        k = self.k
        ps = self.bankB()
        k.I("pe", "transpose", ps[0:64, 0:128], self.s5st[:, j, 0, :], self.ident_f)
        k.I("pe", "transpose", ps[0:64, 128:256], self.s5st[:, j, 1, :], self.ident_f)
        o = k.nxt("stage")
        k.I("act", "activation", out=o[0:64, 0:128], in_=ps[0:64, 0:128], func=AF.Copy)
        k.I("dve", "tensor_scalar", o[0:64, 128:256], ps[0:64, 128:256], -1.0, None, ALU.mult)
        k.dma("sp", re_out, o[0:64, 0:128])
        k.dma("sp", im_out, o[0:64, 128:256])

    def even_init_state(self, j):
        k = self.k
        k.I("dve", "memset", self.ssdH[:, j, :], 0.0)
        k.I("dve", "memset", self.convst[:, j, :, :], 0.0)
        k.I("dve", "memset", self.foxtot[:, j, :], 0.0)

    def even_mixer(self, i, T, t0, seq):
        k = self.k
        j = i // 2
        TB = min(T, 128)
        nb = T // TB
        W = self.w_in_even[j]
        evs = self.evs
        kt_scr, v_scr = seq["kt_scr"], seq["v_scr"]
        with k.scope():
            ofT = k.sb("ofT", [128, 8, T], BF16)
            ynT = k.sb("ynT", [128, 16, T], BF16)
            self.rmsnorm_T(self.xblk(TB), self.norm_mix[i:i + 1, :], T)
            xT = lambda kc, a, b: self.xnT[:, kc, a:b]
            with k.scope():
                qT = k.sb("qT", [128, 8, T], BF16)
                cTq = k.sb("cTq", [8, T], F32)
                cTm = k.sb("cTm", [8, 8, T], F32)
                k.rotating("kth", 2, [128, t0 + T], BF16)
                k.rotating("vh", 2, [128, (t0 + T + 127) // 128, 128], BF16)
                k.rotating("kst", 2, [128, 4, TB], BF16)
                k.rotating("sm32", 4, [128, 32], F32)
                gq = self.load_g128(self.fox_qn[j:j + 1, :])
                gk = self.load_g128(self.fox_kn[j:j + 1, :])

                def qkv(tb, cb, w, ps):
                    tok0 = t0 + tb * TB
                    if cb < 2:
                        qb = k.nxt("tmpb")
                        self.headnorm(ps, TB, 4, gq, lambda h: qb[:TB, h * 128:(h + 1) * 128], extra=128.0 ** -0.5)
                        pt = self.bankT()
                        for h in range(4):
                            k.I("pe", "transpose", pt[:, h * TB:(h + 1) * TB], qb[:TB, h * 128:(h + 1) * 128], self.ident_b[:TB, :TB])
                        self.copy(qT[:, cb * 4:cb * 4 + 4, tb * TB:(tb + 1) * TB], pt[:, 0:4 * TB].rearrange("p (h t) -> p h t", t=TB))
                    elif cb < 4:
                        c2 = cb - 2
                        stg = k.nxt("stage")
                        self.headnorm(ps, TB, 4, gk, lambda h: stg[:TB, h * 128:(h + 1) * 128])
                        k.dma("sp", seq["fox_k"][j, tok0:tok0 + TB, c2 * 512:(c2 + 1) * 512], stg[:TB, :])
                        kb = k.nxt("tmpb")
                        k.I("dve", "tensor_copy", kb[:TB, :], stg[:TB, :])
                        pt = self.bankT()
                        for h in range(4):
                            k.I("pe", "transpose", pt[:, h * TB:(h + 1) * TB], kb[:TB, h * 128:(h + 1) * 128], self.ident_b[:TB, :TB])
                        kst = k.nxt("kst")
                        self.copy(kst[:, :, :TB], pt[:, 0:4 * TB].rearrange("p (h t) -> p h t", t=TB))
                        k.dma("sp", kt_scr[j, :, c2 * 4:c2 * 4 + 4, tok0:tok0 + TB], kst[:, :, :TB])
                    else:
                        c2 = cb - 4
                        stg = k.nxt("stage")
                        k.I("act", "activation", out=stg[:TB, :], in_=ps, func=AF.Copy)
                        k.dma("sp", seq["fox_v"][j, tok0:tok0 + TB, c2 * 512:(c2 + 1) * 512], stg[:TB, :])
                        vb = k.nxt("tmpb")
                        k.I("dve", "tensor_copy", vb[:TB, :], stg[:TB, :])
                        k.dma("sp", v_scr[j, tok0:tok0 + TB, c2 * 512:(c2 + 1) * 512], vb[:TB, :])
                self.linear_tm(xT, KC, W, 0, 3072, T, qkv)

                def fcons(tb, cb, w, ps):
                    tok0 = t0 + tb * TB
                    blk = tok0 // 128
                    s1 = k.nxt("sm32")
                    s2 = k.nxt("sm32")
                    k.I("dve", "tensor_tensor", s1[:TB, 0:8], ps, evs[:TB, j, 0:8], ALU.add)
                    k.I("act", "activation", out=s1[:TB, 8:16], in_=s1[:TB, 0:8], func=AF.Exp, scale=-1.0)
                    k.I("act", "activation", out=s1[:TB, 16:24], in_=s1[:TB, 8:16], func=AF.Ln, bias=1.0)
                    k.I("dve", "tensor_scalar", s2[:TB, 0:8], s1[:TB, 16:24], -1.0, None, ALU.mult)
                    k.dma("sp", seq["fox_lf"][j, tok0:tok0 + TB, :], s2[:TB, 0:8])
                    pc = self.bankB()
                    k.I("pe", "matmul", pc[:TB, 0:8], self.U_f[:TB, :TB], s2[:TB, 0:8], start=True, stop=True)
                    k.I("pe", "matmul", pc[:, 8:16], self.onesf[:TB, :], s2[:TB, 0:8], start=True, stop=True)
                    k.I("dve", "tensor_tensor", s2[:TB, 8:16], pc[:TB, 0:8], self.foxtot[:TB, j, :], ALU.add)
                    k.I("dve", "tensor_scalar", self.negc[:TB, j, blk, :], s2[:TB, 8:16], -1.0, None, ALU.mult)
                    k.I("dve", "tensor_tensor", self.foxtot[:, j, :], self.foxtot[:, j, :], pc[:, 8:16], ALU.add)
                    pc2 = self.bankB()
                    k.I("pe", "transpose", pc2[0:8, 0:TB], s2[:TB, 8:16], self.ident_f[:TB, :TB])
                    k.I("act", "activation", out=cTq[0:8, tb * TB:(tb + 1) * TB], in_=pc2[0:8, 0:TB], func=AF.Copy)
                self.linear_tm(xT, KC, W, 3072, 8, T, fcons)
                for h in range(8):
                    k.I("dve", "tensor_scalar", cTm[0:8, h, :], cTq[0:8, :], self.ident_f[0:8, h:h + 1], None, ALU.mult)

                if seq.get("past"):
                    self.attention_sample(j, T, qT, cTm, ofT, seq)
                else:
                    tend = t0 + T
                    nkb = (tend + 127) // 128
                    for h in range(8):
                        kth = k.nxt("kth")
                        vh = k.nxt("vh")
                        k.dma("sp", kth[:, 0:tend], kt_scr[j, :, h, 0:tend])
                        k.dma("sp", vh[:, 0:nkb, :], v_scr[j, 0:tend, h * 128:(h + 1) * 128].rearrange("(b s) d -> s b d", s=128))
                        pso = self.psA[2 + h % 2]
                        psd = self.psB[h % 2]
                        for kb in range(nkb):
                            ks = min(128, tend - kb * 128)
                            q0 = max(0, kb * 128 - t0)
                            N = T - q0
                            pss = self.psA[kb % 2]
                            k.I("pe", "matmul", pss[:ks, :N], kth[:, kb * 128:kb * 128 + ks], qT[:, h, q0:T], start=True, stop=False)
                            k.I("pe", "matmul", pss[:ks, :N], self.onesf[0:8, 0:ks], cTm[0:8, h, q0:T], start=False, stop=True)
                            pT = k.nxt("pT")
                            k.I("act", "activation", out=pT[:ks, :N], in_=pss[:ks, :N], func=AF.Exp, bias=self.negc[:ks, j, kb, h:h + 1])
                            if kb * 128 >= t0:
                                dw = min(128, N)
                                k.I("dve", "tensor_tensor", pT[:ks, 0:dw], pT[:ks, 0:dw], self.maskU_b[:ks, 0:dw], ALU.mult)
                            k.I("pe", "matmul", pso[:, q0:T], vh[:ks, kb, :], pT[:ks, :N], start=(kb == 0), stop=(kb == nkb - 1))
                            k.I("pe", "matmul", psd[:, q0:T], self.ones_b[:ks, :], pT[:ks, :N], start=(kb == 0), stop=(kb == nkb - 1))
                        rd = k.nxt("tmpf")
                        k.I("dve", "reciprocal", rd[:, :T], psd[:, :T])
                        k.I("dve", "tensor_tensor", ofT[:, h, :T], pso[:, :T], rd[:, :T], ALU.mult)

            with k.scope():
                zs = k.sb("zs", [128, nb, D], BF16)
                xbcT = k.sb("xbcT", [128, 24, T], BF16)
                dtt = k.sb("dtt", [128, nb, 32], F32)
                att = k.sb("att", [128, nb, 32], F32)
                k.rotating("craw", 2, [128, T + 3], F32)
                k.rotating("sm32", 8, [128, 32], F32)
                k.rotating("sq", 6, [128, 128], F32)
                k.rotating("sqb", 3, [128, 128], BF16)
                k.rotating("cbs", 2, [128, 128], F32)
                k.rotating("acm", 4, [32, 128], F32)
                xs_tm = k.sb("xs_tm", [128, D], BF16)
                B_tm = k.sb("B_tm", [128, 512], BF16)
                xdt = k.sb("xdt", [128, D], BF16)
                xdtw = k.sb("xdtw", [128, D], BF16)
                ysb = k.sb("ysb", [128, D], F32)
                hbf = k.sb("hbf", [128, D], BF16)
                acT = k.sb("acT", [32, 128], F32)

                def zcons(tb, cb, w, ps):
                    k.I("act", "activation", out=TA(zs[:TB, tb, cb * 512:(cb + 1) * 512], tb), in_=ps, func=AF.Silu)
                self.linear_tm(xT, KC, W, 3080, 2048, T, zcons)

                def xbc_cons(m, mw, ps):
                    cr = k.nxt("craw")
                    k.I("dve", "tensor_copy", cr[:, 0:3], self.convst[:, j, m, :])
                    k.I("act", "activation", out=cr[:, 3:3 + T], in_=ps, func=AF.Copy)
                    k.I("dve", "tensor_copy", self.convst[:, j, m, :], cr[:, T:T + 3])
                    acc = k.nxt("tmpf")
                    cw = self.cwT[:, j, m, :]
                    k.I("dve", "tensor_scalar", acc[:, :T], cr[:, 0:T], cw[:, 0:1], cw[:, 4:5], ALU.mult, ALU.add)
                    for kk in range(1, 4):
                        k.I("dve", "scalar_tensor_tensor", acc[:, :T], cr[:, kk:kk + T], cw[:, kk:kk + 1], acc[:, :T], ALU.mult, ALU.add)
                    k.I("act", "activation", out=TA(xbcT[:, m, :T], m), in_=acc[:, :T], func=AF.Silu)
                self.linear_fm(xT, KC, W, 5128, 3072, T, xbc_cons)

                def dtcons(tb, cb, w, ps):
                    s1 = k.nxt("sm32")
                    k.I("dve", "tensor_tensor", s1[:TB, :], ps, evs[:TB, j, 8:40], ALU.add)
                    k.I("act", "activation", out=s1[:TB, :], in_=s1[:TB, :], func=AF.Exp)
                    k.I("act", "activation", out=TA(dtt[:TB, tb, :], tb), in_=s1[:TB, :], func=AF.Ln, bias=1.0)
                    k.I("dve", "tensor_tensor", TA(att[:TB, tb, :], tb), TA(dtt[:TB, tb, :], tb), self.evA[:TB, j, :], ALU.mult)
                self.linear_tm(xT, KC, W, 8200, 32, T, dtcons)

                gss = self.load_gain(self.ssd_norm[j:j + 1, :])
                H = self.ssdH
                dttn = dtt[:].tensor.name
                for tb in range(nb):
                    cs = slice(tb * TB, (tb + 1) * TB)
                    a_t = TA(att[:TB, tb, :], tb)
                    pa = self.bankB()
                    k.I("pe", "matmul", pa[:TB, 0:32], self.U_f[:TB, :TB], a_t, start=True, stop=True)
                    k.I("pe", "matmul", pa[:, 32:64], self.onesf[:TB, :], a_t, start=True, stop=True)
                    acum = k.nxt("sm32")
                    k.I("act", "activation", out=acum[:TB, :], in_=pa[:TB, 0:32], func=AF.Copy)
                    expa = k.nxt("sm32")
                    k.I("act", "activation", out=expa[:TB, :], in_=pa[:TB, 0:32], func=AF.Exp)
                    cd = k.nxt("sm32")
                    k.I("act", "activation", out=cd[:, :], in_=pa[:, 32:64], func=AF.Exp)
                    wgt = k.nxt("sm32")
                    k.I("dve", "tensor_tensor", wgt[:TB, :], pa[:TB, 32:64], acum[:TB, :], ALU.subtract)
                    k.I("act", "activation", out=wgt[:TB, :], in_=wgt[:TB, :], func=AF.Exp)
                    k.I("dve", "tensor_tensor", wgt[:TB, :], wgt[:TB, :], TA(dtt[:TB, tb, :], tb), ALU.mult)
                    pa2 = self.bankB()
                    k.I("pe", "transpose", pa2[0:32, 0:TB], acum[:TB, :], self.ident_f[:TB, :TB])
                    k.I("act", "activation", out=acT[:, 0:TB], in_=pa2[0:32, 0:TB], func=AF.Copy)
                    for c0 in range(0, 20, 8):
                        n8 = min(8, 20 - c0)
                        pt = self.bankT()
                        for c8 in range(n8):
                            m = c0 + c8
                            k.I("pe", "transpose", pt[:TB, c8 * 128:(c8 + 1) * 128], TA(xbcT[:, m, cs], m), self.ident_b)
                        if c0 < 16:
                            self.copy(xs_tm[:TB, c0 * 128:(c0 + n8) * 128], pt[:TB, 0:n8 * 128])
                        else:
                            self.copy(B_tm[:TB, 0:512], pt[:TB, 0:512])
                    v3 = lambda t: t.rearrange("p (h d) -> p h d", d=64)
                    k.I("dve", "tensor_tensor", v3(xdt[:TB, :]), v3(xs_tm[:TB, :]),
                        dtt[:TB, tb, :].unsqueeze(2).to_broadcast([TB, 32, 64]), ALU.mult, _R=[(dttn, tb)])
                    k.I("dve", "tensor_tensor", v3(xdtw[:TB, :]), v3(xs_tm[:TB, :]),
                        wgt[:TB, :].unsqueeze(2).to_broadcast([TB, 32, 64]), ALU.mult)
                    k.I("act", "activation", out=hbf[:, :], in_=H[:, j, :], func=AF.Copy)
                    for g in range(4):
                        BT = TA(xbcT[:, 16 + g, cs], 16 + g)
                        CT = TA(xbcT[:, 20 + g, cs], 20 + g)
                        pcb = self.psA[0]
                        k.I("pe", "matmul", pcb[:TB, :TB], BT, CT, start=True, stop=True)
                        cbs = k.nxt("cbs")
                        k.I("act", "activation", out=cbs[:TB, :TB], in_=pcb[:TB, :TB], func=AF.Copy)
                        pyo = self.psA[1]
                        k.I("pe", "matmul", pyo[:TB, :], CT, hbf[:, g * 512:(g + 1) * 512], start=True, stop=True)
                        pyd = self.psA[2]
                        for r in range(8):
                            hh = 8 * g + r
                            pbc = self.psB[r % 2]
                            acm = k.nxt("acm")
                            k.I("dve", "tensor_scalar", acm[:, 0:TB], acT[:, 0:TB], self.ident_f[0:32, hh:hh + 1], None, ALU.mult)
                            k.I("pe", "matmul", pbc[:TB, :TB], self.onesf[0:32, 0:TB], acm[:, 0:TB], start=True, stop=True)
                            tm = k.nxt("sq")
                            k.I("dve", "scalar_tensor_tensor", tm[:TB, :TB], pbc[:TB, :TB], acum[:TB, hh:hh + 1], self.maskneg[:TB, :TB],
                                ALU.subtract, ALU.add)
                            k.I("act", "activation", out=tm[:TB, :TB], in_=tm[:TB, :TB], func=AF.Exp)
                            MT = k.nxt("sqb")
                            k.I("dve", "tensor_tensor", MT[:TB, :TB], tm[:TB, :TB], cbs[:TB, :TB], ALU.mult)
                            k.I("pe", "matmul", pyd[:TB, r * 64:(r + 1) * 64], MT[:TB, :TB], xdt[:TB, hh * 64:(hh + 1) * 64], start=True, stop=True)
                        tf = k.nxt("tmpf")
                        k.I("dve", "tensor_tensor", v3(tf[:TB, :]), v3(pyo[:TB, :]),
                            expa[:TB, 8 * g:8 * g + 8].unsqueeze(2).to_broadcast([TB, 8, 64]), ALU.mult)
                        k.I("dve", "tensor_tensor", ysb[:TB, g * 512:(g + 1) * 512], tf[:TB, :], pyd[:TB, :], ALU.add)
                        pst = self.psA[3]
                        k.I("pe", "matmul", pst[:, :], B_tm[:TB, g * 128:(g + 1) * 128], xdtw[:TB, g * 512:(g + 1) * 512], start=True, stop=True)
                        Hg = H[:, j, g * 512:(g + 1) * 512]
                        k.I("dve", "tensor_tensor", v3(Hg), v3(Hg), cd[:, 8 * g:8 * g + 8].unsqueeze(2).to_broadcast([128, 8, 64]), ALU.mult)
                        k.I("dve", "tensor_tensor", Hg, Hg, pst[:, :], ALU.add)
                    tD = k.nxt("xnb")
                    k.I("dve", "tensor_tensor", v3(tD[:TB, :]), v3(xs_tm[:TB, :]),
                        evs[:TB, j, 72:104].unsqueeze(2).to_broadcast([TB, 32, 64]), ALU.mult)
                    k.I("dve", "tensor_tensor", ysb[:TB, :], ysb[:TB, :], tD[:TB, :], ALU.add)
                    k.I("dve", "tensor_tensor", ysb[:TB, :], ysb[:TB, :], TA(zs[:TB, tb, :], tb), ALU.mult)
                    ynb = k.nxt("xnb")
                    self.norm_rows(ysb[:TB, :], gss, TB, ynb)
                    self.transpose16(ynb, TB, ynT, tb)

            if self.cfg.get("dbg"):
                k.dma("sp", self.dbg_of[t0 // T], ofT[:, :, :])
                k.dma("sp", self.dbg_yn[t0 // T], ynT[:, :, :])

            def oT(kc, a, b):
                return ofT[:, kc, a:b] if kc < 8 else ynT[:, kc - 8, a:b]
            self.linear_tm(oT, 24, self.w_out_even[j], 0, D, T, self.resid_add(TB))

    def attention_sample(self, j, T, qT, cTm, ofT, seq):
        k = self.k
        NP = seq["n_pages"]
        nrows = seq["cfk"][0].shape[0]
        k.rotating("kpg", 2, [128, 1024], F32)
        k.rotating("vpg", 2, [128, 1024], F32)
        k.rotating("kpb", 2, [128, 1024], BF16)
        k.rotating("vpb", 2, [128, 1024], BF16)
        k.rotating("ktp", 2, [128, 8, 128], BF16)
        k.rotating("e64", 3, [128, 64], F32)
        k.rotating("p64", 3, [128, 64], BF16)
        lf = k.sb("lf_all", [128, NP, 8], F32)
        tail = k.sb("tail", [128, NP, 8], F32)
        pref = k.sb("pref", [128, NP, 8], F32)
        idx = self.pidx
        for pg in range(NP):
            k.dma("pool", lf[:, pg, :], seq["cfl"][j], meth="indirect_dma_start", out_offset=None,
                  in_offset=bass.IndirectOffsetOnAxis(ap=idx[:, pg:pg + 1], axis=0),
                  xr=[(idx[:].tensor.name, None)])
        lf2 = lf[:, :, :].rearrange("p a h -> p (a h)")
        tl2 = tail[:, :, :].rearrange("p a h -> p (a h)")
        pf2 = pref[:, :, :].rearrange("p a h -> p (a h)")
        ncol = NP * 8
        for c0 in range(0, ncol, 512):
            cw = min(512, ncol - c0)
            p1 = self.psA[0]
            p2 = self.psA[1]
            k.I("pe", "matmul", p1[:, :cw], self.cst[:, 256:384], lf2[:, c0:c0 + cw], start=True, stop=True)
            k.I("pe", "matmul", p2[:, :cw], self.onesf[:, :], lf2[:, c0:c0 + cw], start=True, stop=True)
            k.I("act", "activation", out=tl2[:, c0:c0 + cw], in_=p1[:, :cw], func=AF.Copy)
            k.I("act", "activation", out=pf2[:, c0:c0 + cw], in_=p2[:, :cw], func=AF.Copy)
        for h in range(8):
            k.I("dve", "tensor_tensor_scan", lf[:, :, h], self.onesf[:, 0:NP], pref[:, :, h], 0.0, ALU.mult, ALU.add)
        for h in range(8):
            k.I("dve", "tensor_scalar", pref[:, :, h], lf[:, :, h], -1.0, lf[:, NP - 1, h:h + 1], ALU.mult, ALU.add)
        k.I("dve", "tensor_tensor", tl2, tl2, pf2, ALU.add)
        pso = self.psA[2]
        psd = self.psA[3]
        for pg in range(NP):
            kpg, vpg = k.nxt("kpg"), k.nxt("vpg")
            for (dst, src) in ((kpg, seq["cfk"]), (vpg, seq["cfv"])):
                k.dma("pool", dst[:, :], src[j], meth="indirect_dma_start", out_offset=None,
                      in_offset=bass.IndirectOffsetOnAxis(ap=idx[:, pg:pg + 1], axis=0),
                      xr=[(idx[:].tensor.name, None)])
            kpb, vpb = k.nxt("kpb"), k.nxt("vpb")
            k.I("act", "activation", out=kpb[:, :], in_=kpg[:, :], func=AF.Copy)
            k.I("dve", "tensor_copy", vpb[:, :], vpg[:, :])
            pt = self.bankT()
            for h in range(8):
                k.I("pe", "transpose", pt[:, h * 128:(h + 1) * 128], kpb[:, h * 128:(h + 1) * 128], self.ident_b)
            ktp = k.nxt("ktp")
            self.copy(ktp[:, :, :], pt[:, :].rearrange("p (h s) -> p h s", s=128))
            pss = self.psA[pg % 2]
            for h in range(8):
                k.I("pe", "matmul", pss[:, h * T:(h + 1) * T], ktp[:, h, :], qT[:, h, 0:T], start=True, stop=False)
                k.I("pe", "matmul", pss[:, h * T:(h + 1) * T], self.onesf[0:8, :], cTm[0:8, h, 0:T], start=False, stop=True)
            e = k.nxt("e64")
            k.I("dve", "tensor_tensor", e[:, :].rearrange("p (h t) -> p h t", t=T), pss[:, 0:8 * T].rearrange("p (h t) -> p h t", t=T),
                tail[:, pg, :].unsqueeze(2).to_broadcast([128, 8, T]), ALU.add)
            pT = k.nxt("p64")
            k.I("act", "activation", out=pT[:, :], in_=e[:, :], func=AF.Exp)
            for h in range(8):
                k.I("pe", "matmul", pso[:, h * T:(h + 1) * T], vpb[:, h * 128:(h + 1) * 128], pT[:, h * T:(h + 1) * T], start=(pg == 0), stop=False)
                k.I("pe", "matmul", psd[:, h * T:(h + 1) * T], self.ones_b, pT[:, h * T:(h + 1) * T], start=(pg == 0), stop=False)
        kt_scr, v_scr = seq["kt_scr"], seq["v_scr"]
        ktn = k.nxt("ktp")
        vn = k.nxt("vpb")
        k.dma("sp", ktn[:, :, 0:T], kt_scr[j, :, :, 0:T])
        k.dma("sp", vn[:T, :], v_scr[j, 0:T, :])
        pss = self.psA[0]
        for h in range(8):
            k.I("pe", "matmul", pss[:T, h * T:(h + 1) * T], ktn[:, h, 0:T], qT[:, h, 0:T], start=True, stop=False)
            k.I("pe", "matmul", pss[:T, h * T:(h + 1) * T], self.onesf[0:8, 0:T], cTm[0:8, h, 0:T], start=False, stop=True)
        e = k.nxt("e64")
        k.I("dve", "tensor_tensor", e[:T, :].rearrange("p (h t) -> p h t", t=T), pss[:T, 0:8 * T].rearrange("p (h t) -> p h t", t=T),
            self.negc[:T, j, 0, :].unsqueeze(2).to_broadcast([T, 8, T]), ALU.add)
        pT = k.nxt("p64")
        k.I("act", "activation", out=pT[:T, :], in_=e[:T, :], func=AF.Exp)
        k.I("dve", "tensor_tensor", pT[:T, :].rearrange("p (h t) -> p h t", t=T), pT[:T, :].rearrange("p (h t) -> p h t", t=T),
            self.maskU_b[:T, 0:T].unsqueeze(1).to_broadcast([T, 8, T]), ALU.mult)
        for h in range(8):
            k.I("pe", "matmul", pso[:, h * T:(h + 1) * T], vn[:T, h * 128:(h + 1) * 128], pT[:T, h * T:(h + 1) * T], start=(NP == 0), stop=True)
            k.I("pe", "matmul", psd[:, h * T:(h + 1) * T], self.ones_b[:T, :], pT[:T, h * T:(h + 1) * T], start=(NP == 0), stop=True)
        rd = k.nxt("e64")
        k.I("dve", "reciprocal", rd[:, :], psd[:, 0:8 * T])
        k.I("dve", "tensor_tensor", ofT[:, :, :].rearrange("p h t -> p (h t)"), pso[:, 0:8 * T], rd[:, :], ALU.mult)

    def even_store_state(self, j, ssd_out, conv_out):
        k = self.k
        for m in range(16):
            ps = self.bankB()
            k.I("pe", "transpose", ps[:, 0:128], self.ssdH[:, j, m * 128:(m + 1) * 128], self.ident_f)
            o = k.nxt("stage")
            self.copy(o[:, 0:128], ps[:, 0:128])
            k.dma("sp", ssd_out[m * 128:(m + 1) * 128, :], o[:, 0:128])
        for m in range(24):
            k.dma("sp", conv_out[:, m * 128:(m + 1) * 128].rearrange("k p -> p k"), self.convst[:, j, m, :],
                  allow_slow_non_contiguous=True)

    def even_load_state(self, j, ssd_in, conv_in):
        k = self.k
        for m in range(16):
            stg = k.nxt("stage")
            k.dma("sp", stg[:, 0:128], ssd_in[m * 128:(m + 1) * 128, :])
            ps = self.bankB()
            k.I("pe", "transpose", ps[:, 0:128], stg[:, 0:128], self.ident_f)
            self.copy(self.ssdH[:, j, m * 128:(m + 1) * 128], ps[:, 0:128])
        for m in range(24):
            k.dma("sp", self.convst[:, j, m, :], conv_in[:, m * 128:(m + 1) * 128].rearrange("k p -> p k"),
                  allow_slow_non_contiguous=True)
        k.I("dve", "memset", self.foxtot[:, j, :], 0.0)

    def s5_load_state(self, j, re_in, im_in):
        k = self.k
        stg = k.nxt("stage")
        k.dma("sp", stg[0:64, 0:128], re_in)
        k.dma("sp", stg[0:64, 128:256], im_in)
        ps = self.bankB()
        k.I("pe", "transpose", ps[:, 0:64], stg[0:64, 0:128], self.ident_f[0:64, 0:64])
        k.I("pe", "transpose", ps[:, 64:128], stg[0:64, 128:256], self.ident_f[0:64, 0:64])
        k.I("act", "activation", out=self.s5st[:, j, 0, :], in_=ps[:, 0:64], func=AF.Copy)
        k.I("dve", "tensor_copy", self.s5vi[:, j, 0, :], ps[:, 0:64])
        k.I("dve", "tensor_copy", self.s5vi[:, j, 1, :], ps[:, 64:128])
        k.I("act", "activation", out=self.s5st[:, j, 1, :], in_=ps[:, 64:128], func=AF.Copy, scale=-1.0)

    def run_seq(self, seq):
        k = self.k
        T = seq["T"]
        TB = min(T, 128)
        nb = T // TB
        ntile = seq["SEQ"] // T
        for ti in range(ntile):
            for tb in range(nb):
                k.dma("sp", TA(self.x[:TB, tb, :], tb), seq["x_in"][ti * T + tb * TB: ti * T + (tb + 1) * TB, :])
            for i in range(self.depth):
                if "mixer" in self.parts:
                    if i % 2 == 1:
                        self.s5_mixer(i, T)
                    else:
                        self.even_mixer(i, T, ti * T, seq)
                if "cross" in self.parts:
                    self.cross(i, T, seq["mem_k"][i], seq["mem_v"][i])
                if "ffn" in self.parts:
                    self.ffn(i, T)
            for tb in range(nb):
                k.dma("sp", seq["y_out"][ti * T + tb * TB: ti * T + (tb + 1) * TB, :], TA(self.x[:TB, tb, :], tb))
        if "mixer" in self.parts:
            for j in range(self.nod):
                self.s5_store_state(j, seq["s5_re"][j], seq["s5_im"][j])
            for j in range(self.nev):
                self.even_store_state(j, seq["ssd"][j], seq["conv"][j])

    def build(self):
        k = self.k
        if "memkv" in self.parts:
            self.mem_kv_prompt()
        if "mixer" in self.parts:
            for j in range(self.nod):
                self.s5_setup(j)
                self.s5_init_state(j)
            for j in range(self.nev):
                self.even_init_state(j)
        pseq = dict(T=self.T, SEQ=self.SEQ, x_in=self.xp, y_out=self.y_p, mem_k=self.mem_k_p, mem_v=self.mem_v_p,
                    kt_scr=self.kt_scr, v_scr=self.v_scr, fox_k=self.fox_k_p, fox_v=self.fox_v_p, fox_lf=self.fox_logf_p,
                    ssd=self.ssd_p, conv=self.conv_p, past=False)
        if self.nod:
            pseq.update(s5_re=self.s5_re_p, s5_im=self.s5_im_p)
        self.run_seq(pseq)
        if self.do_sample:
            k.barrier()
            NP = self.NP
            pti = k.sb("pti", [128, NP], I32)
            ptf = k.sb("ptf", [128, NP], F32)
            self.pidx = k.sb("pidx", [128, NP], I32)
            k.dma("sp", pti[:], self.ptab[0:1, :].partition_broadcast(128))
            k.I("dve", "tensor_scalar", ptf[:], pti[:], 128.0, None, ALU.mult)
            k.I("dve", "tensor_tensor", self.pidx[:], ptf[:], self.cst[:, 384:384 + NP], ALU.add)
            if "mixer" in self.parts:
                for j in range(self.nod):
                    self.s5_load_state(j, self.s5re_in[j], self.s5im_in[j])
                for j in range(self.nev):
                    self.even_load_state(j, self.sssd[j], self.sconv[j])
            sseq = dict(T=8, SEQ=8, x_in=self.xs_in, y_out=self.y_s, mem_k=self.cmk, mem_v=self.cmv,
                        kt_scr=self.kt_scr_s, v_scr=self.v_scr_s, fox_k=self.fox_k_s, fox_v=self.fox_v_s, fox_lf=self.fox_logf_s,
                        ssd=self.ssd_s, conv=self.conv_s, past=True, n_pages=NP, cfk=self.cfk, cfv=self.cfv, cfl=self.cfl)
            if self.nod:
                sseq.update(s5_re=self.s5_re_s, s5_im=self.s5_im_s)
            self.run_seq(sseq)
        k.finish()
        return k.nc


def make_consts():
    c = np.zeros((128, 1024), np.float32)
    c[:, 0:128] = np.eye(128, dtype=np.float32)
    s = np.arange(128)[:, None]
    t = np.arange(128)[None, :]
    c[:, 128:256] = (s <= t)
    c[:, 256:384] = (s > t)
    c[:, 384:512] = np.arange(128, dtype=np.float32)[:, None]
    c[0, 512:1024] = np.arange(1, 513)
    es = np.zeros((32, 32, 128), np.float32)
    for h in range(32):
        es[h, h, :] = 1.0
    return c, es.reshape(32, 32 * 128)


def even_layout(ev):
    nev = ev['w_in_even'].shape[0]
    small = np.zeros((nev, 136), np.float32)
    small[:, 0:8] = ev['fox_b_forget']
    small[:, 8:40] = ev['ssd_dt_bias']
    small[:, 40:72] = ev['ssd_A_log']
    small[:, 72:104] = ev['ssd_D']
    cw = np.zeros((nev, 128, 24, 5), np.float32)
    for j in range(nev):
        cw[j, :, :, 0:4] = ev['ssd_conv_w'][j].T.reshape(24, 128, 4).transpose(1, 0, 2)
        cw[j, :, :, 4] = ev['ssd_conv_b'][j].reshape(24, 128).T
    return dict(w_in_even=np.ascontiguousarray(ev['w_in_even']), w_out_even=np.ascontiguousarray(ev['w_out_even']),
                fox_q_norm=np.ascontiguousarray(ev['fox_q_norm']), fox_k_norm=np.ascontiguousarray(ev['fox_k_norm']),
                ev_small=small, ssd_norm=np.ascontiguousarray(ev['ssd_norm']), conv_wT=cw.reshape(nev, 128, 120))


def s5_layout(A_re, A_im, log_dt, B_re, B_im, C_re, C_im, Dv):
    nod = A_re.shape[0]
    par = np.zeros((nod, 128, 3 * 64), np.float32)
    Bp = np.zeros((nod, 2, 64, 128, 128), np.float32)
    Cp = np.zeros((nod, 2, 64, 128, 128), np.float32)
    Dl = np.zeros((nod, 128, 16), np.float32)
    for j in range(nod):
        a_re = A_re[j].reshape(64, 2, 64).transpose(1, 2, 0).reshape(128, 64)
        a_im = A_im[j].reshape(64, 2, 64).transpose(1, 2, 0).reshape(128, 64)
        ld = np.repeat(log_dt[j].reshape(64, 2, 1), 64, axis=2).transpose(1, 2, 0).reshape(128, 64)
        par[j, :, 0:64] = a_re
        par[j, :, 64:128] = a_im
        par[j, :, 128:192] = ld
        for i in range(64):
            for gp in range(2):
                g = 2 * i + gp
                c0 = (i % 4) * 32 + gp * 16
                Bp[j, 0, i, gp * 64:(gp + 1) * 64, c0:c0 + 16] = B_re[j, g]
                Bp[j, 1, i, gp * 64:(gp + 1) * 64, c0:c0 + 16] = B_im[j, g]
                Cp[j, 0, i, gp * 64:(gp + 1) * 64, c0:c0 + 16] = C_re[j, g].T
                Cp[j, 1, i, gp * 64:(gp + 1) * 64, c0:c0 + 16] = C_im[j, g].T
        Dl[j] = Dv[j].reshape(16, 128).T
    return par, Bp, Cp, Dl


_CACHE = {}


def _f32(a):
    return np.ascontiguousarray(np.asarray(a), dtype=np.float32)


def make_in_maps(inp, ncores, depth):
    nev = (depth + 1) // 2
    nod = depth // 2
    c, _ = make_consts()
    sh = dict(consts=c)
    for kk in ("norm_mix", "norm_cross", "norm_mem", "norm_ffn", "w_mq", "w_mkv", "mem_q_norm", "mem_k_norm", "w_mo",
               "w_ffn_up", "w_ffn_down"):
        sh[kk] = _f32(inp[kk])
    sh.update(even_layout({kk: np.asarray(inp[kk]) for kk in ("w_in_even", "w_out_even", "fox_b_forget", "fox_q_norm", "fox_k_norm",
                                                              "ssd_conv_w", "ssd_conv_b", "ssd_dt_bias", "ssd_A_log", "ssd_D", "ssd_norm")}))
    if nod:
        par, Bp, Cp, Dl = s5_layout(np.asarray(inp["s5_A_re"]), np.asarray(inp["s5_A_im"]), np.asarray(inp["s5_log_dt"]),
                                    np.asarray(inp["s5_B_re"]), np.asarray(inp["s5_B_im"]), np.asarray(inp["s5_C_re"]),
                                    np.asarray(inp["s5_C_im"]), np.asarray(inp["s5_D"]))
        sh.update(w_in_odd=_f32(inp["w_in_odd"]), s5_w_glu=_f32(inp["s5_w_glu"]), s5_par=par, s5_Bp=Bp, s5_Cp=Cp, s5_D=Dl)
    cfk = _f32(inp["cache_fox_k"]).reshape(nev, -1, 1024)
    cfv = _f32(inp["cache_fox_v"]).reshape(nev, -1, 1024)
    cfl = _f32(inp["cache_fox_logf"]).reshape(nev, -1, 8)
    for jj in range(nev):
        sh["cfk%d" % jj] = cfk[jj]
        sh["cfv%d" % jj] = cfv[jj]
        sh["cfl%d" % jj] = cfl[jj]
    B = np.asarray(inp["x_prompt"]).shape[0]
    DB = np.asarray(inp["x_sample"]).shape[0]
    maps = []
    for core in range(ncores):
        b = core % B
        sb = core % DB
        m = dict(sh)
        m["xp"] = _f32(inp["x_prompt"][b])
        m["mem"] = _f32(inp["mem_prompt"][b])
        m["xs"] = _f32(inp["x_sample"][sb])
        m["cmk"] = _f32(np.asarray(inp["cache_mem_k"])[:, sb]).reshape(depth, 256, 512)
        m["cmv"] = _f32(np.asarray(inp["cache_mem_v"])[:, sb]).reshape(depth, 256, 512)
        m["sssd"] = _f32(np.asarray(inp["state_ssd"])[:, sb]).reshape(nev, 2048, 128)
        m["sconv"] = _f32(np.asarray(inp["state_conv"])[:, sb])
        if nod:
            m["s5re_in"] = _f32(np.asarray(inp["state_s5_re"])[:, sb]).reshape(nod, 64, 128)
            m["s5im_in"] = _f32(np.asarray(inp["state_s5_im"])[:, sb]).reshape(nod, 64, 128)
        m["ptab"] = np.ascontiguousarray(np.asarray(inp["page_table"])[sb:sb + 1].astype(np.int32))
        maps.append(m)
    return maps


def assemble(results, ncores, depth, SEQ, B=None, DB=None):
    nev = (depth + 1) // 2
    nod = depth // 2
    B = B or min(ncores, 4)
    DB = DB or ncores
    pr = [results[b] for b in range(B)]
    sr = [results[c] for c in range(DB)]

    def st(rs, key, shape_tail, lead):
        a = np.stack([np.asarray(r[key]) for r in rs], axis=1)
        return np.ascontiguousarray(a.reshape((lead, len(rs)) + shape_tail)).astype(np.float32)
    outs = [
        np.stack([np.asarray(r["y_p"]) for r in pr]).astype(np.float32),
        np.stack([np.asarray(r["y_s"]) for r in sr]).astype(np.float32),
        st(pr, "fox_k_p", (SEQ, 8, 128), nev), st(pr, "fox_v_p", (SEQ, 8, 128), nev), st(pr, "fox_logf_p", (SEQ, 8), nev),
        st(pr, "mem_k_p", (256, 4, 128), depth), st(pr, "mem_v_p", (256, 4, 128), depth),
        st(pr, "ssd_p", (32, 64, 128), nev), st(pr, "conv_p", (3, 3072), nev),
        st(pr, "s5_re_p", (128, 64), nod), st(pr, "s5_im_p", (128, 64), nod),
        st(sr, "fox_k_s", (8, 8, 128), nev), st(sr, "fox_v_s", (8, 8, 128), nev), st(sr, "fox_logf_s", (8, 8), nev),
        st(sr, "ssd_s", (32, 64, 128), nev), st(sr, "conv_s", (3, 3072), nev),
        st(sr, "s5_re_s", (128, 64), nod), st(sr, "s5_im_s", (128, 64), nod),
    ]
    return tuple(outs)


def kernel(**inp):
    depth = 4
    SEQ = int(np.asarray(inp["x_prompt"]).shape[1])
    n_pool = int(np.asarray(inp["cache_fox_k"]).shape[1])
    n_pages = int(np.asarray(inp["page_table"]).shape[1])
    cfg = dict(T=256, SEQ=SEQ, depth=depth, sample=True, n_pages=n_pages, n_pool=n_pool)
    key = (SEQ, n_pool, n_pages)
    if _CACHE.get("key") != key:
        _CACHE["nc"] = Prog(cfg).build()
        _CACHE["key"] = key
    nc = _CACHE["nc"]
    ncores = 8
    in_maps = make_in_maps(inp, ncores, depth)
    res = run_bass_kernel_spmd(nc, in_maps, core_ids=list(range(ncores)))
    return assemble(res.results, ncores, depth, SEQ, B=int(np.asarray(inp["x_prompt"]).shape[0]),
                    DB=int(np.asarray(inp["x_sample"]).shape[0]))
```

```python
import contextlib
import math
import numpy as np
import concourse.bass as bass
import concourse.mybir as mybir
from concourse.bass_utils import run_bass_kernel_spmd

F32 = mybir.dt.float32
BF16 = mybir.dt.bfloat16
I32 = mybir.dt.int32
AF = mybir.ActivationFunctionType
ALU = mybir.AluOpType
AX = mybir.AxisListType

D = 2048
KC = 16
EPS = 1e-6
SEM_LIMIT = 30000
EMBED_WAIT = True
NSLOT = 6
FFN_H = 5632
EVEN_IN = 8232
TWO_PI = 2.0 * math.pi
MAGIC = 12582912.0


class TA:
    def __init__(self, ap, key):
        self.ap = ap
        self.key = key


def _isap(a):
    return hasattr(a, "tensor") and hasattr(a, "ap") and hasattr(a, "offset")


class KB:
    def __init__(self):
        self.nc = bass.Bass("TRN2", target_bir_lowering=False)
        self.es = contextlib.ExitStack()
        self.ses = contextlib.ExitStack()
        nc = self.nc
        self.engs = {"pe": nc.tensor, "act": nc.scalar, "dve": nc.vector, "pool": nc.gpsimd, "sp": nc.sync}
        self.semh = {}
        self.nsem = 0
        self.sem = {}
        self.cnt = {}
        self.seen = {e: {} for e in self.engs}
        self.retired = set()
        for e in self.engs:
            self.sem[e] = self._sem()
            self.cnt[e] = 0
        self.dq = {q: {"slots": [[self._sem(), 0] for _ in range(NSLOT)], "i": 0} for q in ("sp", "pool", "act")}
        self.bufs = {}
        self.rot = {}
        self.ninst = 0

    def _sem(self):
        h = self.ses.enter_context(self.nc.semaphore("s%d" % self.nsem))
        uid = self.nsem
        self.nsem += 1
        self.semh[uid] = h
        return uid

    def sb(self, name, shape, dt):
        self.nalloc = getattr(self, "nalloc", 0) + 1
        return self.es.enter_context(self.nc.sbuf_tensor("%s_%d" % (name, self.nalloc), list(shape), dt))

    def psum(self, name, shape, dt):
        return self.es.enter_context(self.nc.psum_tensor(name, list(shape), dt))

    def dram(self, name, shape, dt, kind):
        return self.nc.dram_tensor(name, list(shape), dt, kind=kind).ap()

    def rotating(self, name, n, shape, dt):
        self.rot[name] = [[self.sb("%s_%d" % (name, i), shape, dt) for i in range(n)], 0]

    def nxt(self, name):
        r = self.rot[name]
        t = r[0][r[1] % len(r[0])]
        r[1] += 1
        return t

    def _retire(self, uid, final):
        for e2 in self.engs:
            if self.seen[e2].get(uid, 0) < final:
                self.engs[e2].wait_ge(self.semh[uid], final)
        for e2 in self.engs:
            self.seen[e2].pop(uid, None)
        self.retired.add(uid)

    def _emit_waits(self, e, toks, embed=False):
        need = {}
        for uid, v in toks:
            if uid in self.retired:
                continue
            if need.get(uid, 0) < v:
                need[uid] = v
        todo = []
        for uid, v in need.items():
            if self.seen[e].get(uid, 0) >= v:
                continue
            if e == "pe" and uid == self.sem["pe"]:
                continue
            todo.append((uid, v))
        last = None
        if embed and EMBED_WAIT and todo:
            last = todo.pop()
        for uid, v in todo:
            self.engs[e].wait_ge(self.semh[uid], v)
            self.seen[e][uid] = v
        if last is not None:
            self.seen[e][last[0]] = last[1]
        return last

    def _collect(self, args, kw):
        reads, writes, nargs, nkw = [], [], [], {}

        def key_of(a):
            if isinstance(a, TA):
                return (a.ap.tensor.name, a.key), a.ap
            return (a.tensor.name, None), a

        for i, a in enumerate(args):
            if isinstance(a, TA) or _isap(a):
                k, ap = key_of(a)
                (writes if i == 0 else reads).append(k)
                nargs.append(ap)
            else:
                nargs.append(a)
        for kk, a in kw.items():
            if isinstance(a, TA) or _isap(a):
                k, ap = key_of(a)
                (writes if kk in ("out", "accum_out") else reads).append(k)
                nkw[kk] = ap
            else:
                nkw[kk] = a
        return reads, writes, nargs, nkw

    def _deps(self, reads, writes):
        toks = []
        for k in reads:
            b = self.bufs.get(k)
            if b:
                toks += b["w"]
        for k in writes:
            b = self.bufs.get(k)
            if b:
                toks += b["w"]
                toks += list(b["r"].items())
        return toks

    def _record(self, reads, writes, tok):
        for k in reads:
            b = self.bufs.setdefault(k, {"w": [], "r": {}})
            if b["r"].get(tok[0], 0) < tok[1]:
                b["r"][tok[0]] = tok[1]
        for k in writes:
            self.bufs[k] = {"w": [tok], "r": {}}

    def I(self, e, meth, *args, **kw):
        xr = kw.pop("_R", ())
        xw = kw.pop("_W", ())
        reads, writes, nargs, nkw = self._collect(args, kw)
        reads += list(xr)
        writes += list(xw)
        if self.cnt[e] >= SEM_LIMIT:
            self._retire(self.sem[e], self.cnt[e])
            self.sem[e] = self._sem()
            self.cnt[e] = 0
        last = self._emit_waits(e, self._deps(reads, writes), embed=True)
        ins = getattr(self.engs[e], meth)(*nargs, **nkw)
        if last is not None:
            ins._wait_ge(self.semh[last[0]], last[1])
        self.cnt[e] += 1
        ins.then_inc(self.semh[self.sem[e]], 1)
        self._record(reads, writes, (self.sem[e], self.cnt[e]))
        self.ninst += 1
        return ins

    def dma(self, q, out, in_, meth="dma_start", xr=(), **kw):
        reads, writes, _, nkw = self._collect((), dict(out=out, in_=in_, **kw))
        reads += list(xr)
        dq = self.dq[q]
        slot = dq["slots"][dq["i"] % NSLOT]
        dq["i"] += 1
        if slot[1] + 16 > SEM_LIMIT:
            self._retire(slot[0], slot[1])
            slot[0] = self._sem()
            slot[1] = 0
        toks = self._deps(reads, writes)
        if slot[1] > 0:
            toks.append((slot[0], slot[1]))
        last = self._emit_waits(q, toks, embed=True)
        ins = getattr(self.engs[q], meth)(**nkw)
        if last is not None:
            ins._wait_ge(self.semh[last[0]], last[1])
        slot[1] += 16
        ins.then_inc(self.semh[slot[0]], 16)
        self._record(reads, writes, (slot[0], slot[1]))
        self.ninst += 1
        return ins

    def barrier(self):
        toks = []
        for e in self.engs:
            if self.cnt[e] > 0:
                toks.append((self.sem[e], self.cnt[e]))
        for q, dq in self.dq.items():
            for uid, tgt in dq["slots"]:
                if tgt > 0:
                    toks.append((uid, tgt))
        for e in self.engs:
            self._emit_waits(e, toks)
        self.bufs = {}

    @contextlib.contextmanager
    def scope(self):
        outer = self.es
        outer_rot = dict(self.rot)
        self.es = contextlib.ExitStack()
        try:
            yield
        finally:
            self.barrier()
            self.es.close()
            self.es = outer
            self.rot = outer_rot

    def finish(self):
        for q, dq in self.dq.items():
            for uid, tgt in dq["slots"]:
                if tgt > 0 and uid not in self.retired and self.seen["sp"].get(uid, 0) < tgt:
                    self.engs["sp"].wait_ge(self.semh[uid], tgt)
        for e in self.engs:
            if self.cnt[e] > 0 and e != "sp":
                self.engs["sp"].wait_ge(self.semh[self.sem[e]], self.cnt[e])


class Prog:
    def __init__(self, cfg):
        self.cfg = cfg
        self.k = KB()
        k = self.k
        self.T = cfg.get("T", 256)
        T = self.T
        self.SEQ = cfg.get("SEQ", 2048)
        self.depth = cfg.get("depth", 4)
        self.parts = cfg.get("parts", ("memkv", "mixer", "cross", "ffn"))
        self.do_sample = cfg.get("sample", True)
        nev = (self.depth + 1) // 2
        nod = self.depth // 2
        self.nev, self.nod = nev, nod
        dep = self.depth
        dr = k.dram
        SEQ = self.SEQ
        self.xp = dr("xp", [SEQ, D], F32, "ExternalInput")
        self.mem = dr("mem", [256, D], F32, "ExternalInput")
        self.consts = dr("consts", [128, 1024], F32, "ExternalInput")
        self.norm_mix = dr("norm_mix", [dep, D], F32, "ExternalInput")
        self.norm_cross = dr("norm_cross", [dep, D], F32, "ExternalInput")
        self.norm_mem = dr("norm_mem", [dep, D], F32, "ExternalInput")
        self.norm_ffn = dr("norm_ffn", [dep, D], F32, "ExternalInput")
        self.w_mq = dr("w_mq", [dep, D, 512], F32, "ExternalInput")
        self.w_mkv = dr("w_mkv", [dep, D, 1024], F32, "ExternalInput")
        self.mem_q_norm = dr("mem_q_norm", [dep, 128], F32, "ExternalInput")
        self.mem_k_norm = dr("mem_k_norm", [dep, 128], F32, "ExternalInput")
        self.w_mo = dr("w_mo", [dep, 512, D], F32, "ExternalInput")
        self.w_up = dr("w_ffn_up", [dep, D, 2 * FFN_H], F32, "ExternalInput")
        self.w_down = dr("w_ffn_down", [dep, FFN_H, D], F32, "ExternalInput")
        self.w_in_even = dr("w_in_even", [nev, D, EVEN_IN], F32, "ExternalInput")
        self.w_out_even = dr("w_out_even", [nev, 3072, D], F32, "ExternalInput")
        self.fox_qn = dr("fox_q_norm", [nev, 128], F32, "ExternalInput")
        self.fox_kn = dr("fox_k_norm", [nev, 128], F32, "ExternalInput")
        self.ev_small = dr("ev_small", [nev, 136], F32, "ExternalInput")
        self.ssd_norm = dr("ssd_norm", [nev, D], F32, "ExternalInput")
        self.conv_wT = dr("conv_wT", [nev, 128, 24 * 5], F32, "ExternalInput")
        if nod:
            self.w_in_odd = dr("w_in_odd", [nod, D, D], F32, "ExternalInput")
            self.s5_w_glu = dr("s5_w_glu", [nod, D, 2 * D], F32, "ExternalInput")
            self.s5_par = dr("s5_par", [nod, 128, 3 * 64], F32, "ExternalInput")
            self.s5_Bp = dr("s5_Bp", [nod, 2, 64, 128, 128], F32, "ExternalInput")
            self.s5_Cp = dr("s5_Cp", [nod, 2, 64, 128, 128], F32, "ExternalInput")
            self.s5_D = dr("s5_D", [nod, 128, 16], F32, "ExternalInput")
            self.s5_bbT = dr("s5_bbT", [nod, 64, 128, 2, 128], BF16, "Internal")
        self.kt_scr = dr("kt_scr", [nev, 128, 8, SEQ], BF16, "Internal")
        self.v_scr = dr("v_scr", [nev, SEQ, 1024], BF16, "Internal")
        self.y_p = dr("y_p", [SEQ, D], F32, "ExternalOutput")
        self.mem_k_p = dr("mem_k_p", [dep, 256, 512], F32, "ExternalOutput")
        self.mem_v_p = dr("mem_v_p", [dep, 256, 512], F32, "ExternalOutput")
        self.fox_k_p = dr("fox_k_p", [nev, SEQ, 1024], F32, "ExternalOutput")
        self.fox_v_p = dr("fox_v_p", [nev, SEQ, 1024], F32, "ExternalOutput")
        self.fox_logf_p = dr("fox_logf_p", [nev, SEQ, 8], F32, "ExternalOutput")
        self.ssd_p = dr("ssd_p", [nev, 2048, 128], F32, "ExternalOutput")
        self.conv_p = dr("conv_p", [nev, 3, 3072], F32, "ExternalOutput")
        if nod:
            self.s5_re_p = dr("s5_re_p", [nod, 64, 128], F32, "ExternalOutput")
            self.s5_im_p = dr("s5_im_p", [nod, 64, 128], F32, "ExternalOutput")
        if self.do_sample:
            NP = cfg.get("n_pages", 128)
            NROWS = cfg.get("n_pool", 1280) * 128
            self.NP = NP
            self.xs_in = dr("xs", [8, D], F32, "ExternalInput")
            self.cfk = [dr("cfk%d" % jj, [NROWS, 1024], F32, "ExternalInput") for jj in range(nev)]
            self.cfv = [dr("cfv%d" % jj, [NROWS, 1024], F32, "ExternalInput") for jj in range(nev)]
            self.cfl = [dr("cfl%d" % jj, [NROWS, 8], F32, "ExternalInput") for jj in range(nev)]
            self.cmk = dr("cmk", [dep, 256, 512], F32, "ExternalInput")
            self.cmv = dr("cmv", [dep, 256, 512], F32, "ExternalInput")
            self.sssd = dr("sssd", [nev, 2048, 128], F32, "ExternalInput")
            self.sconv = dr("sconv", [nev, 3, 3072], F32, "ExternalInput")
            self.ptab = dr("ptab", [1, NP], I32, "ExternalInput")
            self.y_s = dr("y_s", [8, D], F32, "ExternalOutput")
            self.fox_k_s = dr("fox_k_s", [nev, 8, 1024], F32, "ExternalOutput")
            self.fox_v_s = dr("fox_v_s", [nev, 8, 1024], F32, "ExternalOutput")
            self.fox_logf_s = dr("fox_logf_s", [nev, 8, 8], F32, "ExternalOutput")
            self.ssd_s = dr("ssd_s", [nev, 2048, 128], F32, "ExternalOutput")
            self.conv_s = dr("conv_s", [nev, 3, 3072], F32, "ExternalOutput")
            self.kt_scr_s = dr("kt_scr_s", [nev, 128, 8, 8], BF16, "Internal")
            self.v_scr_s = dr("v_scr_s", [nev, 8, 1024], BF16, "Internal")
            if nod:
                self.s5re_in = dr("s5re_in", [nod, 64, 128], F32, "ExternalInput")
                self.s5im_in = dr("s5im_in", [nod, 64, 128], F32, "ExternalInput")
                self.s5_re_s = dr("s5_re_s", [nod, 64, 128], F32, "ExternalOutput")
                self.s5_im_s = dr("s5_im_s", [nod, 64, 128], F32, "ExternalOutput")
        if cfg.get("dbg"):
            self.dbg_of = dr("dbg_of", [SEQ // T, 128, 8, T], BF16, "ExternalOutput")
            self.dbg_yn = dr("dbg_yn", [SEQ // T, 128, 16, T], BF16, "ExternalOutput")
        NBK = max(1, T // 128)
        self.NBK = NBK
        sb = k.sb
        self.cst = sb("cst", [128, 512], F32)
        self.cbf = sb("cbf", [128, 384], BF16)
        self.onesf = sb("onesf", [128, 128], F32)
        self.maskneg = sb("maskneg", [128, 128], F32)
        self.iota = sb("iota", [128, 512], F32)
        self.x = sb("x", [128, NBK, D], F32)
        self.xnT = sb("xnT", [128, KC, T], BF16)
        k.rotating("wt", 2, [128, 16, 512], BF16)
        k.rotating("gain", 1, [128, D], F32)
        k.rotating("g128", 2, [128, 128], F32)
        k.rotating("stat", 6, [128, 16], F32)
        k.rotating("xnb", 2, [128, D], BF16)
        k.rotating("tmpf", 3, [128, 512], F32)
        k.rotating("tmpb", 3, [128, 512], BF16)
        k.rotating("pT", 3, [128, 512], BF16)
        k.rotating("stage", 3, [128, 512], F32)
        self.mkT = sb("mkT", [128, 4, 256], BF16)
        self.mv = sb("mv", [128, 2, 512], BF16)
        self.cqT = sb("cqT", [128, 4, T], BF16)
        self.coT = sb("coT", [128, 4, T], BF16)
        if nod:
            self.s5r = sb("s5r", [128, nod, 64], F32)
            self.s5th = sb("s5th", [128, nod, 64], F32)
            self.s5st = sb("s5st", [128, nod, 2, 64], F32)
            self.s5vi = sb("s5vi", [128, nod, 2, 64], F32)
            self.s5Dt = sb("s5Dt", [128, nod, 16], F32)
        self.ssdH = sb("ssdH", [128, nev, 2048], F32)
        self.convst = sb("convst", [128, nev, 24, 3], F32)
        self.foxtot = sb("foxtot", [128, nev, 8], F32)
        self.negc = sb("negc", [128, nev, SEQ // 128 if SEQ >= 128 else 1, 8], F32)
        self.evs = sb("evs", [128, nev, 136], F32)
        self.evA = sb("evA", [128, nev, 32], F32)
        self.cwT = sb("cwT", [128, nev, 24, 5], F32)
        self.psA = [k.psum("psA%d" % i, [128, 512], F32) for i in range(4)]
        self.psB = [k.psum("psB%d" % i, [128, 512], F32) for i in range(2)]
        self.psT = [k.psum("psT%d" % i, [128, 1024], BF16) for i in range(2)]
        self.iA = self.iB = self.iT = 0
        self.cpy = 0
        k.dma("sp", self.cst[:], self.consts[:, 0:512])
        k.I("dve", "tensor_copy", self.cbf[:, 0:128], self.cst[:, 0:128])
        k.I("dve", "memset", self.cbf[:, 128:256], 1.0)
        k.I("dve", "tensor_copy", self.cbf[:, 256:384], self.cst[:, 128:256])
        k.I("dve", "memset", self.onesf[:], 1.0)
        k.I("dve", "tensor_scalar", self.maskneg[:], self.cst[:, 128:256], -1.0, 30000.0, ALU.add, ALU.mult)
        k.dma("sp", self.iota[:], self.consts[0:1, 512:1024].partition_broadcast(128))
        for j in range(nev):
            k.dma("sp", self.evs[:, j, :], self.ev_small[j:j + 1, :].partition_broadcast(128))
            k.dma("sp", self.cwT[:, j, :, :], self.conv_wT[j].rearrange("p (m c) -> p m c", c=5))
            k.I("act", "activation", out=self.evA[:, j, :], in_=self.evs[:, j, 40:72], func=AF.Exp)
            k.I("dve", "tensor_scalar", self.evA[:, j, :], self.evA[:, j, :], -1.0, None, ALU.mult)

    @property
    def ident_f(self):
        return self.cst[:, 0:128]

    @property
    def U_f(self):
        return self.cst[:, 128:256]

    @property
    def ident_b(self):
        return self.cbf[:, 0:128]

    @property
    def ones_b(self):
        return self.cbf[:, 128:256]

    @property
    def maskU_b(self):
        return self.cbf[:, 256:384]

    def bankA(self):
        self.iA += 1
        return self.psA[self.iA % 4]

    def bankB(self):
        self.iB += 1
        return self.psB[self.iB % 2]

    def bankT(self):
        self.iT += 1
        return self.psT[self.iT % 2]

    def copy(self, out, in_):
        self.cpy += 1
        if self.cpy % 2:
            return self.k.I("act", "activation", out=out, in_=in_, func=AF.Copy)
        return self.k.I("dve", "tensor_copy", out, in_)

    def load_w(self, W, k0, kn, c0, w):
        k = self.k
        wt = k.nxt("wt")
        if not hasattr(self, "wcache"):
            self.wcache = {}
        key = (W.tensor.name, int(W.offset), k0, kn, c0, w)
        scr = self.wcache.get(key)
        if scr is None:
            src = W[k0 * 128:(k0 + kn) * 128, c0:c0 + w].rearrange("(kc p) n -> p kc n", p=128)
            k.dma("pool", wt[:, 0:kn, 0:w], src)
            if self.cfg.get("wcache", True):
                scr = k.dram("wscr%d" % len(self.wcache), [128, kn * w // 2], F32, "Internal")
                self.wcache[key] = scr
                k.dma("sp", scr[:, :].rearrange("p (a n) -> p a n", n=w // 2), wt[:, 0:kn, 0:w].bitcast(F32))
        else:
            k.dma(self.cfg.get("wq", "sp"), wt[:, 0:kn, 0:w].bitcast(F32), scr[:, :].rearrange("p (a n) -> p a n", n=w // 2))
        return wt

    def linear_tm(self, xT, kcn, W, c0, ncols, T, consume):
        k = self.k
        TB = min(T, 128)
        nb = T // TB
        ncb = (ncols + 511) // 512
        for cb in range(ncb):
            w = min(512, ncols - cb * 512)
            kgs = [(k0, min(16, kcn - k0)) for k0 in range(0, kcn, 16)]
            if len(kgs) == 1:
                wt = self.load_w(W, 0, kcn, c0 + cb * 512, w)
                for tb in range(nb):
                    ps = self.bankA()
                    for kc in range(kcn):
                        k.I("pe", "matmul", ps[:TB, :w], xT(kc, tb * TB, (tb + 1) * TB), wt[:, kc, :w],
                            start=(kc == 0), stop=(kc == kcn - 1))
                    consume(tb, cb, w, ps[:TB, :w])
            else:
                assert nb <= 4
                pss = [self.bankA() for _ in range(nb)]
                for (k0, kn) in kgs:
                    wt = self.load_w(W, k0, kn, c0 + cb * 512, w)
                    for tb in range(nb):
                        for kc in range(kn):
                            k.I("pe", "matmul", pss[tb][:TB, :w], xT(k0 + kc, tb * TB, (tb + 1) * TB), wt[:, kc, :w],
                                start=(k0 + kc == 0), stop=(k0 + kc == kcn - 1))
                for tb in range(nb):
                    consume(tb, cb, w, pss[tb][:TB, :w])

    def linear_fm(self, xT, kcn, W, c0, ncols, T, consume):
        k = self.k
        assert kcn <= 16
        ncb = (ncols + 511) // 512
        for cb in range(ncb):
            w = min(512, ncols - cb * 512)
            wt = self.load_w(W, 0, kcn, c0 + cb * 512, w)
            for mi in range((w + 127) // 128):
                mw = min(128, w - mi * 128)
                ps = self.bankA()
                for kc in range(kcn):
                    k.I("pe", "matmul", ps[:mw, :T], wt[:, kc, mi * 128:mi * 128 + mw], xT(kc, 0, T),
                        start=(kc == 0), stop=(kc == kcn - 1))
                consume(cb * 4 + mi, mw, ps[:mw, :T])

    def load_gain(self, row):
        g = self.k.nxt("gain")
        self.k.dma("sp", g[:], row.partition_broadcast(128))
        return g

    def load_g128(self, row):
        g = self.k.nxt("g128")
        self.k.dma("sp", g[:], row.partition_broadcast(128))
        return g

    def rstd(self, st, n, TB, c_in, c_out, width, extra=None):
        k = self.k
        k.I("act", "activation", out=st[:TB, 12:12 + width], in_=st[:TB, c_in:c_in + width], func=AF.Sqrt,
            scale=1.0 / n, bias=EPS)
        k.I("dve", "reciprocal", st[:TB, c_out:c_out + width], st[:TB, 12:12 + width])
        if extra is not None:
            k.I("dve", "tensor_scalar", st[:TB, c_out:c_out + width], st[:TB, c_out:c_out + width], extra, None, ALU.mult)

    def norm_rows(self, xb, g, TB, xnb):
        k = self.k
        st = k.nxt("stat")
        k.I("dve", "memset", st[:TB, 0:1], 0.0)
        k.I("act", "activation", out=xnb[:TB, :], in_=xb, func=AF.Square, accum_out=st[:TB, 0:1])
        self.rstd(st, D, TB, 0, 1, 1)
        k.I("dve", "scalar_tensor_tensor", xnb[:TB, :], xb, st[:TB, 1:2], g[:TB, :], ALU.mult, ALU.mult)

    def transpose16(self, xnb, TB, dst, tb, nchunk=16):
        k = self.k
        for c0 in range(0, nchunk, 8):
            n8 = min(8, nchunk - c0)
            pt = self.bankT()
            for c8 in range(n8):
                kc = c0 + c8
                k.I("pe", "transpose", pt[:, c8 * TB:(c8 + 1) * TB], xnb[:TB, kc * 128:(kc + 1) * 128],
                    self.ident_b[:TB, :TB])
            self.copy(dst[:, c0:c0 + n8, tb * TB:(tb + 1) * TB], pt[:, 0:n8 * TB].rearrange("p (c t) -> p c t", t=TB))

    def rmsnorm_T(self, xblk, gain_row, T):
        k = self.k
        TB = min(T, 128)
        nb = T // TB
        g = self.load_gain(gain_row)
        for tb in range(nb):
            xnb = k.nxt("xnb")
            self.norm_rows(xblk(tb), g, TB, xnb)
            self.transpose16(xnb, TB, self.xnT, tb)

    def xblk(self, TB):
        return lambda tb: TA(self.x[:TB, tb, :], tb)

    def resid_add(self, TB):
        def f(tb, cb, w, ps):
            xa = TA(self.x[:TB, tb, cb * 512:cb * 512 + w], tb)
            self.k.I("dve", "tensor_tensor", xa, xa, ps, ALU.add)
        return f

    def headnorm(self, ps, TB, nh, gain128, out_fn, extra=None):
        k = self.k
        st = k.nxt("stat")
        tf = k.nxt("tmpf")
        k.I("act", "activation", out=tf[:TB, :nh * 128], in_=ps, func=AF.Square)
        k.I("dve", "tensor_reduce", st[:TB, 0:nh], tf[:TB, :nh * 128].rearrange("p (h d) -> p h d", d=128), AX.X, ALU.add)
        self.rstd(st, 128, TB, 0, 4, nh, extra)
        for h in range(nh):
            k.I("dve", "scalar_tensor_tensor", out_fn(h), ps[:, h * 128:(h + 1) * 128], st[:TB, 4 + h:5 + h],
                gain128[:TB, :], ALU.mult, ALU.mult)

    def ffn(self, i, T):
        k = self.k
        TB = min(T, 128)
        with k.scope():
            hT = k.sb("hT", [128, 44, T], BF16)
            gbuf = k.sb("gbuf", [128, self.NBK, 512], BF16)
            self.rmsnorm_T(self.xblk(TB), self.norm_ffn[i:i + 1, :], T)
            xT = lambda kc, a, b: self.xnT[:, kc, a:b]
            for j in range(11):
                def c_gate(tb, cb, w, ps, j=j):
                    k.I("act", "activation", out=TA(gbuf[:TB, tb, :], tb), in_=ps, func=AF.Silu)

                def c_up(tb, cb, w, ps, j=j):
                    hb = k.nxt("tmpb")
                    k.I("dve", "tensor_tensor", hb[:TB, :], TA(gbuf[:TB, tb, :], tb), ps, ALU.mult)
                    pt = self.bankT()
                    for c in range(4):
                        k.I("pe", "transpose", pt[:, c * TB:(c + 1) * TB], hb[:TB, c * 128:(c + 1) * 128], self.ident_b[:TB, :TB])
                    self.copy(TA(hT[:, j * 4:j * 4 + 4, tb * TB:(tb + 1) * TB], (j, tb)),
                              pt[:, 0:4 * TB].rearrange("p (c t) -> p c t", t=TB))
                self.linear_tm(xT, KC, self.w_up[i], j * 512, 512, T, c_gate)
                self.linear_tm(xT, KC, self.w_up[i], FFN_H + j * 512, 512, T, c_up)
            hTf = lambda kc, a, b: TA(hT[:, kc, a:b], (kc // 4, a // TB))
            self.linear_tm(hTf, 44, self.w_down[i], 0, D, T, self.resid_add(TB))

    def mem_kv_prompt(self):
        k = self.k
        for tb in range(2):
            k.dma("sp", TA(self.x[:, 0, :], 0), self.mem[tb * 128:(tb + 1) * 128, :])
            for i in range(self.depth):
                g = self.load_gain(self.norm_mem[i:i + 1, :])
                gk = self.load_g128(self.mem_k_norm[i:i + 1, :])
                xnb = k.nxt("xnb")
                self.norm_rows(TA(self.x[:, 0, :], 0), g, 128, xnb)
                self.transpose16(xnb, 128, self.xnT, 0)
                xT = lambda kc, a, b: self.xnT[:, kc, a:b]

                def cons(tb_, cb, w, ps, i=i, gk=gk, tb=tb):
                    stg = k.nxt("stage")
                    if cb == 0:
                        self.headnorm(ps, 128, 4, gk, lambda h: stg[:, h * 128:(h + 1) * 128])
                        k.dma("sp", self.mem_k_p[i, tb * 128:(tb + 1) * 128, :], stg[:, :])
                    else:
                        k.I("act", "activation", out=stg[:, :], in_=ps, func=AF.Copy)
                        k.dma("sp", self.mem_v_p[i, tb * 128:(tb + 1) * 128, :], stg[:, :])
                self.linear_tm(xT, KC, self.w_mkv[i], 0, 1024, 128, cons)

    def load_mem_kv(self, kd, vd):
        k = self.k
        for mb in range(2):
            kb = k.nxt("tmpb")
            k.dma("pool", kb[:, :], kd[mb * 128:(mb + 1) * 128, :])
            k.dma("pool", self.mv[:, mb, :], vd[mb * 128:(mb + 1) * 128, :])
            pt = self.bankT()
            for h in range(4):
                k.I("pe", "transpose", pt[:, h * 128:(h + 1) * 128], kb[:, h * 128:(h + 1) * 128], self.ident_b)
            self.copy(self.mkT[:, :, mb * 128:(mb + 1) * 128], pt[:, 0:512].rearrange("p (h t) -> p h t", t=128))

    def cross(self, i, T, kd, vd):
        k = self.k
        TB = min(T, 128)
        self.load_mem_kv(kd, vd)
        self.rmsnorm_T(self.xblk(TB), self.norm_cross[i:i + 1, :], T)
        gq = self.load_g128(self.mem_q_norm[i:i + 1, :])
        xT = lambda kc, a, b: self.xnT[:, kc, a:b]

        def qcons(tb, cb, w, ps):
            qb = k.nxt("tmpb")
            self.headnorm(ps, TB, 4, gq, lambda h: qb[:TB, h * 128:(h + 1) * 128])
            pt = self.bankT()
            for h in range(4):
                k.I("pe", "transpose", pt[:, h * TB:(h + 1) * TB], qb[:TB, h * 128:(h + 1) * 128], self.ident_b[:TB, :TB])
            self.copy(self.cqT[:, :, tb * TB:(tb + 1) * TB], pt[:, 0:4 * TB].rearrange("p (h t) -> p h t", t=TB))
        self.linear_tm(xT, KC, self.w_mq[i], 0, 512, T, qcons)
        scale = 128.0 ** -0.5
        for h in range(4):
            pso = self.psA[2 + h % 2]
            psd = self.psB[h % 2]
            for mb in range(2):
                pss = self.psA[mb]
                k.I("pe", "matmul", pss[:, :T], self.mkT[:, h, mb * 128:(mb + 1) * 128], self.cqT[:, h, :T],
                    start=True, stop=True)
                pT = k.nxt("pT")
                k.I("act", "activation", out=pT[:, :T], in_=pss[:, :T], func=AF.Exp, scale=scale)
                k.I("pe", "matmul", pso[:, :T], self.mv[:, mb, h * 128:(h + 1) * 128], pT[:, :T],
                    start=(mb == 0), stop=(mb == 1))
                k.I("pe", "matmul", psd[:, :T], self.ones_b, pT[:, :T], start=(mb == 0), stop=(mb == 1))
            rd = k.nxt("tmpf")
            k.I("dve", "reciprocal", rd[:, :T], psd[:, :T])
            k.I("dve", "tensor_tensor", self.coT[:, h, :T], pso[:, :T], rd[:, :T], ALU.mult)
        cT = lambda kc, a, b: self.coT[:, kc, a:b]
        self.linear_tm(cT, 4, self.w_mo[i], 0, D, T, self.resid_add(TB))

    def s5_setup(self, j):
        k = self.k
        with k.scope():
            W = k.sb("s5w", [128, 18, 64], F32)
            k.dma("sp", W[:, 0:3, :], self.s5_par[j].rearrange("p (a i) -> p a i", i=64))
            k.dma("sp", self.s5Dt[:, j, :], self.s5_D[j])
            A_re, A_im, ldt = W[:, 0, :], W[:, 1, :], W[:, 2, :]
            lam_re, dt, r, th = W[:, 3, :], W[:, 4, :], self.s5r[:, j, :], self.s5th[:, j, :]
            k.I("dve", "tensor_scalar", lam_re, A_re, -1e-4, None, ALU.min)
            k.I("act", "activation", out=dt, in_=ldt, func=AF.Exp)
            k.I("dve", "tensor_tensor", W[:, 5, :], lam_re, dt, ALU.mult)
            k.I("act", "activation", out=r, in_=W[:, 5, :], func=AF.Exp)
            k.I("dve", "tensor_tensor", th, A_im, dt, ALU.mult)
            k.I("dve", "tensor_scalar", th, th, 1.0 / TWO_PI, None, ALU.mult)
            k.I("dve", "tensor_scalar", W[:, 6, :], th, MAGIC, MAGIC, ALU.add, ALU.subtract)
            k.I("dve", "tensor_tensor", W[:, 6, :], th, W[:, 6, :], ALU.subtract)
            k.I("act", "activation", out=W[:, 7, :], in_=W[:, 6, :], func=AF.Abs)
            k.I("act", "activation", out=W[:, 8, :], in_=W[:, 6, :], func=AF.Sin, scale=TWO_PI)
            k.I("act", "activation", out=W[:, 9, :], in_=W[:, 7, :], func=AF.Sin, scale=-TWO_PI, bias=0.5 * math.pi)
            ni, nr = W[:, 10, :], W[:, 11, :]
            k.I("dve", "tensor_tensor", ni, W[:, 8, :], r, ALU.mult)
            k.I("dve", "tensor_tensor", nr, W[:, 9, :], r, ALU.mult)
            k.I("dve", "tensor_scalar", nr, nr, -1.0, None, ALU.add)
            den, t1, t2 = W[:, 12, :], W[:, 13, :], W[:, 14, :]
            k.I("dve", "tensor_tensor", den, lam_re, lam_re, ALU.mult)
            k.I("dve", "tensor_tensor", t1, A_im, A_im, ALU.mult)
            k.I("dve", "tensor_tensor", den, den, t1, ALU.add)
            k.I("dve", "reciprocal", den, den)
            cre, cim = W[:, 15, :], W[:, 16, :]
            k.I("dve", "tensor_tensor", t1, nr, lam_re, ALU.mult)
            k.I("dve", "tensor_tensor", t2, ni, A_im, ALU.mult)
            k.I("dve", "tensor_tensor", t1, t1, t2, ALU.add)
            k.I("dve", "tensor_tensor", cre, t1, den, ALU.mult)
            k.I("dve", "tensor_tensor", t1, ni, lam_re, ALU.mult)
            k.I("dve", "tensor_tensor", t2, nr, A_im, ALU.mult)
            k.I("dve", "tensor_tensor", t1, t1, t2, ALU.subtract)
            k.I("dve", "tensor_tensor", cim, t1, den, ALU.mult)
            k.I("dve", "tensor_scalar", W[:, 17, :], cim, -1.0, None, ALU.mult)
            ncim = W[:, 17, :]
            for i in range(64):
                br = k.nxt("stage")
                bi = k.nxt("stage")
                k.dma("sp", br[:, 0:128], self.s5_Bp[j, 0, i])
                k.dma("sp", bi[:, 0:128], self.s5_Bp[j, 1, i])
                o = k.nxt("tmpb")
                t = k.nxt("tmpf")
                k.I("dve", "tensor_scalar", t[:, 0:128], bi[:, 0:128], ncim[:, i:i + 1], None, ALU.mult)
                k.I("dve", "scalar_tensor_tensor", o[:, 0:128], br[:, 0:128], cre[:, i:i + 1], t[:, 0:128], ALU.mult, ALU.add)
                k.I("dve", "tensor_scalar", t[:, 128:256], br[:, 0:128], cim[:, i:i + 1], None, ALU.mult)
                k.I("dve", "scalar_tensor_tensor", o[:, 128:256], bi[:, 0:128], cre[:, i:i + 1], t[:, 128:256], ALU.mult, ALU.add)
                pt = self.bankT()
                k.I("pe", "transpose", pt[:, 0:128], o[:, 0:128], self.ident_b)
                k.I("pe", "transpose", pt[:, 128:256], o[:, 128:256], self.ident_b)
                o2 = k.nxt("tmpb")
                self.copy(o2[:, 0:256], pt[:, 0:256])
                k.dma("sp", self.s5_bbT[j, i], o2[:, 0:256].rearrange("p (a c) -> p a c", c=128))

    def s5_init_state(self, j):
        k = self.k
        k.I("dve", "memset", self.s5st[:, j, :, :], 0.0)
        k.I("dve", "memset", self.s5vi[:, j, :, :], 0.0)

    def s5_mixer(self, i, T):
        k = self.k
        j = i // 2
        TB = min(T, 128)
        with k.scope():
            uT = k.sb("uT", [128, KC, T], BF16)
            gT = k.sb("gT", [128, KC, T], BF16)
            abuf = k.sb("abuf", [128, self.NBK, 512], F32)
            k.rotating("s5lw", 2, [128, 4, 4, 128], BF16)
            k.rotating("s5f", 14, [128, T], F32)
            k.rotating("s5b", 4, [128, T], BF16)
            self.rmsnorm_T(self.xblk(TB), self.norm_mix[i:i + 1, :], T)
            xT = lambda kc, a, b: self.xnT[:, kc, a:b]

            def ucons(m, mw, ps):
                self.copy(TA(uT[:, m, :T], m), ps)
            self.linear_fm(xT, KC, self.w_in_odd[j], 0, D, T, ucons)
            nb2 = 0
            for cc in range(16):
                lw = k.nxt("s5lw")
                k.dma("sp", lw[:, :, 0:2, :], self.s5_bbT[j, cc * 4:(cc + 1) * 4].rearrange("i p a c -> p i a c"))
                for a in range(2):
                    k.dma("pool", lw[:, :, 2 + a, :], self.s5_Cp[j, a, cc * 4:(cc + 1) * 4].rearrange("i p c -> p i c"))
                psy = self.psA[cc % 2]
                for i4 in range(4):
                    ti = cc * 4 + i4
                    nb2 += 1
                    pbr = (self.psA[2], self.psB[0])[nb2 % 2]
                    pbi = (self.psA[3], self.psB[1])[nb2 % 2]
                    k.I("pe", "matmul", pbr[:, :T], lw[:, i4, 0, :], TA(uT[:, cc, :T], cc), start=True, stop=True)
                    k.I("pe", "matmul", pbi[:, :T], lw[:, i4, 1, :], TA(uT[:, cc, :T], cc), start=True, stop=True)
                    f = lambda: k.nxt("s5f")
                    ph, ph2, S, C = f(), f(), f(), f()
                    k.I("dve", "tensor_scalar", ph[:, :T], self.iota[:, :T], self.s5th[:, j, ti:ti + 1], None, ALU.mult)
                    k.I("dve", "tensor_scalar", ph2[:, :T], ph[:, :T], MAGIC, MAGIC, ALU.add, ALU.subtract)
                    k.I("dve", "tensor_tensor", ph[:, :T], ph[:, :T], ph2[:, :T], ALU.subtract)
                    k.I("act", "activation", out=ph2[:, :T], in_=ph[:, :T], func=AF.Abs)
                    k.I("act", "activation", out=S[:, :T], in_=ph[:, :T], func=AF.Sin, scale=TWO_PI)
                    k.I("act", "activation", out=C[:, :T], in_=ph2[:, :T], func=AF.Sin, scale=-TWO_PI, bias=0.5 * math.pi)
                    t1, t2, cre, cim = f(), f(), f(), f()
                    k.I("dve", "tensor_tensor", t1[:, :T], C[:, :T], pbr[:, :T], ALU.mult)
                    k.I("dve", "tensor_tensor", t2[:, :T], S[:, :T], pbi[:, :T], ALU.mult)
                    k.I("dve", "tensor_tensor", cre[:, :T], t1[:, :T], t2[:, :T], ALU.add)
                    k.I("dve", "tensor_tensor", t1[:, :T], C[:, :T], pbi[:, :T], ALU.mult)
                    k.I("dve", "tensor_tensor", t2[:, :T], S[:, :T], pbr[:, :T], ALU.mult)
                    k.I("dve", "tensor_tensor", cim[:, :T], t1[:, :T], t2[:, :T], ALU.subtract)
                    vre, vim = f(), f()
                    rb = self.s5r[:, j, ti:ti + 1].to_broadcast([128, T])
                    k.I("dve", "tensor_tensor_scan", vre[:, :T], rb, cre[:, :T], self.s5vi[:, j, 0, ti:ti + 1], ALU.mult, ALU.add)
                    k.I("dve", "tensor_tensor_scan", vim[:, :T], rb, cim[:, :T], self.s5vi[:, j, 1, ti:ti + 1], ALU.mult, ALU.add)
                    sre, simn = f(), f()
                    k.I("dve", "tensor_tensor", t1[:, :T], C[:, :T], vre[:, :T], ALU.mult)
                    k.I("dve", "tensor_tensor", t2[:, :T], S[:, :T], vim[:, :T], ALU.mult)
                    k.I("dve", "tensor_tensor", sre[:, :T], t1[:, :T], t2[:, :T], ALU.subtract)
                    k.I("dve", "tensor_tensor", t1[:, :T], C[:, :T], vim[:, :T], ALU.mult)
                    k.I("dve", "tensor_tensor", t2[:, :T], S[:, :T], vre[:, :T], ALU.mult)
                    k.I("dve", "scalar_tensor_tensor", simn[:, :T], t1[:, :T], -1.0, t2[:, :T], ALU.mult, ALU.subtract)
                    sb1, sb2 = k.nxt("s5b"), k.nxt("s5b")
                    k.I("act", "activation", out=sb1[:, :T], in_=sre[:, :T], func=AF.Copy)
                    k.I("act", "activation", out=sb2[:, :T], in_=simn[:, :T], func=AF.Copy)
                    k.I("act", "activation", out=self.s5st[:, j, 0, ti:ti + 1], in_=sre[:, T - 1:T], func=AF.Copy)
                    k.I("act", "activation", out=self.s5st[:, j, 1, ti:ti + 1], in_=simn[:, T - 1:T], func=AF.Copy)
                    k.I("act", "activation", out=self.s5vi[:, j, 0, ti:ti + 1], in_=sre[:, T - 1:T], func=AF.Copy)
                    k.I("act", "activation", out=self.s5vi[:, j, 1, ti:ti + 1], in_=simn[:, T - 1:T], func=AF.Copy, scale=-1.0)
                    k.I("pe", "matmul", psy[:, :T], lw[:, i4, 2, :], sb1[:, :T], start=(i4 == 0), stop=False)
                    k.I("pe", "matmul", psy[:, :T], lw[:, i4, 3, :], sb2[:, :T], start=False, stop=(i4 == 3))
                yf = k.nxt("tmpf")
                k.I("dve", "scalar_tensor_tensor", yf[:, :T], TA(uT[:, cc, :T], cc), self.s5Dt[:, j, cc:cc + 1], psy[:, :T],
                    ALU.mult, ALU.add)
                k.I("act", "activation", out=TA(gT[:, cc, :T], cc), in_=yf[:, :T], func=AF.Gelu)
            gTf = lambda kc, a, b: TA(gT[:, kc, a:b], kc)
            for cb in range(4):
                def acons(tb, cb_, w, ps, cb=cb):
                    self.copy(TA(abuf[:TB, tb, :], tb), ps)

                def gcons(tb, cb_, w, ps, cb=cb):
                    sg = k.nxt("tmpf")
                    k.I("act", "activation", out=sg[:TB, :], in_=ps, func=AF.Sigmoid)
                    k.I("dve", "tensor_tensor", sg[:TB, :], sg[:TB, :], TA(abuf[:TB, tb, :], tb), ALU.mult)
                    xa = TA(self.x[:TB, tb, cb * 512:(cb + 1) * 512], tb)
                    k.I("dve", "tensor_tensor", xa, xa, sg[:TB, :], ALU.add)
                self.linear_tm(gTf, KC, self.s5_w_glu[j], cb * 512, 512, T, acons)
                self.linear_tm(gTf, KC, self.s5_w_glu[j], D + cb * 512, 512, T, gcons)

    def s5_store_state(self, j, re_out, im_out):
        k = self.k
        ps = self.bankB()
        k.I("pe", "transpose", ps[0:64, 0:128], self.s5st[:, j, 0, :], self.ident_f)
        k.I("pe", "transpose", ps[0:64, 128:256], self.s5st[:, j, 1, :], self.ident_f)
        o = k.nxt("stage")
        k.I("act", "activation", out=o[0:64, 0:128], in_=ps[0:64, 0:128], func=AF.Copy)
        k.I("dve", "tensor_scalar", o[0:64, 128:256], ps[0:64, 128:256], -1.0, None, ALU.mult)
        k.dma("sp", re_out, o[0:64, 0:128])
        k.dma("sp", im_out, o[0:64, 128:256])

    def even_init_state(self, j):
        k = self.k
        k.I("dve", "memset", self.ssdH[:, j, :], 0.0)
        k.I("dve", "memset", self.convst[:, j, :, :], 0.0)
        k.I("dve", "memset", self.foxtot[:, j, :], 0.0)

    def even_mixer(self, i, T, t0, seq):
        k = self.k
        j = i // 2
        TB = min(T, 128)
        nb = T // TB
        W = self.w_in_even[j]
        evs = self.evs
        kt_scr, v_scr = seq["kt_scr"], seq["v_scr"]
        with k.scope():
            ofT = k.sb("ofT", [128, 8, T], BF16)
            ynT = k.sb("ynT", [128, 16, T], BF16)
            self.rmsnorm_T(self.xblk(TB), self.norm_mix[i:i + 1, :], T)
            xT = lambda kc, a, b: self.xnT[:, kc, a:b]
            with k.scope():
                qT = k.sb("qT", [128, 8, T], BF16)
                cTq = k.sb("cTq", [8, T], F32)
                cTm = k.sb("cTm", [8, 8, T], F32)
                k.rotating("kth", 2, [128, t0 + T], BF16)
                k.rotating("vh", 2, [128, (t0 + T + 127) // 128, 128], BF16)
                k.rotating("kst", 2, [128, 4, TB], BF16)
                k.rotating("sm32", 4, [128, 32], F32)
                gq = self.load_g128(self.fox_qn[j:j + 1, :])
                gk = self.load_g128(self.fox_kn[j:j + 1, :])

                def qkv(tb, cb, w, ps):
                    tok0 = t0 + tb * TB
                    if cb < 2:
                        qb = k.nxt("tmpb")
                        self.headnorm(ps, TB, 4, gq, lambda h: qb[:TB, h * 128:(h + 1) * 128], extra=128.0 ** -0.5)
                        pt = self.bankT()
                        for h in range(4):
                            k.I("pe", "transpose", pt[:, h * TB:(h + 1) * TB], qb[:TB, h * 128:(h + 1) * 128], self.ident_b[:TB, :TB])
                        self.copy(qT[:, cb * 4:cb * 4 + 4, tb * TB:(tb + 1) * TB], pt[:, 0:4 * TB].rearrange("p (h t) -> p h t", t=TB))
                    elif cb < 4:
                        c2 = cb - 2
                        stg = k.nxt("stage")
                        self.headnorm(ps, TB, 4, gk, lambda h: stg[:TB, h * 128:(h + 1) * 128])
                        k.dma("sp", seq["fox_k"][j, tok0:tok0 + TB, c2 * 512:(c2 + 1) * 512], stg[:TB, :])
                        kb = k.nxt("tmpb")
                        k.I("dve", "tensor_copy", kb[:TB, :], stg[:TB, :])
                        pt = self.bankT()
                        for h in range(4):
                            k.I("pe", "transpose", pt[:, h * TB:(h + 1) * TB], kb[:TB, h * 128:(h + 1) * 128], self.ident_b[:TB, :TB])
                        kst = k.nxt("kst")
                        self.copy(kst[:, :, :TB], pt[:, 0:4 * TB].rearrange("p (h t) -> p h t", t=TB))
                        k.dma("sp", kt_scr[j, :, c2 * 4:c2 * 4 + 4, tok0:tok0 + TB], kst[:, :, :TB])
                    else:
                        c2 = cb - 4
                        stg = k.nxt("stage")
                        k.I("act", "activation", out=stg[:TB, :], in_=ps, func=AF.Copy)
                        k.dma("sp", seq["fox_v"][j, tok0:tok0 + TB, c2 * 512:(c2 + 1) * 512], stg[:TB, :])
                        vb = k.nxt("tmpb")
                        k.I("dve", "tensor_copy", vb[:TB, :], stg[:TB, :])
                        k.dma("sp", v_scr[j, tok0:tok0 + TB, c2 * 512:(c2 + 1) * 512], vb[:TB, :])
                self.linear_tm(xT, KC, W, 0, 3072, T, qkv)

                def fcons(tb, cb, w, ps):
                    tok0 = t0 + tb * TB
                    blk = tok0 // 128
                    s1 = k.nxt("sm32")
                    s2 = k.nxt("sm32")
                    k.I("dve", "tensor_tensor", s1[:TB, 0:8], ps, evs[:TB, j, 0:8], ALU.add)
                    k.I("act", "activation", out=s1[:TB, 8:16], in_=s1[:TB, 0:8], func=AF.Exp, scale=-1.0)
                    k.I("act", "activation", out=s1[:TB, 16:24], in_=s1[:TB, 8:16], func=AF.Ln, bias=1.0)
                    k.I("dve", "tensor_scalar", s2[:TB, 0:8], s1[:TB, 16:24], -1.0, None, ALU.mult)
                    k.dma("sp", seq["fox_lf"][j, tok0:tok0 + TB, :], s2[:TB, 0:8])
                    pc = self.bankB()
                    k.I("pe", "matmul", pc[:TB, 0:8], self.U_f[:TB, :TB], s2[:TB, 0:8], start=True, stop=True)
                    k.I("pe", "matmul", pc[:, 8:16], self.onesf[:TB, :], s2[:TB, 0:8], start=True, stop=True)
                    k.I("dve", "tensor_tensor", s2[:TB, 8:16], pc[:TB, 0:8], self.foxtot[:TB, j, :], ALU.add)
                    k.I("dve", "tensor_scalar", self.negc[:TB, j, blk, :], s2[:TB, 8:16], -1.0, None, ALU.mult)
                    k.I("dve", "tensor_tensor", self.foxtot[:, j, :], self.foxtot[:, j, :], pc[:, 8:16], ALU.add)
                    pc2 = self.bankB()
                    k.I("pe", "transpose", pc2[0:8, 0:TB], s2[:TB, 8:16], self.ident_f[:TB, :TB])
                    k.I("act", "activation", out=cTq[0:8, tb * TB:(tb + 1) * TB], in_=pc2[0:8, 0:TB], func=AF.Copy)
                self.linear_tm(xT, KC, W, 3072, 8, T, fcons)
                for h in range(8):
                    k.I("dve", "tensor_scalar", cTm[0:8, h, :], cTq[0:8, :], self.ident_f[0:8, h:h + 1], None, ALU.mult)

                if seq.get("past"):
                    self.attention_sample(j, T, qT, cTm, ofT, seq)
                else:
                    tend = t0 + T
                    nkb = (tend + 127) // 128
                    for h in range(8):
                        kth = k.nxt("kth")
                        vh = k.nxt("vh")
                        k.dma("sp", kth[:, 0:tend], kt_scr[j, :, h, 0:tend])
                        k.dma("sp", vh[:, 0:nkb, :], v_scr[j, 0:tend, h * 128:(h + 1) * 128].rearrange("(b s) d -> s b d", s=128))
                        pso = self.psA[2 + h % 2]
                        psd = self.psB[h % 2]
                        for kb in range(nkb):
                            ks = min(128, tend - kb * 128)
                            q0 = max(0, kb * 128 - t0)
                            N = T - q0
                            pss = self.psA[kb % 2]
                            k.I("pe", "matmul", pss[:ks, :N], kth[:, kb * 128:kb * 128 + ks], qT[:, h, q0:T], start=True, stop=False)
                            k.I("pe", "matmul", pss[:ks, :N], self.onesf[0:8, 0:ks], cTm[0:8, h, q0:T], start=False, stop=True)
                            pT = k.nxt("pT")
                            k.I("act", "activation", out=pT[:ks, :N], in_=pss[:ks, :N], func=AF.Exp, bias=self.negc[:ks, j, kb, h:h + 1])
                            if kb * 128 >= t0:
                                dw = min(128, N)
                                k.I("dve", "tensor_tensor", pT[:ks, 0:dw], pT[:ks, 0:dw], self.maskU_b[:ks, 0:dw], ALU.mult)
                            k.I("pe", "matmul", pso[:, q0:T], vh[:ks, kb, :], pT[:ks, :N], start=(kb == 0), stop=(kb == nkb - 1))
                            k.I("pe", "matmul", psd[:, q0:T], self.ones_b[:ks, :], pT[:ks, :N], start=(kb == 0), stop=(kb == nkb - 1))
                        rd = k.nxt("tmpf")
                        k.I("dve", "reciprocal", rd[:, :T], psd[:, :T])
                        k.I("dve", "tensor_tensor", ofT[:, h, :T], pso[:, :T], rd[:, :T], ALU.mult)

            with k.scope():
                zs = k.sb("zs", [128, nb, D], BF16)
                xbcT = k.sb("xbcT", [128, 24, T], BF16)
                dtt = k.sb("dtt", [128, nb, 32], F32)
                att = k.sb("att", [128, nb, 32], F32)
                k.rotating("craw", 2, [128, T + 3], F32)
                k.rotating("sm32", 8, [128, 32], F32)
                k.rotating("sq", 6, [128, 128], F32)
                k.rotating("sqb", 3, [128, 128], BF16)
                k.rotating("cbs", 2, [128, 128], F32)
                k.rotating("acm", 4, [32, 128], F32)
                xs_tm = k.sb("xs_tm", [128, D], BF16)
                B_tm = k.sb("B_tm", [128, 512], BF16)
                xdt = k.sb("xdt", [128, D], BF16)
                xdtw = k.sb("xdtw", [128, D], BF16)
                ysb = k.sb("ysb", [128, D], F32)
                hbf = k.sb("hbf", [128, D], BF16)
                acT = k.sb("acT", [32, 128], F32)

                def zcons(tb, cb, w, ps):
                    k.I("act", "activation", out=TA(zs[:TB, tb, cb * 512:(cb + 1) * 512], tb), in_=ps, func=AF.Silu)
                self.linear_tm(xT, KC, W, 3080, 2048, T, zcons)

                def xbc_cons(m, mw, ps):
                    cr = k.nxt("craw")
                    k.I("dve", "tensor_copy", cr[:, 0:3], self.convst[:, j, m, :])
                    k.I("act", "activation", out=cr[:, 3:3 + T], in_=ps, func=AF.Copy)
                    k.I("dve", "tensor_copy", self.convst[:, j, m, :], cr[:, T:T + 3])
                    acc = k.nxt("tmpf")
                    cw = self.cwT[:, j, m, :]
                    k.I("dve", "tensor_scalar", acc[:, :T], cr[:, 0:T], cw[:, 0:1], cw[:, 4:5], ALU.mult, ALU.add)
                    for kk in range(1, 4):
                        k.I("dve", "scalar_tensor_tensor", acc[:, :T], cr[:, kk:kk + T], cw[:, kk:kk + 1], acc[:, :T], ALU.mult, ALU.add)
                    k.I("act", "activation", out=TA(xbcT[:, m, :T], m), in_=acc[:, :T], func=AF.Silu)
                self.linear_fm(xT, KC, W, 5128, 3072, T, xbc_cons)

                def dtcons(tb, cb, w, ps):
                    s1 = k.nxt("sm32")
                    k.I("dve", "tensor_tensor", s1[:TB, :], ps, evs[:TB, j, 8:40], ALU.add)
                    k.I("act", "activation", out=s1[:TB, :], in_=s1[:TB, :], func=AF.Exp)
                    k.I("act", "activation", out=TA(dtt[:TB, tb, :], tb), in_=s1[:TB, :], func=AF.Ln, bias=1.0)
                    k.I("dve", "tensor_tensor", TA(att[:TB, tb, :], tb), TA(dtt[:TB, tb, :], tb), self.evA[:TB, j, :], ALU.mult)
                self.linear_tm(xT, KC, W, 8200, 32, T, dtcons)

                gss = self.load_gain(self.ssd_norm[j:j + 1, :])
                H = self.ssdH
                dttn = dtt[:].tensor.name
                for tb in range(nb):
                    cs = slice(tb * TB, (tb + 1) * TB)
                    a_t = TA(att[:TB, tb, :], tb)
                    pa = self.bankB()
                    k.I("pe", "matmul", pa[:TB, 0:32], self.U_f[:TB, :TB], a_t, start=True, stop=True)
                    k.I("pe", "matmul", pa[:, 32:64], self.onesf[:TB, :], a_t, start=True, stop=True)
                    acum = k.nxt("sm32")
                    k.I("act", "activation", out=acum[:TB, :], in_=pa[:TB, 0:32], func=AF.Copy)
                    expa = k.nxt("sm32")
                    k.I("act", "activation", out=expa[:TB, :], in_=pa[:TB, 0:32], func=AF.Exp)
                    cd = k.nxt("sm32")
                    k.I("act", "activation", out=cd[:, :], in_=pa[:, 32:64], func=AF.Exp)
                    wgt = k.nxt("sm32")
                    k.I("dve", "tensor_tensor", wgt[:TB, :], pa[:TB, 32:64], acum[:TB, :], ALU.subtract)
                    k.I("act", "activation", out=wgt[:TB, :], in_=wgt[:TB, :], func=AF.Exp)
                    k.I("dve", "tensor_tensor", wgt[:TB, :], wgt[:TB, :], TA(dtt[:TB, tb, :], tb), ALU.mult)
                    pa2 = self.bankB()
                    k.I("pe", "transpose", pa2[0:32, 0:TB], acum[:TB, :], self.ident_f[:TB, :TB])
                    k.I("act", "activation", out=acT[:, 0:TB], in_=pa2[0:32, 0:TB], func=AF.Copy)
                    for c0 in range(0, 20, 8):
                        n8 = min(8, 20 - c0)
                        pt = self.bankT()
                        for c8 in range(n8):
                            m = c0 + c8
                            k.I("pe", "transpose", pt[:TB, c8 * 128:(c8 + 1) * 128], TA(xbcT[:, m, cs], m), self.ident_b)
                        if c0 < 16:
                            self.copy(xs_tm[:TB, c0 * 128:(c0 + n8) * 128], pt[:TB, 0:n8 * 128])
                        else:
                            self.copy(B_tm[:TB, 0:512], pt[:TB, 0:512])
                    v3 = lambda t: t.rearrange("p (h d) -> p h d", d=64)
                    k.I("dve", "tensor_tensor", v3(xdt[:TB, :]), v3(xs_tm[:TB, :]),
                        dtt[:TB, tb, :].unsqueeze(2).to_broadcast([TB, 32, 64]), ALU.mult, _R=[(dttn, tb)])
                    k.I("dve", "tensor_tensor", v3(xdtw[:TB, :]), v3(xs_tm[:TB, :]),
                        wgt[:TB, :].unsqueeze(2).to_broadcast([TB, 32, 64]), ALU.mult)
                    k.I("act", "activation", out=hbf[:, :], in_=H[:, j, :], func=AF.Copy)
                    for g in range(4):
                        BT = TA(xbcT[:, 16 + g, cs], 16 + g)
                        CT = TA(xbcT[:, 20 + g, cs], 20 + g)
                        pcb = self.psA[0]
                        k.I("pe", "matmul", pcb[:TB, :TB], BT, CT, start=True, stop=True)
                        cbs = k.nxt("cbs")
                        k.I("act", "activation", out=cbs[:TB, :TB], in_=pcb[:TB, :TB], func=AF.Copy)
                        pyo = self.psA[1]
                        k.I("pe", "matmul", pyo[:TB, :], CT, hbf[:, g * 512:(g + 1) * 512], start=True, stop=True)
                        pyd = self.psA[2]
                        for r in range(8):
                            hh = 8 * g + r
                            pbc = self.psB[r % 2]
                            acm = k.nxt("acm")
                            k.I("dve", "tensor_scalar", acm[:, 0:TB], acT[:, 0:TB], self.ident_f[0:32, hh:hh + 1], None, ALU.mult)
                            k.I("pe", "matmul", pbc[:TB, :TB], self.onesf[0:32, 0:TB], acm[:, 0:TB], start=True, stop=True)
                            tm = k.nxt("sq")
                            k.I("dve", "scalar_tensor_tensor", tm[:TB, :TB], pbc[:TB, :TB], acum[:TB, hh:hh + 1], self.maskneg[:TB, :TB],
                                ALU.subtract, ALU.add)
                            k.I("act", "activation", out=tm[:TB, :TB], in_=tm[:TB, :TB], func=AF.Exp)
                            MT = k.nxt("sqb")
                            k.I("dve", "tensor_tensor", MT[:TB, :TB], tm[:TB, :TB], cbs[:TB, :TB], ALU.mult)
                            k.I("pe", "matmul", pyd[:TB, r * 64:(r + 1) * 64], MT[:TB, :TB], xdt[:TB, hh * 64:(hh + 1) * 64], start=True, stop=True)
                        tf = k.nxt("tmpf")
                        k.I("dve", "tensor_tensor", v3(tf[:TB, :]), v3(pyo[:TB, :]),
                            expa[:TB, 8 * g:8 * g + 8].unsqueeze(2).to_broadcast([TB, 8, 64]), ALU.mult)
                        k.I("dve", "tensor_tensor", ysb[:TB, g * 512:(g + 1) * 512], tf[:TB, :], pyd[:TB, :], ALU.add)
                        pst = self.psA[3]
                        k.I("pe", "matmul", pst[:, :], B_tm[:TB, g * 128:(g + 1) * 128], xdtw[:TB, g * 512:(g + 1) * 512], start=True, stop=True)
                        Hg = H[:, j, g * 512:(g + 1) * 512]
                        k.I("dve", "tensor_tensor", v3(Hg), v3(Hg), cd[:, 8 * g:8 * g + 8].unsqueeze(2).to_broadcast([128, 8, 64]), ALU.mult)
                        k.I("dve", "tensor_tensor", Hg, Hg, pst[:, :], ALU.add)
                    tD = k.nxt("xnb")
                    k.I("dve", "tensor_tensor", v3(tD[:TB, :]), v3(xs_tm[:TB, :]),
                        evs[:TB, j, 72:104].unsqueeze(2).to_broadcast([TB, 32, 64]), ALU.mult)
                    k.I("dve", "tensor_tensor", ysb[:TB, :], ysb[:TB, :], tD[:TB, :], ALU.add)
                    k.I("dve", "tensor_tensor", ysb[:TB, :], ysb[:TB, :], TA(zs[:TB, tb, :], tb), ALU.mult)
                    ynb = k.nxt("xnb")
                    self.norm_rows(ysb[:TB, :], gss, TB, ynb)
                    self.transpose16(ynb, TB, ynT, tb)

            if self.cfg.get("dbg"):
                k.dma("sp", self.dbg_of[t0 // T], ofT[:, :, :])
                k.dma("sp", self.dbg_yn[t0 // T], ynT[:, :, :])

            def oT(kc, a, b):
                return ofT[:, kc, a:b] if kc < 8 else ynT[:, kc - 8, a:b]
            self.linear_tm(oT, 24, self.w_out_even[j], 0, D, T, self.resid_add(TB))

    def attention_sample(self, j, T, qT, cTm, ofT, seq):
        k = self.k
        NP = seq["n_pages"]
        nrows = seq["cfk"][0].shape[0]
        k.rotating("kpg", 2, [128, 1024], F32)
        k.rotating("vpg", 2, [128, 1024], F32)
        k.rotating("kpb", 2, [128, 1024], BF16)
        k.rotating("vpb", 2, [128, 1024], BF16)
        k.rotating("ktp", 2, [128, 8, 128], BF16)
        k.rotating("e64", 3, [128, 64], F32)
        k.rotating("p64", 3, [128, 64], BF16)
        lf = k.sb("lf_all", [128, NP, 8], F32)
        tail = k.sb("tail", [128, NP, 8], F32)
        pref = k.sb("pref", [128, NP, 8], F32)
        idx = self.pidx
        for pg in range(NP):
            k.dma("pool", lf[:, pg, :], seq["cfl"][j], meth="indirect_dma_start", out_offset=None,
                  in_offset=bass.IndirectOffsetOnAxis(ap=idx[:, pg:pg + 1], axis=0),
                  xr=[(idx[:].tensor.name, None)])
        lf2 = lf[:, :, :].rearrange("p a h -> p (a h)")
        tl2 = tail[:, :, :].rearrange("p a h -> p (a h)")
        pf2 = pref[:, :, :].rearrange("p a h -> p (a h)")
        ncol = NP * 8
        for c0 in range(0, ncol, 512):
            cw = min(512, ncol - c0)
            p1 = self.psA[0]
            p2 = self.psA[1]
            k.I("pe", "matmul", p1[:, :cw], self.cst[:, 256:384], lf2[:, c0:c0 + cw], start=True, stop=True)
            k.I("pe", "matmul", p2[:, :cw], self.onesf[:, :], lf2[:, c0:c0 + cw], start=True, stop=True)
            k.I("act", "activation", out=tl2[:, c0:c0 + cw], in_=p1[:, :cw], func=AF.Copy)
            k.I("act", "activation", out=pf2[:, c0:c0 + cw], in_=p2[:, :cw], func=AF.Copy)
        for h in range(8):
            k.I("dve", "tensor_tensor_scan", lf[:, :, h], self.onesf[:, 0:NP], pref[:, :, h], 0.0, ALU.mult, ALU.add)
        for h in range(8):
            k.I("dve", "tensor_scalar", pref[:, :, h], lf[:, :, h], -1.0, lf[:, NP - 1, h:h + 1], ALU.mult, ALU.add)
        k.I("dve", "tensor_tensor", tl2, tl2, pf2, ALU.add)
        pso = self.psA[2]
        psd = self.psA[3]
        for pg in range(NP):
            kpg, vpg = k.nxt("kpg"), k.nxt("vpg")
            for (dst, src) in ((kpg, seq["cfk"]), (vpg, seq["cfv"])):
                k.dma("pool", dst[:, :], src[j], meth="indirect_dma_start", out_offset=None,
                      in_offset=bass.IndirectOffsetOnAxis(ap=idx[:, pg:pg + 1], axis=0),
                      xr=[(idx[:].tensor.name, None)])
            kpb, vpb = k.nxt("kpb"), k.nxt("vpb")
            k.I("act", "activation", out=kpb[:, :], in_=kpg[:, :], func=AF.Copy)
            k.I("dve", "tensor_copy", vpb[:, :], vpg[:, :])
            pt = self.bankT()
            for h in range(8):
                k.I("pe", "transpose", pt[:, h * 128:(h + 1) * 128], kpb[:, h * 128:(h + 1) * 128], self.ident_b)
            ktp = k.nxt("ktp")
            self.copy(ktp[:, :, :], pt[:, :].rearrange("p (h s) -> p h s", s=128))
            pss = self.psA[pg % 2]
            for h in range(8):
                k.I("pe", "matmul", pss[:, h * T:(h + 1) * T], ktp[:, h, :], qT[:, h, 0:T], start=True, stop=False)
                k.I("pe", "matmul", pss[:, h * T:(h + 1) * T], self.onesf[0:8, :], cTm[0:8, h, 0:T], start=False, stop=True)
            e = k.nxt("e64")
            k.I("dve", "tensor_tensor", e[:, :].rearrange("p (h t) -> p h t", t=T), pss[:, 0:8 * T].rearrange("p (h t) -> p h t", t=T),
                tail[:, pg, :].unsqueeze(2).to_broadcast([128, 8, T]), ALU.add)
            pT = k.nxt("p64")
            k.I("act", "activation", out=pT[:, :], in_=e[:, :], func=AF.Exp)
            for h in range(8):
                k.I("pe", "matmul", pso[:, h * T:(h + 1) * T], vpb[:, h * 128:(h + 1) * 128], pT[:, h * T:(h + 1) * T], start=(pg == 0), stop=False)
                k.I("pe", "matmul", psd[:, h * T:(h + 1) * T], self.ones_b, pT[:, h * T:(h + 1) * T], start=(pg == 0), stop=False)
        kt_scr, v_scr = seq["kt_scr"], seq["v_scr"]
        ktn = k.nxt("ktp")
        vn = k.nxt("vpb")
        k.dma("sp", ktn[:, :, 0:T], kt_scr[j, :, :, 0:T])
        k.dma("sp", vn[:T, :], v_scr[j, 0:T, :])
        pss = self.psA[0]
        for h in range(8):
            k.I("pe", "matmul", pss[:T, h * T:(h + 1) * T], ktn[:, h, 0:T], qT[:, h, 0:T], start=True, stop=False)
            k.I("pe", "matmul", pss[:T, h * T:(h + 1) * T], self.onesf[0:8, 0:T], cTm[0:8, h, 0:T], start=False, stop=True)
        e = k.nxt("e64")
        k.I("dve", "tensor_tensor", e[:T, :].rearrange("p (h t) -> p h t", t=T), pss[:T, 0:8 * T].rearrange("p (h t) -> p h t", t=T),
            self.negc[:T, j, 0, :].unsqueeze(2).to_broadcast([T, 8, T]), ALU.add)
        pT = k.nxt("p64")
        k.I("act", "activation", out=pT[:T, :], in_=e[:T, :], func=AF.Exp)
        k.I("dve", "tensor_tensor", pT[:T, :].rearrange("p (h t) -> p h t", t=T), pT[:T, :].rearrange("p (h t) -> p h t", t=T),
            self.maskU_b[:T, 0:T].unsqueeze(1).to_broadcast([T, 8, T]), ALU.mult)
        for h in range(8):
            k.I("pe", "matmul", pso[:, h * T:(h + 1) * T], vn[:T, h * 128:(h + 1) * 128], pT[:T, h * T:(h + 1) * T], start=(NP == 0), stop=True)
            k.I("pe", "matmul", psd[:, h * T:(h + 1) * T], self.ones_b[:T, :], pT[:T, h * T:(h + 1) * T], start=(NP == 0), stop=True)
        rd = k.nxt("e64")
        k.I("dve", "reciprocal", rd[:, :], psd[:, 0:8 * T])
        k.I("dve", "tensor_tensor", ofT[:, :, :].rearrange("p h t -> p (h t)"), pso[:, 0:8 * T], rd[:, :], ALU.mult)

    def even_store_state(self, j, ssd_out, conv_out):
        k = self.k
        for m in range(16):
            ps = self.bankB()
            k.I("pe", "transpose", ps[:, 0:128], self.ssdH[:, j, m * 128:(m + 1) * 128], self.ident_f)
            o = k.nxt("stage")
            self.copy(o[:, 0:128], ps[:, 0:128])
            k.dma("sp", ssd_out[m * 128:(m + 1) * 128, :], o[:, 0:128])
        for m in range(24):
            k.dma("sp", conv_out[:, m * 128:(m + 1) * 128].rearrange("k p -> p k"), self.convst[:, j, m, :],
                  allow_slow_non_contiguous=True)

    def even_load_state(self, j, ssd_in, conv_in):
        k = self.k
        for m in range(16):
            stg = k.nxt("stage")
            k.dma("sp", stg[:, 0:128], ssd_in[m * 128:(m + 1) * 128, :])
            ps = self.bankB()
            k.I("pe", "transpose", ps[:, 0:128], stg[:, 0:128], self.ident_f)
            self.copy(self.ssdH[:, j, m * 128:(m + 1) * 128], ps[:, 0:128])
        for m in range(24):
            k.dma("sp", self.convst[:, j, m, :], conv_in[:, m * 128:(m + 1) * 128].rearrange("k p -> p k"),
                  allow_slow_non_contiguous=True)
        k.I("dve", "memset", self.foxtot[:, j, :], 0.0)

    def s5_load_state(self, j, re_in, im_in):
        k = self.k
        stg = k.nxt("stage")
        k.dma("sp", stg[0:64, 0:128], re_in)
        k.dma("sp", stg[0:64, 128:256], im_in)
        ps = self.bankB()
        k.I("pe", "transpose", ps[:, 0:64], stg[0:64, 0:128], self.ident_f[0:64, 0:64])
        k.I("pe", "transpose", ps[:, 64:128], stg[0:64, 128:256], self.ident_f[0:64, 0:64])
        k.I("act", "activation", out=self.s5st[:, j, 0, :], in_=ps[:, 0:64], func=AF.Copy)
        k.I("dve", "tensor_copy", self.s5vi[:, j, 0, :], ps[:, 0:64])
        k.I("dve", "tensor_copy", self.s5vi[:, j, 1, :], ps[:, 64:128])
        k.I("act", "activation", out=self.s5st[:, j, 1, :], in_=ps[:, 64:128], func=AF.Copy, scale=-1.0)

    def run_seq(self, seq):
        k = self.k
        T = seq["T"]
        TB = min(T, 128)
        nb = T // TB
        ntile = seq["SEQ"] // T
        for ti in range(ntile):
            for tb in range(nb):
                k.dma("sp", TA(self.x[:TB, tb, :], tb), seq["x_in"][ti * T + tb * TB: ti * T + (tb + 1) * TB, :])
            for i in range(self.depth):
                if "mixer" in self.parts:
                    if i % 2 == 1:
                        self.s5_mixer(i, T)
                    else:
                        self.even_mixer(i, T, ti * T, seq)
                if "cross" in self.parts:
                    self.cross(i, T, seq["mem_k"][i], seq["mem_v"][i])
                if "ffn" in self.parts:
                    self.ffn(i, T)
            for tb in range(nb):
                k.dma("sp", seq["y_out"][ti * T + tb * TB: ti * T + (tb + 1) * TB, :], TA(self.x[:TB, tb, :], tb))
        if "mixer" in self.parts:
            for j in range(self.nod):
                self.s5_store_state(j, seq["s5_re"][j], seq["s5_im"][j])
            for j in range(self.nev):
                self.even_store_state(j, seq["ssd"][j], seq["conv"][j])

    def build(self):
        k = self.k
        if "memkv" in self.parts:
            self.mem_kv_prompt()
        if "mixer" in self.parts:
            for j in range(self.nod):
                self.s5_setup(j)
                self.s5_init_state(j)
            for j in range(self.nev):
                self.even_init_state(j)
        pseq = dict(T=self.T, SEQ=self.SEQ, x_in=self.xp, y_out=self.y_p, mem_k=self.mem_k_p, mem_v=self.mem_v_p,
                    kt_scr=self.kt_scr, v_scr=self.v_scr, fox_k=self.fox_k_p, fox_v=self.fox_v_p, fox_lf=self.fox_logf_p,
                    ssd=self.ssd_p, conv=self.conv_p, past=False)
        if self.nod:
            pseq.update(s5_re=self.s5_re_p, s5_im=self.s5_im_p)
        self.run_seq(pseq)
        if self.do_sample:
            k.barrier()
            NP = self.NP
            pti = k.sb("pti", [128, NP], I32)
            ptf = k.sb("ptf", [128, NP], F32)
            self.pidx = k.sb("pidx", [128, NP], I32)
            k.dma("sp", pti[:], self.ptab[0:1, :].partition_broadcast(128))
            k.I("dve", "tensor_scalar", ptf[:], pti[:], 128.0, None, ALU.mult)
            k.I("dve", "tensor_tensor", self.pidx[:], ptf[:], self.cst[:, 384:384 + NP], ALU.add)
            if "mixer" in self.parts:
                for j in range(self.nod):
                    self.s5_load_state(j, self.s5re_in[j], self.s5im_in[j])
                for j in range(self.nev):
                    self.even_load_state(j, self.sssd[j], self.sconv[j])
            sseq = dict(T=8, SEQ=8, x_in=self.xs_in, y_out=self.y_s, mem_k=self.cmk, mem_v=self.cmv,
                        kt_scr=self.kt_scr_s, v_scr=self.v_scr_s, fox_k=self.fox_k_s, fox_v=self.fox_v_s, fox_lf=self.fox_logf_s,
                        ssd=self.ssd_s, conv=self.conv_s, past=True, n_pages=NP, cfk=self.cfk, cfv=self.cfv, cfl=self.cfl)
            if self.nod:
                sseq.update(s5_re=self.s5_re_s, s5_im=self.s5_im_s)
            self.run_seq(sseq)
        k.finish()
        return k.nc


def make_consts():
    c = np.zeros((128, 1024), np.float32)
    c[:, 0:128] = np.eye(128, dtype=np.float32)
    s = np.arange(128)[:, None]
    t = np.arange(128)[None, :]
    c[:, 128:256] = (s <= t)
    c[:, 256:384] = (s > t)
    c[:, 384:512] = np.arange(128, dtype=np.float32)[:, None]
    c[0, 512:1024] = np.arange(1, 513)
    es = np.zeros((32, 32, 128), np.float32)
    for h in range(32):
        es[h, h, :] = 1.0
    return c, es.reshape(32, 32 * 128)


def even_layout(ev):
    nev = ev['w_in_even'].shape[0]
    small = np.zeros((nev, 136), np.float32)
    small[:, 0:8] = ev['fox_b_forget']
    small[:, 8:40] = ev['ssd_dt_bias']
    small[:, 40:72] = ev['ssd_A_log']
    small[:, 72:104] = ev['ssd_D']
    cw = np.zeros((nev, 128, 24, 5), np.float32)
    for j in range(nev):
        cw[j, :, :, 0:4] = ev['ssd_conv_w'][j].T.reshape(24, 128, 4).transpose(1, 0, 2)
        cw[j, :, :, 4] = ev['ssd_conv_b'][j].reshape(24, 128).T
    return dict(w_in_even=np.ascontiguousarray(ev['w_in_even']), w_out_even=np.ascontiguousarray(ev['w_out_even']),
                fox_q_norm=np.ascontiguousarray(ev['fox_q_norm']), fox_k_norm=np.ascontiguousarray(ev['fox_k_norm']),
                ev_small=small, ssd_norm=np.ascontiguousarray(ev['ssd_norm']), conv_wT=cw.reshape(nev, 128, 120))


def s5_layout(A_re, A_im, log_dt, B_re, B_im, C_re, C_im, Dv):
    nod = A_re.shape[0]
    par = np.zeros((nod, 128, 3 * 64), np.float32)
    Bp = np.zeros((nod, 2, 64, 128, 128), np.float32)
    Cp = np.zeros((nod, 2, 64, 128, 128), np.float32)
    Dl = np.zeros((nod, 128, 16), np.float32)
    for j in range(nod):
        a_re = A_re[j].reshape(64, 2, 64).transpose(1, 2, 0).reshape(128, 64)
        a_im = A_im[j].reshape(64, 2, 64).transpose(1, 2, 0).reshape(128, 64)
        ld = np.repeat(log_dt[j].reshape(64, 2, 1), 64, axis=2).transpose(1, 2, 0).reshape(128, 64)
        par[j, :, 0:64] = a_re
        par[j, :, 64:128] = a_im
        par[j, :, 128:192] = ld
        for i in range(64):
            for gp in range(2):
                g = 2 * i + gp
                c0 = (i % 4) * 32 + gp * 16
                Bp[j, 0, i, gp * 64:(gp + 1) * 64, c0:c0 + 16] = B_re[j, g]
                Bp[j, 1, i, gp * 64:(gp + 1) * 64, c0:c0 + 16] = B_im[j, g]
                Cp[j, 0, i, gp * 64:(gp + 1) * 64, c0:c0 + 16] = C_re[j, g].T
                Cp[j, 1, i, gp * 64:(gp + 1) * 64, c0:c0 + 16] = C_im[j, g].T
        Dl[j] = Dv[j].reshape(16, 128).T
    return par, Bp, Cp, Dl


_CACHE = {}


def _f32(a):
    return np.ascontiguousarray(np.asarray(a), dtype=np.float32)


def make_in_maps(inp, ncores, depth):
    nev = (depth + 1) // 2
    nod = depth // 2
    c, _ = make_consts()
    sh = dict(consts=c)
    for kk in ("norm_mix", "norm_cross", "norm_mem", "norm_ffn", "w_mq", "w_mkv", "mem_q_norm", "mem_k_norm", "w_mo",
               "w_ffn_up", "w_ffn_down"):
        sh[kk] = _f32(inp[kk])
    sh.update(even_layout({kk: np.asarray(inp[kk]) for kk in ("w_in_even", "w_out_even", "fox_b_forget", "fox_q_norm", "fox_k_norm",
                                                              "ssd_conv_w", "ssd_conv_b", "ssd_dt_bias", "ssd_A_log", "ssd_D", "ssd_norm")}))
    if nod:
        par, Bp, Cp, Dl = s5_layout(np.asarray(inp["s5_A_re"]), np.asarray(inp["s5_A_im"]), np.asarray(inp["s5_log_dt"]),
                                    np.asarray(inp["s5_B_re"]), np.asarray(inp["s5_B_im"]), np.asarray(inp["s5_C_re"]),
                                    np.asarray(inp["s5_C_im"]), np.asarray(inp["s5_D"]))
        sh.update(w_in_odd=_f32(inp["w_in_odd"]), s5_w_glu=_f32(inp["s5_w_glu"]), s5_par=par, s5_Bp=Bp, s5_Cp=Cp, s5_D=Dl)
    cfk = _f32(inp["cache_fox_k"]).reshape(nev, -1, 1024)
    cfv = _f32(inp["cache_fox_v"]).reshape(nev, -1, 1024)
    cfl = _f32(inp["cache_fox_logf"]).reshape(nev, -1, 8)
    for jj in range(nev):
        sh["cfk%d" % jj] = cfk[jj]
        sh["cfv%d" % jj] = cfv[jj]
        sh["cfl%d" % jj] = cfl[jj]
    B = np.asarray(inp["x_prompt"]).shape[0]
    DB = np.asarray(inp["x_sample"]).shape[0]
    maps = []
    for core in range(ncores):
        b = core % B
        sb = core % DB
        m = dict(sh)
        m["xp"] = _f32(inp["x_prompt"][b])
        m["mem"] = _f32(inp["mem_prompt"][b])
        m["xs"] = _f32(inp["x_sample"][sb])
        m["cmk"] = _f32(np.asarray(inp["cache_mem_k"])[:, sb]).reshape(depth, 256, 512)
        m["cmv"] = _f32(np.asarray(inp["cache_mem_v"])[:, sb]).reshape(depth, 256, 512)
        m["sssd"] = _f32(np.asarray(inp["state_ssd"])[:, sb]).reshape(nev, 2048, 128)
        m["sconv"] = _f32(np.asarray(inp["state_conv"])[:, sb])
        if nod:
            m["s5re_in"] = _f32(np.asarray(inp["state_s5_re"])[:, sb]).reshape(nod, 64, 128)
            m["s5im_in"] = _f32(np.asarray(inp["state_s5_im"])[:, sb]).reshape(nod, 64, 128)
        m["ptab"] = np.ascontiguousarray(np.asarray(inp["page_table"])[sb:sb + 1].astype(np.int32))
        maps.append(m)
    return maps


def assemble(results, ncores, depth, SEQ, B=None, DB=None):
    nev = (depth + 1) // 2
    nod = depth // 2
    B = B or min(ncores, 4)
    DB = DB or ncores
    pr = [results[b] for b in range(B)]
    sr = [results[c] for c in range(DB)]

    def st(rs, key, shape_tail, lead):
        a = np.stack([np.asarray(r[key]) for r in rs], axis=1)
        return np.ascontiguousarray(a.reshape((lead, len(rs)) + shape_tail)).astype(np.float32)
    outs = [
        np.stack([np.asarray(r["y_p"]) for r in pr]).astype(np.float32),
        np.stack([np.asarray(r["y_s"]) for r in sr]).astype(np.float32),
        st(pr, "fox_k_p", (SEQ, 8, 128), nev), st(pr, "fox_v_p", (SEQ, 8, 128), nev), st(pr, "fox_logf_p", (SEQ, 8), nev),
        st(pr, "mem_k_p", (256, 4, 128), depth), st(pr, "mem_v_p", (256, 4, 128), depth),
        st(pr, "ssd_p", (32, 64, 128), nev), st(pr, "conv_p", (3, 3072), nev),
        st(pr, "s5_re_p", (128, 64), nod), st(pr, "s5_im_p", (128, 64), nod),
        st(sr, "fox_k_s", (8, 8, 128), nev), st(sr, "fox_v_s", (8, 8, 128), nev), st(sr, "fox_logf_s", (8, 8), nev),
        st(sr, "ssd_s", (32, 64, 128), nev), st(sr, "conv_s", (3, 3072), nev),
        st(sr, "s5_re_s", (128, 64), nod), st(sr, "s5_im_s", (128, 64), nod),
    ]
    return tuple(outs)


def kernel(**inp):
    depth = 4
    SEQ = int(np.asarray(inp["x_prompt"]).shape[1])
    n_pool = int(np.asarray(inp["cache_fox_k"]).shape[1])
    n_pages = int(np.asarray(inp["page_table"]).shape[1])
    cfg = dict(T=256, SEQ=SEQ, depth=depth, sample=True, n_pages=n_pages, n_pool=n_pool)
    key = (SEQ, n_pool, n_pages)
    if _CACHE.get("key") != key:
        _CACHE["nc"] = Prog(cfg).build()
        _CACHE["key"] = key
    nc = _CACHE["nc"]
    ncores = 8
    in_maps = make_in_maps(inp, ncores, depth)
    res = run_bass_kernel_spmd(nc, in_maps, core_ids=list(range(ncores)))
    return assemble(res.results, ncores, depth, SEQ, B=int(np.asarray(inp["x_prompt"]).shape[0]),
                    DB=int(np.asarray(inp["x_sample"]).shape[0]))
```

```python
import contextlib
import math
import numpy as np
import concourse.bass as bass
import concourse.mybir as mybir
from concourse.bass_utils import run_bass_kernel_spmd

F32 = mybir.dt.float32
BF16 = mybir.dt.bfloat16
I32 = mybir.dt.int32
AF = mybir.ActivationFunctionType
ALU = mybir.AluOpType
AX = mybir.AxisListType

D = 2048
KC = 16
EPS = 1e-6
SEM_LIMIT = 30000
EMBED_WAIT = True
NSLOT = 6
FFN_H = 5632
EVEN_IN = 8232
TWO_PI = 2.0 * math.pi
MAGIC = 12582912.0


class TA:
    def __init__(self, ap, key):
        self.ap = ap
        self.key = key


def _isap(a):
    return hasattr(a, "tensor") and hasattr(a, "ap") and hasattr(a, "offset")


class KB:
    def __init__(self):
        self.nc = bass.Bass("TRN2", target_bir_lowering=False)
        self.es = contextlib.ExitStack()
        self.ses = contextlib.ExitStack()
        nc = self.nc
        self.engs = {"pe": nc.tensor, "act": nc.scalar, "dve": nc.vector, "pool": nc.gpsimd, "sp": nc.sync}
        self.semh = {}
        self.nsem = 0
        self.sem = {}
        self.cnt = {}
        self.seen = {e: {} for e in self.engs}
        self.retired = set()
        for e in self.engs:
            self.sem[e] = self._sem()
            self.cnt[e] = 0
        self.dq = {q: {"slots": [[self._sem(), 0] for _ in range(NSLOT)], "i": 0} for q in ("sp", "pool", "act")}
        self.bufs = {}
        self.rot = {}
        self.ninst = 0

    def _sem(self):
        h = self.ses.enter_context(self.nc.semaphore("s%d" % self.nsem))
        uid = self.nsem
        self.nsem += 1
        self.semh[uid] = h
        return uid

    def sb(self, name, shape, dt):
        self.nalloc = getattr(self, "nalloc", 0) + 1
        return self.es.enter_context(self.nc.sbuf_tensor("%s_%d" % (name, self.nalloc), list(shape), dt))

    def psum(self, name, shape, dt):
        return self.es.enter_context(self.nc.psum_tensor(name, list(shape), dt))

    def dram(self, name, shape, dt, kind):
        return self.nc.dram_tensor(name, list(shape), dt, kind=kind).ap()

    def rotating(self, name, n, shape, dt):
        self.rot[name] = [[self.sb("%s_%d" % (name, i), shape, dt) for i in range(n)], 0]

    def nxt(self, name):
        r = self.rot[name]
        t = r[0][r[1] % len(r[0])]
        r[1] += 1
        return t

    def _retire(self, uid, final):
        for e2 in self.engs:
            if self.seen[e2].get(uid, 0) < final:
                self.engs[e2].wait_ge(self.semh[uid], final)
        for e2 in self.engs:
            self.seen[e2].pop(uid, None)
        self.retired.add(uid)

    def _emit_waits(self, e, toks, embed=False):
        need = {}
        for uid, v in toks:
            if uid in self.retired:
                continue
            if need.get(uid, 0) < v:
                need[uid] = v
        todo = []
        for uid, v in need.items():
            if self.seen[e].get(uid, 0) >= v:
                continue
            if e == "pe" and uid == self.sem["pe"]:
                continue
            todo.append((uid, v))
        last = None
        if embed and EMBED_WAIT and todo:
            last = todo.pop()
        for uid, v in todo:
            self.engs[e].wait_ge(self.semh[uid], v)
            self.seen[e][uid] = v
        if last is not None:
            self.seen[e][last[0]] = last[1]
        return last

    def _collect(self, args, kw):
        reads, writes, nargs, nkw = [], [], [], {}

        def key_of(a):
            if isinstance(a, TA):
                return (a.ap.tensor.name, a.key), a.ap
            return (a.tensor.name, None), a

        for i, a in enumerate(args):
            if isinstance(a, TA) or _isap(a):
                k, ap = key_of(a)
                (writes if i == 0 else reads).append(k)
                nargs.append(ap)
            else:
                nargs.append(a)
        for kk, a in kw.items():
            if isinstance(a, TA) or _isap(a):
                k, ap = key_of(a)
                (writes if kk in ("out", "accum_out") else reads).append(k)
                nkw[kk] = ap
            else:
                nkw[kk] = a
        return reads, writes, nargs, nkw

    def _deps(self, reads, writes):
        toks = []
        for k in reads:
            b = self.bufs.get(k)
            if b:
                toks += b["w"]
        for k in writes:
            b = self.bufs.get(k)
            if b:
                toks += b["w"]
                toks += list(b["r"].items())
        return toks

    def _record(self, reads, writes, tok):
        for k in reads:
            b = self.bufs.setdefault(k, {"w": [], "r": {}})
            if b["r"].get(tok[0], 0) < tok[1]:
                b["r"][tok[0]] = tok[1]
        for k in writes:
            self.bufs[k] = {"w": [tok], "r": {}}

    def I(self, e, meth, *args, **kw):
        xr = kw.pop("_R", ())
        xw = kw.pop("_W", ())
        reads, writes, nargs, nkw = self._collect(args, kw)
        reads += list(xr)
        writes += list(xw)
        if self.cnt[e] >= SEM_LIMIT:
            self._retire(self.sem[e], self.cnt[e])
            self.sem[e] = self._sem()
            self.cnt[e] = 0
        last = self._emit_waits(e, self._deps(reads, writes), embed=True)
        ins = getattr(self.engs[e], meth)(*nargs, **nkw)
        if last is not None:
            ins._wait_ge(self.semh[last[0]], last[1])
        self.cnt[e] += 1
        ins.then_inc(self.semh[self.sem[e]], 1)
        self._record(reads, writes, (self.sem[e], self.cnt[e]))
        self.ninst += 1
        return ins

    def dma(self, q, out, in_, meth="dma_start", xr=(), **kw):
        reads, writes, _, nkw = self._collect((), dict(out=out, in_=in_, **kw))
        reads += list(xr)
        dq = self.dq[q]
        slot = dq["slots"][dq["i"] % NSLOT]
        dq["i"] += 1
        if slot[1] + 16 > SEM_LIMIT:
            self._retire(slot[0], slot[1])
            slot[0] = self._sem()
            slot[1] = 0
        toks = self._deps(reads, writes)
        if slot[1] > 0:
            toks.append((slot[0], slot[1]))
        last = self._emit_waits(q, toks, embed=True)
        ins = getattr(self.engs[q], meth)(**nkw)
        if last is not None:
            ins._wait_ge(self.semh[last[0]], last[1])
        slot[1] += 16
        ins.then_inc(self.semh[slot[0]], 16)
        self._record(reads, writes, (slot[0], slot[1]))
        self.ninst += 1
        return ins

    def barrier(self):
        toks = []
        for e in self.engs:
            if self.cnt[e] > 0:
                toks.append((self.sem[e], self.cnt[e]))
        for q, dq in self.dq.items():
            for uid, tgt in dq["slots"]:
                if tgt > 0:
                    toks.append((uid, tgt))
        for e in self.engs:
            self._emit_waits(e, toks)
        self.bufs = {}

    @contextlib.contextmanager
    def scope(self):
        outer = self.es
        outer_rot = dict(self.rot)
        self.es = contextlib.ExitStack()
        try:
            yield
        finally:
            self.barrier()
            self.es.close()
            self.es = outer
            self.rot = outer_rot

    def finish(self):
        for q, dq in self.dq.items():
            for uid, tgt in dq["slots"]:
                if tgt > 0 and uid not in self.retired and self.seen["sp"].get(uid, 0) < tgt:
                    self.engs["sp"].wait_ge(self.semh[uid], tgt)
        for e in self.engs:
            if self.cnt[e] > 0 and e != "sp":
                self.engs["sp"].wait_ge(self.semh[self.sem[e]], self.cnt[e])


class Prog:
    def __init__(self, cfg):
        self.cfg = cfg
        self.k = KB()
        k = self.k
        self.T = cfg.get("T", 256)
        T = self.T
        self.SEQ = cfg.get("SEQ", 2048)
        self.depth = cfg.get("depth", 4)
        self.parts = cfg.get("parts", ("memkv", "mixer", "cross", "ffn"))
        self.do_sample = cfg.get("sample", True)
        nev = (self.depth + 1) // 2
        nod = self.depth // 2
        self.nev, self.nod = nev, nod
        dep = self.depth
        dr = k.dram
        SEQ = self.SEQ
        self.xp = dr("xp", [SEQ, D], F32, "ExternalInput")
        self.mem = dr("mem", [256, D], F32, "ExternalInput")
        self.consts = dr("consts", [128, 1024], F32, "ExternalInput")
        self.norm_mix = dr("norm_mix", [dep, D], F32, "ExternalInput")
        self.norm_cross = dr("norm_cross", [dep, D], F32, "ExternalInput")
        self.norm_mem = dr("norm_mem", [dep, D], F32, "ExternalInput")
        self.norm_ffn = dr("norm_ffn", [dep, D], F32, "ExternalInput")
        self.w_mq = dr("w_mq", [dep, D, 512], F32, "ExternalInput")
        self.w_mkv = dr("w_mkv", [dep, D, 1024], F32, "ExternalInput")
        self.mem_q_norm = dr("mem_q_norm", [dep, 128], F32, "ExternalInput")
        self.mem_k_norm = dr("mem_k_norm", [dep, 128], F32, "ExternalInput")
        self.w_mo = dr("w_mo", [dep, 512, D], F32, "ExternalInput")
        self.w_up = dr("w_ffn_up", [dep, D, 2 * FFN_H], F32, "ExternalInput")
        self.w_down = dr("w_ffn_down", [dep, FFN_H, D], F32, "ExternalInput")
        self.w_in_even = dr("w_in_even", [nev, D, EVEN_IN], F32, "ExternalInput")
        self.w_out_even = dr("w_out_even", [nev, 3072, D], F32, "ExternalInput")
        self.fox_qn = dr("fox_q_norm", [nev, 128], F32, "ExternalInput")
        self.fox_kn = dr("fox_k_norm", [nev, 128], F32, "ExternalInput")
        self.ev_small = dr("ev_small", [nev, 136], F32, "ExternalInput")
        self.ssd_norm = dr("ssd_norm", [nev, D], F32, "ExternalInput")
        self.conv_wT = dr("conv_wT", [nev, 128, 24 * 5], F32, "ExternalInput")
        if nod:
            self.w_in_odd = dr("w_in_odd", [nod, D, D], F32, "ExternalInput")
            self.s5_w_glu = dr("s5_w_glu", [nod, D, 2 * D], F32, "ExternalInput")
            self.s5_par = dr("s5_par", [nod, 128, 3 * 64], F32, "ExternalInput")
            self.s5_Bp = dr("s5_Bp", [nod, 2, 64, 128, 128], F32, "ExternalInput")
            self.s5_Cp = dr("s5_Cp", [nod, 2, 64, 128, 128], F32, "ExternalInput")
            self.s5_D = dr("s5_D", [nod, 128, 16], F32, "ExternalInput")
            self.s5_bbT = dr("s5_bbT", [nod, 64, 128, 2, 128], BF16, "Internal")
        self.kt_scr = dr("kt_scr", [nev, 128, 8, SEQ], BF16, "Internal")
        self.v_scr = dr("v_scr", [nev, SEQ, 1024], BF16, "Internal")
        self.y_p = dr("y_p", [SEQ, D], F32, "ExternalOutput")
        self.mem_k_p = dr("mem_k_p", [dep, 256, 512], F32, "ExternalOutput")
        self.mem_v_p = dr("mem_v_p", [dep, 256, 512], F32, "ExternalOutput")
        self.fox_k_p = dr("fox_k_p", [nev, SEQ, 1024], F32, "ExternalOutput")
        self.fox_v_p = dr("fox_v_p", [nev, SEQ, 1024], F32, "ExternalOutput")
        self.fox_logf_p = dr("fox_logf_p", [nev, SEQ, 8], F32, "ExternalOutput")
        self.ssd_p = dr("ssd_p", [nev, 2048, 128], F32, "ExternalOutput")
        self.conv_p = dr("conv_p", [nev, 3, 3072], F32, "ExternalOutput")
        if nod:
            self.s5_re_p = dr("s5_re_p", [nod, 64, 128], F32, "ExternalOutput")
            self.s5_im_p = dr("s5_im_p", [nod, 64, 128], F32, "ExternalOutput")
        if self.do_sample:
            NP = cfg.get("n_pages", 128)
            NROWS = cfg.get("n_pool", 1280) * 128
            self.NP = NP
            self.xs_in = dr("xs", [8, D], F32, "ExternalInput")
            self.cfk = [dr("cfk%d" % jj, [NROWS, 1024], F32, "ExternalInput") for jj in range(nev)]
            self.cfv = [dr("cfv%d" % jj, [NROWS, 1024], F32, "ExternalInput") for jj in range(nev)]
            self.cfl = [dr("cfl%d" % jj, [NROWS, 8], F32, "ExternalInput") for jj in range(nev)]
            self.cmk = dr("cmk", [dep, 256, 512], F32, "ExternalInput")
            self.cmv = dr("cmv", [dep, 256, 512], F32, "ExternalInput")
            self.sssd = dr("sssd", [nev, 2048, 128], F32, "ExternalInput")
            self.sconv = dr("sconv", [nev, 3, 3072], F32, "ExternalInput")
            self.ptab = dr("ptab", [1, NP], I32, "ExternalInput")
            self.y_s = dr("y_s", [8, D], F32, "ExternalOutput")
            self.fox_k_s = dr("fox_k_s", [nev, 8, 1024], F32, "ExternalOutput")
            self.fox_v_s = dr("fox_v_s", [nev, 8, 1024], F32, "ExternalOutput")
            self.fox_logf_s = dr("fox_logf_s", [nev, 8, 8], F32, "ExternalOutput")
            self.ssd_s = dr("ssd_s", [nev, 2048, 128], F32, "ExternalOutput")
            self.conv_s = dr("conv_s", [nev, 3, 3072], F32, "ExternalOutput")
            self.kt_scr_s = dr("kt_scr_s", [nev, 128, 8, 8], BF16, "Internal")
            self.v_scr_s = dr("v_scr_s", [nev, 8, 1024], BF16, "Internal")
            if nod:
                self.s5re_in = dr("s5re_in", [nod, 64, 128], F32, "ExternalInput")
                self.s5im_in = dr("s5im_in", [nod, 64, 128], F32, "ExternalInput")
                self.s5_re_s = dr("s5_re_s", [nod, 64, 128], F32, "ExternalOutput")
                self.s5_im_s = dr("s5_im_s", [nod, 64, 128], F32, "ExternalOutput")
        if cfg.get("dbg"):
            self.dbg_of = dr("dbg_of", [SEQ // T, 128, 8, T], BF16, "ExternalOutput")
            self.dbg_yn = dr("dbg_yn", [SEQ // T, 128, 16, T], BF16, "ExternalOutput")
        NBK = max(1, T // 128)
        self.NBK = NBK
        sb = k.sb
        self.cst = sb("cst", [128, 512], F32)
        self.cbf = sb("cbf", [128, 384], BF16)
        self.onesf = sb("onesf", [128, 128], F32)
        self.maskneg = sb("maskneg", [128, 128], F32)
        self.iota = sb("iota", [128, 512], F32)
        self.x = sb("x", [128, NBK, D], F32)
        self.xnT = sb("xnT", [128, KC, T], BF16)
        k.rotating("wt", 2, [128, 16, 512], BF16)
        k.rotating("gain", 1, [128, D], F32)
        k.rotating("g128", 2, [128, 128], F32)
        k.rotating("stat", 6, [128, 16], F32)
        k.rotating("xnb", 2, [128, D], BF16)
        k.rotating("tmpf", 3, [128, 512], F32)
        k.rotating("tmpb", 3, [128, 512], BF16)
        k.rotating("pT", 3, [128, 512], BF16)
        k.rotating("stage", 3, [128, 512], F32)
        self.mkT = sb("mkT", [128, 4, 256], BF16)
        self.mv = sb("mv", [128, 2, 512], BF16)
        self.cqT = sb("cqT", [128, 4, T], BF16)
        self.coT = sb("coT", [128, 4, T], BF16)
        if nod:
            self.s5r = sb("s5r", [128, nod, 64], F32)
            self.s5th = sb("s5th", [128, nod, 64], F32)
            self.s5st = sb("s5st", [128, nod, 2, 64], F32)
            self.s5vi = sb("s5vi", [128, nod, 2, 64], F32)
            self.s5Dt = sb("s5Dt", [128, nod, 16], F32)
        self.ssdH = sb("ssdH", [128, nev, 2048], F32)
        self.convst = sb("convst", [128, nev, 24, 3], F32)
        self.foxtot = sb("foxtot", [128, nev, 8], F32)
        self.negc = sb("negc", [128, nev, SEQ // 128 if SEQ >= 128 else 1, 8], F32)
        self.evs = sb("evs", [128, nev, 136], F32)
        self.evA = sb("evA", [128, nev, 32], F32)
        self.cwT = sb("cwT", [128, nev, 24, 5], F32)
        self.psA = [k.psum("psA%d" % i, [128, 512], F32) for i in range(4)]
        self.psB = [k.psum("psB%d" % i, [128, 512], F32) for i in range(2)]
        self.psT = [k.psum("psT%d" % i, [128, 1024], BF16) for i in range(2)]
        self.iA = self.iB = self.iT = 0
        self.cpy = 0
        k.dma("sp", self.cst[:], self.consts[:, 0:512])
        k.I("dve", "tensor_copy", self.cbf[:, 0:128], self.cst[:, 0:128])
        k.I("dve", "memset", self.cbf[:, 128:256], 1.0)
        k.I("dve", "tensor_copy", self.cbf[:, 256:384], self.cst[:, 128:256])
        k.I("dve", "memset", self.onesf[:], 1.0)
        k.I("dve", "tensor_scalar", self.maskneg[:], self.cst[:, 128:256], -1.0, 30000.0, ALU.add, ALU.mult)
        k.dma("sp", self.iota[:], self.consts[0:1, 512:1024].partition_broadcast(128))
        for j in range(nev):
            k.dma("sp", self.evs[:, j, :], self.ev_small[j:j + 1, :].partition_broadcast(128))
            k.dma("sp", self.cwT[:, j, :, :], self.conv_wT[j].rearrange("p (m c) -> p m c", c=5))
            k.I("act", "activation", out=self.evA[:, j, :], in_=self.evs[:, j, 40:72], func=AF.Exp)
            k.I("dve", "tensor_scalar", self.evA[:, j, :], self.evA[:, j, :], -1.0, None, ALU.mult)

    @property
    def ident_f(self):
        return self.cst[:, 0:128]

    @property
    def U_f(self):
        return self.cst[:, 128:256]

    @property
    def ident_b(self):
        return self.cbf[:, 0:128]

    @property
    def ones_b(self):
        return self.cbf[:, 128:256]

    @property
    def maskU_b(self):
        return self.cbf[:, 256:384]

    def bankA(self):
        self.iA += 1
        return self.psA[self.iA % 4]

    def bankB(self):
        self.iB += 1
        return self.psB[self.iB % 2]

    def bankT(self):
        self.iT += 1
        return self.psT[self.iT % 2]

    def copy(self, out, in_):
        self.cpy += 1
        if self.cpy % 2:
            return self.k.I("act", "activation", out=out, in_=in_, func=AF.Copy)
        return self.k.I("dve", "tensor_copy", out, in_)

    def load_w(self, W, k0, kn, c0, w):
        k = self.k
        wt = k.nxt("wt")
        if not hasattr(self, "wcache"):
            self.wcache = {}
        key = (W.tensor.name, int(W.offset), k0, kn, c0, w)
        scr = self.wcache.get(key)
        if scr is None:
            src = W[k0 * 128:(k0 + kn) * 128, c0:c0 + w].rearrange("(kc p) n -> p kc n", p=128)
            k.dma("pool", wt[:, 0:kn, 0:w], src)
            if self.cfg.get("wcache", True):
                scr = k.dram("wscr%d" % len(self.wcache), [128, kn * w // 2], F32, "Internal")
                self.wcache[key] = scr
                k.dma("sp", scr[:, :].rearrange("p (a n) -> p a n", n=w // 2), wt[:, 0:kn, 0:w].bitcast(F32))
        else:
            k.dma(self.cfg.get("wq", "sp"), wt[:, 0:kn, 0:w].bitcast(F32), scr[:, :].rearrange("p (a n) -> p a n", n=w // 2))
        return wt

    def linear_tm(self, xT, kcn, W, c0, ncols, T, consume):
        k = self.k
        TB = min(T, 128)
        nb = T // TB
        ncb = (ncols + 511) // 512
        for cb in range(ncb):
            w = min(512, ncols - cb * 512)
            kgs = [(k0, min(16, kcn - k0)) for k0 in range(0, kcn, 16)]
            if len(kgs) == 1:
                wt = self.load_w(W, 0, kcn, c0 + cb * 512, w)
                for tb in range(nb):
                    ps = self.bankA()
                    for kc in range(kcn):
                        k.I("pe", "matmul", ps[:TB, :w], xT(kc, tb * TB, (tb + 1) * TB), wt[:, kc, :w],
                            start=(kc == 0), stop=(kc == kcn - 1))
                    consume(tb, cb, w, ps[:TB, :w])
            else:
                assert nb <= 4
                pss = [self.bankA() for _ in range(nb)]
                for (k0, kn) in kgs:
                    wt = self.load_w(W, k0, kn, c0 + cb * 512, w)
                    for tb in range(nb):
                        for kc in range(kn):
                            k.I("pe", "matmul", pss[tb][:TB, :w], xT(k0 + kc, tb * TB, (tb + 1) * TB), wt[:, kc, :w],
                                start=(k0 + kc == 0), stop=(k0 + kc == kcn - 1))
                for tb in range(nb):
                    consume(tb, cb, w, pss[tb][:TB, :w])

    def linear_fm(self, xT, kcn, W, c0, ncols, T, consume):
        k = self.k
        assert kcn <= 16
        ncb = (ncols + 511) // 512
        for cb in range(ncb):
            w = min(512, ncols - cb * 512)
            wt = self.load_w(W, 0, kcn, c0 + cb * 512, w)
            for mi in range((w + 127) // 128):
                mw = min(128, w - mi * 128)
                ps = self.bankA()
                for kc in range(kcn):
                    k.I("pe", "matmul", ps[:mw, :T], wt[:, kc, mi * 128:mi * 128 + mw], xT(kc, 0, T),
                        start=(kc == 0), stop=(kc == kcn - 1))
                consume(cb * 4 + mi, mw, ps[:mw, :T])

    def load_gain(self, row):
        g = self.k.nxt("gain")
        self.k.dma("sp", g[:], row.partition_broadcast(128))
        return g

    def load_g128(self, row):
        g = self.k.nxt("g128")
        self.k.dma("sp", g[:], row.partition_broadcast(128))
        return g

    def rstd(self, st, n, TB, c_in, c_out, width, extra=None):
        k = self.k
        k.I("act", "activation", out=st[:TB, 12:12 + width], in_=st[:TB, c_in:c_in + width], func=AF.Sqrt,
            scale=1.0 / n, bias=EPS)
        k.I("dve", "reciprocal", st[:TB, c_out:c_out + width], st[:TB, 12:12 + width])
        if extra is not None:
            k.I("dve", "tensor_scalar", st[:TB, c_out:c_out + width], st[:TB, c_out:c_out + width], extra, None, ALU.mult)

    def norm_rows(self, xb, g, TB, xnb):
        k = self.k
        st = k.nxt("stat")
        k.I("dve", "memset", st[:TB, 0:1], 0.0)
        k.I("act", "activation", out=xnb[:TB, :], in_=xb, func=AF.Square, accum_out=st[:TB, 0:1])
        self.rstd(st, D, TB, 0, 1, 1)
        k.I("dve", "scalar_tensor_tensor", xnb[:TB, :], xb, st[:TB, 1:2], g[:TB, :], ALU.mult, ALU.mult)

    def transpose16(self, xnb, TB, dst, tb, nchunk=16):
        k = self.k
        for c0 in range(0, nchunk, 8):
            n8 = min(8, nchunk - c0)
            pt = self.bankT()
            for c8 in range(n8):
                kc = c0 + c8
                k.I("pe", "transpose", pt[:, c8 * TB:(c8 + 1) * TB], xnb[:TB, kc * 128:(kc + 1) * 128],
                    self.ident_b[:TB, :TB])
            self.copy(dst[:, c0:c0 + n8, tb * TB:(tb + 1) * TB], pt[:, 0:n8 * TB].rearrange("p (c t) -> p c t", t=TB))

    def rmsnorm_T(self, xblk, gain_row, T):
        k = self.k
        TB = min(T, 128)
        nb = T // TB
        g = self.load_gain(gain_row)
        for tb in range(nb):
            xnb = k.nxt("xnb")
            self.norm_rows(xblk(tb), g, TB, xnb)
            self.transpose16(xnb, TB, self.xnT, tb)

    def xblk(self, TB):
        return lambda tb: TA(self.x[:TB, tb, :], tb)

    def resid_add(self, TB):
        def f(tb, cb, w, ps):
            xa = TA(self.x[:TB, tb, cb * 512:cb * 512 + w], tb)
            self.k.I("dve", "tensor_tensor", xa, xa, ps, ALU.add)
        return f

    def headnorm(self, ps, TB, nh, gain128, out_fn, extra=None):
        k = self.k
        st = k.nxt("stat")
        tf = k.nxt("tmpf")
        k.I("act", "activation", out=tf[:TB, :nh * 128], in_=ps, func=AF.Square)
        k.I("dve", "tensor_reduce", st[:TB, 0:nh], tf[:TB, :nh * 128].rearrange("p (h d) -> p h d", d=128), AX.X, ALU.add)
        self.rstd(st, 128, TB, 0, 4, nh, extra)
        for h in range(nh):
            k.I("dve", "scalar_tensor_tensor", out_fn(h), ps[:, h * 128:(h + 1) * 128], st[:TB, 4 + h:5 + h],
                gain128[:TB, :], ALU.mult, ALU.mult)

    def ffn(self, i, T):
        k = self.k
        TB = min(T, 128)
        with k.scope():
            hT = k.sb("hT", [128, 44, T], BF16)
            gbuf = k.sb("gbuf", [128, self.NBK, 512], BF16)
            if self.cfg.get("ffn_wt3", True):
                extra = k.sb("wtx", [128, 16, 512], BF16)
                k.rot["wt"] = [list(k.rot["wt"][0]) + [extra], k.rot["wt"][1]]
            self.rmsnorm_T(self.xblk(TB), self.norm_ffn[i:i + 1, :], T)
            xT = lambda kc, a, b: self.xnT[:, kc, a:b]
            for j in range(11):
                def c_gate(tb, cb, w, ps, j=j):
                    k.I("act", "activation", out=TA(gbuf[:TB, tb, :], tb), in_=ps, func=AF.Silu)

                def c_up(tb, cb, w, ps, j=j):
                    hb = k.nxt("tmpb")
                    k.I("dve", "tensor_tensor", hb[:TB, :], TA(gbuf[:TB, tb, :], tb), ps, ALU.mult)
                    pt = self.bankT()
                    for c in range(4):
                        k.I("pe", "transpose", pt[:, c * TB:(c + 1) * TB], hb[:TB, c * 128:(c + 1) * 128], self.ident_b[:TB, :TB])
                    self.copy(TA(hT[:, j * 4:j * 4 + 4, tb * TB:(tb + 1) * TB], (j, tb)),
                              pt[:, 0:4 * TB].rearrange("p (c t) -> p c t", t=TB))
                self.linear_tm(xT, KC, self.w_up[i], j * 512, 512, T, c_gate)
                self.linear_tm(xT, KC, self.w_up[i], FFN_H + j * 512, 512, T, c_up)
            hTf = lambda kc, a, b: TA(hT[:, kc, a:b], (kc // 4, a // TB))
            self.linear_tm(hTf, 44, self.w_down[i], 0, D, T, self.resid_add(TB))

    def mem_kv_prompt(self):
        k = self.k
        for tb in range(2):
            k.dma("sp", TA(self.x[:, 0, :], 0), self.mem[tb * 128:(tb + 1) * 128, :])
            for i in range(self.depth):
                g = self.load_gain(self.norm_mem[i:i + 1, :])
                gk = self.load_g128(self.mem_k_norm[i:i + 1, :])
                xnb = k.nxt("xnb")
                self.norm_rows(TA(self.x[:, 0, :], 0), g, 128, xnb)
                self.transpose16(xnb, 128, self.xnT, 0)
                xT = lambda kc, a, b: self.xnT[:, kc, a:b]

                def cons(tb_, cb, w, ps, i=i, gk=gk, tb=tb):
                    stg = k.nxt("stage")
                    if cb == 0:
                        self.headnorm(ps, 128, 4, gk, lambda h: stg[:, h * 128:(h + 1) * 128])
                        k.dma("sp", self.mem_k_p[i, tb * 128:(tb + 1) * 128, :], stg[:, :])
                    else:
                        k.I("act", "activation", out=stg[:, :], in_=ps, func=AF.Copy)
                        k.dma("sp", self.mem_v_p[i, tb * 128:(tb + 1) * 128, :], stg[:, :])
                self.linear_tm(xT, KC, self.w_mkv[i], 0, 1024, 128, cons)

    def load_mem_kv(self, kd, vd):
        k = self.k
        for mb in range(2):
            kb = k.nxt("tmpb")
            k.dma("pool", kb[:, :], kd[mb * 128:(mb + 1) * 128, :])
            k.dma("pool", self.mv[:, mb, :], vd[mb * 128:(mb + 1) * 128, :])
            pt = self.bankT()
            for h in range(4):
                k.I("pe", "transpose", pt[:, h * 128:(h + 1) * 128], kb[:, h * 128:(h + 1) * 128], self.ident_b)
            self.copy(self.mkT[:, :, mb * 128:(mb + 1) * 128], pt[:, 0:512].rearrange("p (h t) -> p h t", t=128))

    def cross(self, i, T, kd, vd):
        k = self.k
        TB = min(T, 128)
        self.load_mem_kv(kd, vd)
        self.rmsnorm_T(self.xblk(TB), self.norm_cross[i:i + 1, :], T)
        gq = self.load_g128(self.mem_q_norm[i:i + 1, :])
        xT = lambda kc, a, b: self.xnT[:, kc, a:b]

        def qcons(tb, cb, w, ps):
            qb = k.nxt("tmpb")
            self.headnorm(ps, TB, 4, gq, lambda h: qb[:TB, h * 128:(h + 1) * 128])
            pt = self.bankT()
            for h in range(4):
                k.I("pe", "transpose", pt[:, h * TB:(h + 1) * TB], qb[:TB, h * 128:(h + 1) * 128], self.ident_b[:TB, :TB])
            self.copy(self.cqT[:, :, tb * TB:(tb + 1) * TB], pt[:, 0:4 * TB].rearrange("p (h t) -> p h t", t=TB))
        self.linear_tm(xT, KC, self.w_mq[i], 0, 512, T, qcons)
        scale = 128.0 ** -0.5
        for h in range(4):
            pso = self.psA[2 + h % 2]
            psd = self.psB[h % 2]
            for mb in range(2):
                pss = self.psA[mb]
                k.I("pe", "matmul", pss[:, :T], self.mkT[:, h, mb * 128:(mb + 1) * 128], self.cqT[:, h, :T],
                    start=True, stop=True)
                pT = k.nxt("pT")
                k.I("act", "activation", out=pT[:, :T], in_=pss[:, :T], func=AF.Exp, scale=scale)
                k.I("pe", "matmul", pso[:, :T], self.mv[:, mb, h * 128:(h + 1) * 128], pT[:, :T],
                    start=(mb == 0), stop=(mb == 1))
                k.I("pe", "matmul", psd[:, :T], self.ones_b, pT[:, :T], start=(mb == 0), stop=(mb == 1))
            rd = k.nxt("tmpf")
            k.I("dve", "reciprocal", rd[:, :T], psd[:, :T])
            k.I("dve", "tensor_tensor", self.coT[:, h, :T], pso[:, :T], rd[:, :T], ALU.mult)
        cT = lambda kc, a, b: self.coT[:, kc, a:b]
        self.linear_tm(cT, 4, self.w_mo[i], 0, D, T, self.resid_add(TB))

    def s5_setup(self, j):
        k = self.k
        with k.scope():
            W = k.sb("s5w", [128, 18, 64], F32)
            k.dma("sp", W[:, 0:3, :], self.s5_par[j].rearrange("p (a i) -> p a i", i=64))
            k.dma("sp", self.s5Dt[:, j, :], self.s5_D[j])
            A_re, A_im, ldt = W[:, 0, :], W[:, 1, :], W[:, 2, :]
            lam_re, dt, r, th = W[:, 3, :], W[:, 4, :], self.s5r[:, j, :], self.s5th[:, j, :]
            k.I("dve", "tensor_scalar", lam_re, A_re, -1e-4, None, ALU.min)
            k.I("act", "activation", out=dt, in_=ldt, func=AF.Exp)
            k.I("dve", "tensor_tensor", W[:, 5, :], lam_re, dt, ALU.mult)
            k.I("act", "activation", out=r, in_=W[:, 5, :], func=AF.Exp)
            k.I("dve", "tensor_tensor", th, A_im, dt, ALU.mult)
            k.I("dve", "tensor_scalar", th, th, 1.0 / TWO_PI, None, ALU.mult)
            k.I("dve", "tensor_scalar", W[:, 6, :], th, MAGIC, MAGIC, ALU.add, ALU.subtract)
            k.I("dve", "tensor_tensor", W[:, 6, :], th, W[:, 6, :], ALU.subtract)
            k.I("act", "activation", out=W[:, 7, :], in_=W[:, 6, :], func=AF.Abs)
            k.I("act", "activation", out=W[:, 8, :], in_=W[:, 6, :], func=AF.Sin, scale=TWO_PI)
            k.I("act", "activation", out=W[:, 9, :], in_=W[:, 7, :], func=AF.Sin, scale=-TWO_PI, bias=0.5 * math.pi)
            ni, nr = W[:, 10, :], W[:, 11, :]
            k.I("dve", "tensor_tensor", ni, W[:, 8, :], r, ALU.mult)
            k.I("dve", "tensor_tensor", nr, W[:, 9, :], r, ALU.mult)
            k.I("dve", "tensor_scalar", nr, nr, -1.0, None, ALU.add)
            den, t1, t2 = W[:, 12, :], W[:, 13, :], W[:, 14, :]
            k.I("dve", "tensor_tensor", den, lam_re, lam_re, ALU.mult)
            k.I("dve", "tensor_tensor", t1, A_im, A_im, ALU.mult)
            k.I("dve", "tensor_tensor", den, den, t1, ALU.add)
            k.I("dve", "reciprocal", den, den)
            cre, cim = W[:, 15, :], W[:, 16, :]
            k.I("dve", "tensor_tensor", t1, nr, lam_re, ALU.mult)
            k.I("dve", "tensor_tensor", t2, ni, A_im, ALU.mult)
            k.I("dve", "tensor_tensor", t1, t1, t2, ALU.add)
            k.I("dve", "tensor_tensor", cre, t1, den, ALU.mult)
            k.I("dve", "tensor_tensor", t1, ni, lam_re, ALU.mult)
            k.I("dve", "tensor_tensor", t2, nr, A_im, ALU.mult)
            k.I("dve", "tensor_tensor", t1, t1, t2, ALU.subtract)
            k.I("dve", "tensor_tensor", cim, t1, den, ALU.mult)
            k.I("dve", "tensor_scalar", W[:, 17, :], cim, -1.0, None, ALU.mult)
            ncim = W[:, 17, :]
            for i in range(64):
                br = k.nxt("stage")
                bi = k.nxt("stage")
                k.dma("sp", br[:, 0:128], self.s5_Bp[j, 0, i])
                k.dma("sp", bi[:, 0:128], self.s5_Bp[j, 1, i])
                o = k.nxt("tmpb")
                t = k.nxt("tmpf")
                k.I("dve", "tensor_scalar", t[:, 0:128], bi[:, 0:128], ncim[:, i:i + 1], None, ALU.mult)
                k.I("dve", "scalar_tensor_tensor", o[:, 0:128], br[:, 0:128], cre[:, i:i + 1], t[:, 0:128], ALU.mult, ALU.add)
                k.I("dve", "tensor_scalar", t[:, 128:256], br[:, 0:128], cim[:, i:i + 1], None, ALU.mult)
                k.I("dve", "scalar_tensor_tensor", o[:, 128:256], bi[:, 0:128], cre[:, i:i + 1], t[:, 128:256], ALU.mult, ALU.add)
                pt = self.bankT()
                k.I("pe", "transpose", pt[:, 0:128], o[:, 0:128], self.ident_b)
                k.I("pe", "transpose", pt[:, 128:256], o[:, 128:256], self.ident_b)
                o2 = k.nxt("tmpb")
                self.copy(o2[:, 0:256], pt[:, 0:256])
                k.dma("sp", self.s5_bbT[j, i], o2[:, 0:256].rearrange("p (a c) -> p a c", c=128))

    def s5_init_state(self, j):
        k = self.k
        k.I("dve", "memset", self.s5st[:, j, :, :], 0.0)
        k.I("dve", "memset", self.s5vi[:, j, :, :], 0.0)

    def s5_mixer(self, i, T):
        k = self.k
        j = i // 2
        TB = min(T, 128)
        with k.scope():
            uT = k.sb("uT", [128, KC, T], BF16)
            gT = k.sb("gT", [128, KC, T], BF16)
            abuf = k.sb("abuf", [128, self.NBK, 512], F32)
            k.rotating("s5lw", 2, [128, 4, 4, 128], BF16)
            k.rotating("s5f", 14, [128, T], F32)
            k.rotating("s5b", 4, [128, T], BF16)
            self.rmsnorm_T(self.xblk(TB), self.norm_mix[i:i + 1, :], T)
            xT = lambda kc, a, b: self.xnT[:, kc, a:b]

            def ucons(m, mw, ps):
                self.copy(TA(uT[:, m, :T], m), ps)
            self.linear_fm(xT, KC, self.w_in_odd[j], 0, D, T, ucons)
            nb2 = 0
            for cc in range(16):
                lw = k.nxt("s5lw")
                k.dma("sp", lw[:, :, 0:2, :], self.s5_bbT[j, cc * 4:(cc + 1) * 4].rearrange("i p a c -> p i a c"))
                for a in range(2):
                    k.dma("pool", lw[:, :, 2 + a, :], self.s5_Cp[j, a, cc * 4:(cc + 1) * 4].rearrange("i p c -> p i c"))
                psy = self.psA[cc % 2]
                for i4 in range(4):
                    ti = cc * 4 + i4
                    nb2 += 1
                    pbr = (self.psA[2], self.psB[0])[nb2 % 2]
                    pbi = (self.psA[3], self.psB[1])[nb2 % 2]
                    k.I("pe", "matmul", pbr[:, :T], lw[:, i4, 0, :], TA(uT[:, cc, :T], cc), start=True, stop=True)
                    k.I("pe", "matmul", pbi[:, :T], lw[:, i4, 1, :], TA(uT[:, cc, :T], cc), start=True, stop=True)
                    f = lambda: k.nxt("s5f")
                    ph, ph2, S, C = f(), f(), f(), f()
                    k.I("dve", "tensor_scalar", ph[:, :T], self.iota[:, :T], self.s5th[:, j, ti:ti + 1], None, ALU.mult)
                    k.I("dve", "tensor_scalar", ph2[:, :T], ph[:, :T], MAGIC, MAGIC, ALU.add, ALU.subtract)
                    k.I("dve", "tensor_tensor", ph[:, :T], ph[:, :T], ph2[:, :T], ALU.subtract)
                    k.I("act", "activation", out=ph2[:, :T], in_=ph[:, :T], func=AF.Abs)
                    k.I("act", "activation", out=S[:, :T], in_=ph[:, :T], func=AF.Sin, scale=TWO_PI)
                    k.I("act", "activation", out=C[:, :T], in_=ph2[:, :T], func=AF.Sin, scale=-TWO_PI, bias=0.5 * math.pi)
                    t1, t2, cre, cim = f(), f(), f(), f()
                    k.I("dve", "tensor_tensor", t1[:, :T], C[:, :T], pbr[:, :T], ALU.mult)
                    k.I("dve", "tensor_tensor", t2[:, :T], S[:, :T], pbi[:, :T], ALU.mult)
                    k.I("dve", "tensor_tensor", cre[:, :T], t1[:, :T], t2[:, :T], ALU.add)
                    k.I("dve", "tensor_tensor", t1[:, :T], C[:, :T], pbi[:, :T], ALU.mult)
                    k.I("dve", "tensor_tensor", t2[:, :T], S[:, :T], pbr[:, :T], ALU.mult)
                    k.I("dve", "tensor_tensor", cim[:, :T], t1[:, :T], t2[:, :T], ALU.subtract)
                    vre, vim = f(), f()
                    rb = self.s5r[:, j, ti:ti + 1].to_broadcast([128, T])
                    k.I("dve", "tensor_tensor_scan", vre[:, :T], rb, cre[:, :T], self.s5vi[:, j, 0, ti:ti + 1], ALU.mult, ALU.add)
                    k.I("dve", "tensor_tensor_scan", vim[:, :T], rb, cim[:, :T], self.s5vi[:, j, 1, ti:ti + 1], ALU.mult, ALU.add)
                    sre, simn = f(), f()
                    k.I("dve", "tensor_tensor", t1[:, :T], C[:, :T], vre[:, :T], ALU.mult)
                    k.I("dve", "tensor_tensor", t2[:, :T], S[:, :T], vim[:, :T], ALU.mult)
                    k.I("dve", "tensor_tensor", sre[:, :T], t1[:, :T], t2[:, :T], ALU.subtract)
                    k.I("dve", "tensor_tensor", t1[:, :T], C[:, :T], vim[:, :T], ALU.mult)
                    k.I("dve", "tensor_tensor", t2[:, :T], S[:, :T], vre[:, :T], ALU.mult)
                    k.I("dve", "scalar_tensor_tensor", simn[:, :T], t1[:, :T], -1.0, t2[:, :T], ALU.mult, ALU.subtract)
                    sb1, sb2 = k.nxt("s5b"), k.nxt("s5b")
                    k.I("act", "activation", out=sb1[:, :T], in_=sre[:, :T], func=AF.Copy)
                    k.I("act", "activation", out=sb2[:, :T], in_=simn[:, :T], func=AF.Copy)
                    k.I("act", "activation", out=self.s5st[:, j, 0, ti:ti + 1], in_=sre[:, T - 1:T], func=AF.Copy)
                    k.I("act", "activation", out=self.s5st[:, j, 1, ti:ti + 1], in_=simn[:, T - 1:T], func=AF.Copy)
                    k.I("act", "activation", out=self.s5vi[:, j, 0, ti:ti + 1], in_=sre[:, T - 1:T], func=AF.Copy)
                    k.I("act", "activation", out=self.s5vi[:, j, 1, ti:ti + 1], in_=simn[:, T - 1:T], func=AF.Copy, scale=-1.0)
                    k.I("pe", "matmul", psy[:, :T], lw[:, i4, 2, :], sb1[:, :T], start=(i4 == 0), stop=False)
                    k.I("pe", "matmul", psy[:, :T], lw[:, i4, 3, :], sb2[:, :T], start=False, stop=(i4 == 3))
                yf = k.nxt("tmpf")
                k.I("dve", "scalar_tensor_tensor", yf[:, :T], TA(uT[:, cc, :T], cc), self.s5Dt[:, j, cc:cc + 1], psy[:, :T],
                    ALU.mult, ALU.add)
                k.I("act", "activation", out=TA(gT[:, cc, :T], cc), in_=yf[:, :T], func=AF.Gelu)
            gTf = lambda kc, a, b: TA(gT[:, kc, a:b], kc)
            for cb in range(4):
                def acons(tb, cb_, w, ps, cb=cb):
                    self.copy(TA(abuf[:TB, tb, :], tb), ps)

                def gcons(tb, cb_, w, ps, cb=cb):
                    sg = k.nxt("tmpf")
                    k.I("act", "activation", out=sg[:TB, :], in_=ps, func=AF.Sigmoid)
                    k.I("dve", "tensor_tensor", sg[:TB, :], sg[:TB, :], TA(abuf[:TB, tb, :], tb), ALU.mult)
                    xa = TA(self.x[:TB, tb, cb * 512:(cb + 1) * 512], tb)
                    k.I("dve", "tensor_tensor", xa, xa, sg[:TB, :], ALU.add)
                self.linear_tm(gTf, KC, self.s5_w_glu[j], cb * 512, 512, T, acons)
                self.linear_tm(gTf, KC, self.s5_w_glu[j], D + cb * 512, 512, T, gcons)

    def s5_store_state(self, j, re_out, im_out):
        k = self.k
        ps = self.bankB()
        k.I("pe", "transpose", ps[0:64, 0:128], self.s5st[:, j, 0, :], self.ident_f)
        k.I("pe", "transpose", ps[0:64, 128:256], self.s5st[:, j, 1, :], self.ident_f)
        o = k.nxt("stage")
        k.I("act", "activation", out=o[0:64, 0:128], in_=ps[0:64, 0:128], func=AF.Copy)
        k.I("dve", "tensor_scalar", o[0:64, 128:256], ps[0:64, 128:256], -1.0, None, ALU.mult)
        k.dma("sp", re_out, o[0:64, 0:128])
        k.dma("sp", im_out, o[0:64, 128:256])

    def even_init_state(self, j):
        k = self.k
        k.I("dve", "memset", self.ssdH[:, j, :], 0.0)
        k.I("dve", "memset", self.convst[:, j, :, :], 0.0)
        k.I("dve", "memset", self.foxtot[:, j, :], 0.0)

    def even_mixer(self, i, T, t0, seq):
        k = self.k
        j = i // 2
        TB = min(T, 128)
        nb = T // TB
        W = self.w_in_even[j]
        evs = self.evs
        kt_scr, v_scr = seq["kt_scr"], seq["v_scr"]
        with k.scope():
            ofT = k.sb("ofT", [128, 8, T], BF16)
            ynT = k.sb("ynT", [128, 16, T], BF16)
            self.rmsnorm_T(self.xblk(TB), self.norm_mix[i:i + 1, :], T)
            xT = lambda kc, a, b: self.xnT[:, kc, a:b]
            with k.scope():
                qT = k.sb("qT", [128, 8, T], BF16)
                cTq = k.sb("cTq", [8, T], F32)
                cTm = k.sb("cTm", [8, 8, T], F32)
                k.rotating("kth", 2, [128, t0 + T], BF16)
                k.rotating("vh", 2, [128, (t0 + T + 127) // 128, 128], BF16)
                k.rotating("kst", 2, [128, 4, TB], BF16)
                k.rotating("sm32", 4, [128, 32], F32)
                gq = self.load_g128(self.fox_qn[j:j + 1, :])
                gk = self.load_g128(self.fox_kn[j:j + 1, :])

                def qkv(tb, cb, w, ps):
                    tok0 = t0 + tb * TB
                    if cb < 2:
                        qb = k.nxt("tmpb")
                        self.headnorm(ps, TB, 4, gq, lambda h: qb[:TB, h * 128:(h + 1) * 128], extra=128.0 ** -0.5)
                        pt = self.bankT()
                        for h in range(4):
                            k.I("pe", "transpose", pt[:, h * TB:(h + 1) * TB], qb[:TB, h * 128:(h + 1) * 128], self.ident_b[:TB, :TB])
                        self.copy(qT[:, cb * 4:cb * 4 + 4, tb * TB:(tb + 1) * TB], pt[:, 0:4 * TB].rearrange("p (h t) -> p h t", t=TB))
                    elif cb < 4:
                        c2 = cb - 2
                        stg = k.nxt("stage")
                        self.headnorm(ps, TB, 4, gk, lambda h: stg[:TB, h * 128:(h + 1) * 128])
                        k.dma("sp", seq["fox_k"][j, tok0:tok0 + TB, c2 * 512:(c2 + 1) * 512], stg[:TB, :])
                        kb = k.nxt("tmpb")
                        k.I("dve", "tensor_copy", kb[:TB, :], stg[:TB, :])
                        pt = self.bankT()
                        for h in range(4):
                            k.I("pe", "transpose", pt[:, h * TB:(h + 1) * TB], kb[:TB, h * 128:(h + 1) * 128], self.ident_b[:TB, :TB])
                        kst = k.nxt("kst")
                        self.copy(kst[:, :, :TB], pt[:, 0:4 * TB].rearrange("p (h t) -> p h t", t=TB))
                        k.dma("sp", kt_scr[j, :, c2 * 4:c2 * 4 + 4, tok0:tok0 + TB], kst[:, :, :TB])
                    else:
                        c2 = cb - 4
                        stg = k.nxt("stage")
                        k.I("act", "activation", out=stg[:TB, :], in_=ps, func=AF.Copy)
                        k.dma("sp", seq["fox_v"][j, tok0:tok0 + TB, c2 * 512:(c2 + 1) * 512], stg[:TB, :])
                        vb = k.nxt("tmpb")
                        k.I("dve", "tensor_copy", vb[:TB, :], stg[:TB, :])
                        k.dma("sp", v_scr[j, tok0:tok0 + TB, c2 * 512:(c2 + 1) * 512], vb[:TB, :])
                self.linear_tm(xT, KC, W, 0, 3072, T, qkv)

                def fcons(tb, cb, w, ps):
                    tok0 = t0 + tb * TB
                    blk = tok0 // 128
                    s1 = k.nxt("sm32")
                    s2 = k.nxt("sm32")
                    k.I("dve", "tensor_tensor", s1[:TB, 0:8], ps, evs[:TB, j, 0:8], ALU.add)
                    k.I("act", "activation", out=s1[:TB, 8:16], in_=s1[:TB, 0:8], func=AF.Exp, scale=-1.0)
                    k.I("act", "activation", out=s1[:TB, 16:24], in_=s1[:TB, 8:16], func=AF.Ln, bias=1.0)
                    k.I("dve", "tensor_scalar", s2[:TB, 0:8], s1[:TB, 16:24], -1.0, None, ALU.mult)
                    k.dma("sp", seq["fox_lf"][j, tok0:tok0 + TB, :], s2[:TB, 0:8])
                    pc = self.bankB()
                    k.I("pe", "matmul", pc[:TB, 0:8], self.U_f[:TB, :TB], s2[:TB, 0:8], start=True, stop=True)
                    k.I("pe", "matmul", pc[:, 8:16], self.onesf[:TB, :], s2[:TB, 0:8], start=True, stop=True)
                    k.I("dve", "tensor_tensor", s2[:TB, 8:16], pc[:TB, 0:8], self.foxtot[:TB, j, :], ALU.add)
                    k.I("dve", "tensor_scalar", self.negc[:TB, j, blk, :], s2[:TB, 8:16], -1.0, None, ALU.mult)
                    k.I("dve", "tensor_tensor", self.foxtot[:, j, :], self.foxtot[:, j, :], pc[:, 8:16], ALU.add)
                    pc2 = self.bankB()
                    k.I("pe", "transpose", pc2[0:8, 0:TB], s2[:TB, 8:16], self.ident_f[:TB, :TB])
                    k.I("act", "activation", out=cTq[0:8, tb * TB:(tb + 1) * TB], in_=pc2[0:8, 0:TB], func=AF.Copy)
                self.linear_tm(xT, KC, W, 3072, 8, T, fcons)
                for h in range(8):
                    k.I("dve", "tensor_scalar", cTm[0:8, h, :], cTq[0:8, :], self.ident_f[0:8, h:h + 1], None, ALU.mult)

                if seq.get("past"):
                    self.attention_sample(j, T, qT, cTm, ofT, seq)
                else:
                    tend = t0 + T
                    nkb = (tend + 127) // 128
                    for h in range(8):
                        kth = k.nxt("kth")
                        vh = k.nxt("vh")
                        k.dma("sp", kth[:, 0:tend], kt_scr[j, :, h, 0:tend])
                        k.dma("sp", vh[:, 0:nkb, :], v_scr[j, 0:tend, h * 128:(h + 1) * 128].rearrange("(b s) d -> s b d", s=128))
                        pso = self.psA[2 + h % 2]
                        psd = self.psB[h % 2]
                        for kb in range(nkb):
                            ks = min(128, tend - kb * 128)
                            q0 = max(0, kb * 128 - t0)
                            N = T - q0
                            pss = self.psA[kb % 2]
                            k.I("pe", "matmul", pss[:ks, :N], kth[:, kb * 128:kb * 128 + ks], qT[:, h, q0:T], start=True, stop=False)
                            k.I("pe", "matmul", pss[:ks, :N], self.onesf[0:8, 0:ks], cTm[0:8, h, q0:T], start=False, stop=True)
                            pT = k.nxt("pT")
                            k.I("act", "activation", out=pT[:ks, :N], in_=pss[:ks, :N], func=AF.Exp, bias=self.negc[:ks, j, kb, h:h + 1])
                            if kb * 128 >= t0:
                                dw = min(128, N)
                                k.I("dve", "tensor_tensor", pT[:ks, 0:dw], pT[:ks, 0:dw], self.maskU_b[:ks, 0:dw], ALU.mult)
                            k.I("pe", "matmul", pso[:, q0:T], vh[:ks, kb, :], pT[:ks, :N], start=(kb == 0), stop=(kb == nkb - 1))
                            k.I("pe", "matmul", psd[:, q0:T], self.ones_b[:ks, :], pT[:ks, :N], start=(kb == 0), stop=(kb == nkb - 1))
                        rd = k.nxt("tmpf")
                        k.I("dve", "reciprocal", rd[:, :T], psd[:, :T])
                        k.I("dve", "tensor_tensor", ofT[:, h, :T], pso[:, :T], rd[:, :T], ALU.mult)

            with k.scope():
                zs = k.sb("zs", [128, nb, D], BF16)
                xbcT = k.sb("xbcT", [128, 24, T], BF16)
                dtt = k.sb("dtt", [128, nb, 32], F32)
                att = k.sb("att", [128, nb, 32], F32)
                k.rotating("craw", 2, [128, T + 3], F32)
                k.rotating("sm32", 8, [128, 32], F32)
                k.rotating("sq", 6, [128, 128], F32)
                k.rotating("sqb", 3, [128, 128], BF16)
                k.rotating("cbs", 2, [128, 128], F32)
                k.rotating("acm", 4, [32, 128], F32)
                xs_tm = k.sb("xs_tm", [128, D], BF16)
                B_tm = k.sb("B_tm", [128, 512], BF16)
                xdt = k.sb("xdt", [128, D], BF16)
                xdtw = k.sb("xdtw", [128, D], BF16)
                ysb = k.sb("ysb", [128, D], F32)
                hbf = k.sb("hbf", [128, D], BF16)
                acT = k.sb("acT", [32, 128], F32)

                def zcons(tb, cb, w, ps):
                    k.I("act", "activation", out=TA(zs[:TB, tb, cb * 512:(cb + 1) * 512], tb), in_=ps, func=AF.Silu)
                self.linear_tm(xT, KC, W, 3080, 2048, T, zcons)

                def xbc_cons(m, mw, ps):
                    cr = k.nxt("craw")
                    k.I("dve", "tensor_copy", cr[:, 0:3], self.convst[:, j, m, :])
                    k.I("act", "activation", out=cr[:, 3:3 + T], in_=ps, func=AF.Copy)
                    k.I("dve", "tensor_copy", self.convst[:, j, m, :], cr[:, T:T + 3])
                    acc = k.nxt("tmpf")
                    cw = self.cwT[:, j, m, :]
                    k.I("dve", "tensor_scalar", acc[:, :T], cr[:, 0:T], cw[:, 0:1], cw[:, 4:5], ALU.mult, ALU.add)
                    for kk in range(1, 4):
                        k.I("dve", "scalar_tensor_tensor", acc[:, :T], cr[:, kk:kk + T], cw[:, kk:kk + 1], acc[:, :T], ALU.mult, ALU.add)
                    k.I("act", "activation", out=TA(xbcT[:, m, :T], m), in_=acc[:, :T], func=AF.Silu)
                self.linear_fm(xT, KC, W, 5128, 3072, T, xbc_cons)

                def dtcons(tb, cb, w, ps):
                    s1 = k.nxt("sm32")
                    k.I("dve", "tensor_tensor", s1[:TB, :], ps, evs[:TB, j, 8:40], ALU.add)
                    k.I("act", "activation", out=s1[:TB, :], in_=s1[:TB, :], func=AF.Exp)
                    k.I("act", "activation", out=TA(dtt[:TB, tb, :], tb), in_=s1[:TB, :], func=AF.Ln, bias=1.0)
                    k.I("dve", "tensor_tensor", TA(att[:TB, tb, :], tb), TA(dtt[:TB, tb, :], tb), self.evA[:TB, j, :], ALU.mult)
                self.linear_tm(xT, KC, W, 8200, 32, T, dtcons)

                gss = self.load_gain(self.ssd_norm[j:j + 1, :])
                H = self.ssdH
                dttn = dtt[:].tensor.name
                for tb in range(nb):
                    cs = slice(tb * TB, (tb + 1) * TB)
                    a_t = TA(att[:TB, tb, :], tb)
                    pa = self.bankB()
                    k.I("pe", "matmul", pa[:TB, 0:32], self.U_f[:TB, :TB], a_t, start=True, stop=True)
                    k.I("pe", "matmul", pa[:, 32:64], self.onesf[:TB, :], a_t, start=True, stop=True)
                    acum = k.nxt("sm32")
                    k.I("act", "activation", out=acum[:TB, :], in_=pa[:TB, 0:32], func=AF.Copy)
                    expa = k.nxt("sm32")
                    k.I("act", "activation", out=expa[:TB, :], in_=pa[:TB, 0:32], func=AF.Exp)
                    cd = k.nxt("sm32")
                    k.I("act", "activation", out=cd[:, :], in_=pa[:, 32:64], func=AF.Exp)
                    wgt = k.nxt("sm32")
                    k.I("dve", "tensor_tensor", wgt[:TB, :], pa[:TB, 32:64], acum[:TB, :], ALU.subtract)
                    k.I("act", "activation", out=wgt[:TB, :], in_=wgt[:TB, :], func=AF.Exp)
                    k.I("dve", "tensor_tensor", wgt[:TB, :], wgt[:TB, :], TA(dtt[:TB, tb, :], tb), ALU.mult)
                    pa2 = self.bankB()
                    k.I("pe", "transpose", pa2[0:32, 0:TB], acum[:TB, :], self.ident_f[:TB, :TB])
                    k.I("act", "activation", out=acT[:, 0:TB], in_=pa2[0:32, 0:TB], func=AF.Copy)
                    for c0 in range(0, 20, 8):
                        n8 = min(8, 20 - c0)
                        pt = self.bankT()
                        for c8 in range(n8):
                            m = c0 + c8
                            k.I("pe", "transpose", pt[:TB, c8 * 128:(c8 + 1) * 128], TA(xbcT[:, m, cs], m), self.ident_b)
                        if c0 < 16:
                            self.copy(xs_tm[:TB, c0 * 128:(c0 + n8) * 128], pt[:TB, 0:n8 * 128])
                        else:
                            self.copy(B_tm[:TB, 0:512], pt[:TB, 0:512])
                    v3 = lambda t: t.rearrange("p (h d) -> p h d", d=64)
                    k.I("dve", "tensor_tensor", v3(xdt[:TB, :]), v3(xs_tm[:TB, :]),
                        dtt[:TB, tb, :].unsqueeze(2).to_broadcast([TB, 32, 64]), ALU.mult, _R=[(dttn, tb)])
                    k.I("dve", "tensor_tensor", v3(xdtw[:TB, :]), v3(xs_tm[:TB, :]),
                        wgt[:TB, :].unsqueeze(2).to_broadcast([TB, 32, 64]), ALU.mult)
                    k.I("act", "activation", out=hbf[:, :], in_=H[:, j, :], func=AF.Copy)
                    for g in range(4):
                        BT = TA(xbcT[:, 16 + g, cs], 16 + g)
                        CT = TA(xbcT[:, 20 + g, cs], 20 + g)
                        pcb = self.psA[0]
                        k.I("pe", "matmul", pcb[:TB, :TB], BT, CT, start=True, stop=True)
                        cbs = k.nxt("cbs")
                        k.I("act", "activation", out=cbs[:TB, :TB], in_=pcb[:TB, :TB], func=AF.Copy)
                        pyo = self.psA[1]
                        k.I("pe", "matmul", pyo[:TB, :], CT, hbf[:, g * 512:(g + 1) * 512], start=True, stop=True)
                        pyd = self.psA[2]
                        for r in range(8):
                            hh = 8 * g + r
                            pbc = self.psB[r % 2]
                            acm = k.nxt("acm")
                            k.I("dve", "tensor_scalar", acm[:, 0:TB], acT[:, 0:TB], self.ident_f[0:32, hh:hh + 1], None, ALU.mult)
                            k.I("pe", "matmul", pbc[:TB, :TB], self.onesf[0:32, 0:TB], acm[:, 0:TB], start=True, stop=True)
                            tm = k.nxt("sq")
                            k.I("dve", "scalar_tensor_tensor", tm[:TB, :TB], pbc[:TB, :TB], acum[:TB, hh:hh + 1], self.maskneg[:TB, :TB],
                                ALU.subtract, ALU.add)
                            k.I("act", "activation", out=tm[:TB, :TB], in_=tm[:TB, :TB], func=AF.Exp)
                            MT = k.nxt("sqb")
                            k.I("dve", "tensor_tensor", MT[:TB, :TB], tm[:TB, :TB], cbs[:TB, :TB], ALU.mult)
                            k.I("pe", "matmul", pyd[:TB, r * 64:(r + 1) * 64], MT[:TB, :TB], xdt[:TB, hh * 64:(hh + 1) * 64], start=True, stop=True)
                        tf = k.nxt("tmpf")
                        k.I("dve", "tensor_tensor", v3(tf[:TB, :]), v3(pyo[:TB, :]),
                            expa[:TB, 8 * g:8 * g + 8].unsqueeze(2).to_broadcast([TB, 8, 64]), ALU.mult)
                        k.I("dve", "tensor_tensor", ysb[:TB, g * 512:(g + 1) * 512], tf[:TB, :], pyd[:TB, :], ALU.add)
                        pst = self.psA[3]
                        k.I("pe", "matmul", pst[:, :], B_tm[:TB, g * 128:(g + 1) * 128], xdtw[:TB, g * 512:(g + 1) * 512], start=True, stop=True)
                        Hg = H[:, j, g * 512:(g + 1) * 512]
                        k.I("dve", "tensor_tensor", v3(Hg), v3(Hg), cd[:, 8 * g:8 * g + 8].unsqueeze(2).to_broadcast([128, 8, 64]), ALU.mult)
                        k.I("dve", "tensor_tensor", Hg, Hg, pst[:, :], ALU.add)
                    tD = k.nxt("xnb")
                    k.I("dve", "tensor_tensor", v3(tD[:TB, :]), v3(xs_tm[:TB, :]),
                        evs[:TB, j, 72:104].unsqueeze(2).to_broadcast([TB, 32, 64]), ALU.mult)
                    k.I("dve", "tensor_tensor", ysb[:TB, :], ysb[:TB, :], tD[:TB, :], ALU.add)
                    k.I("dve", "tensor_tensor", ysb[:TB, :], ysb[:TB, :], TA(zs[:TB, tb, :], tb), ALU.mult)
                    ynb = k.nxt("xnb")
                    self.norm_rows(ysb[:TB, :], gss, TB, ynb)
                    self.transpose16(ynb, TB, ynT, tb)

            if self.cfg.get("dbg"):
                k.dma("sp", self.dbg_of[t0 // T], ofT[:, :, :])
                k.dma("sp", self.dbg_yn[t0 // T], ynT[:, :, :])

            def oT(kc, a, b):
                return ofT[:, kc, a:b] if kc < 8 else ynT[:, kc - 8, a:b]
            self.linear_tm(oT, 24, self.w_out_even[j], 0, D, T, self.resid_add(TB))

    def attention_sample(self, j, T, qT, cTm, ofT, seq):
        k = self.k
        NP = seq["n_pages"]
        nrows = seq["cfk"][0].shape[0]
        k.rotating("kpg", 2, [128, 1024], F32)
        k.rotating("vpg", 2, [128, 1024], F32)
        k.rotating("kpb", 2, [128, 1024], BF16)
        k.rotating("vpb", 2, [128, 1024], BF16)
        k.rotating("ktp", 2, [128, 8, 128], BF16)
        k.rotating("e64", 3, [128, 64], F32)
        k.rotating("p64", 3, [128, 64], BF16)
        lf = k.sb("lf_all", [128, NP, 8], F32)
        tail = k.sb("tail", [128, NP, 8], F32)
        pref = k.sb("pref", [128, NP, 8], F32)
        idx = self.pidx
        for pg in range(NP):
            k.dma("pool", lf[:, pg, :], seq["cfl"][j], meth="indirect_dma_start", out_offset=None,
                  in_offset=bass.IndirectOffsetOnAxis(ap=idx[:, pg:pg + 1], axis=0),
                  xr=[(idx[:].tensor.name, None)])
        lf2 = lf[:, :, :].rearrange("p a h -> p (a h)")
        tl2 = tail[:, :, :].rearrange("p a h -> p (a h)")
        pf2 = pref[:, :, :].rearrange("p a h -> p (a h)")
        ncol = NP * 8
        for c0 in range(0, ncol, 512):
            cw = min(512, ncol - c0)
            p1 = self.psA[0]
            p2 = self.psA[1]
            k.I("pe", "matmul", p1[:, :cw], self.cst[:, 256:384], lf2[:, c0:c0 + cw], start=True, stop=True)
            k.I("pe", "matmul", p2[:, :cw], self.onesf[:, :], lf2[:, c0:c0 + cw], start=True, stop=True)
            k.I("act", "activation", out=tl2[:, c0:c0 + cw], in_=p1[:, :cw], func=AF.Copy)
            k.I("act", "activation", out=pf2[:, c0:c0 + cw], in_=p2[:, :cw], func=AF.Copy)
        for h in range(8):
            k.I("dve", "tensor_tensor_scan", lf[:, :, h], self.onesf[:, 0:NP], pref[:, :, h], 0.0, ALU.mult, ALU.add)
        for h in range(8):
            k.I("dve", "tensor_scalar", pref[:, :, h], lf[:, :, h], -1.0, lf[:, NP - 1, h:h + 1], ALU.mult, ALU.add)
        k.I("dve", "tensor_tensor", tl2, tl2, pf2, ALU.add)
        pso = self.psA[2]
        psd = self.psA[3]
        for pg in range(NP):
            kpg, vpg = k.nxt("kpg"), k.nxt("vpg")
            for (dst, src) in ((kpg, seq["cfk"]), (vpg, seq["cfv"])):
                k.dma("pool", dst[:, :], src[j], meth="indirect_dma_start", out_offset=None,
                      in_offset=bass.IndirectOffsetOnAxis(ap=idx[:, pg:pg + 1], axis=0),
                      xr=[(idx[:].tensor.name, None)])
            kpb, vpb = k.nxt("kpb"), k.nxt("vpb")
            k.I("act", "activation", out=kpb[:, :], in_=kpg[:, :], func=AF.Copy)
            k.I("dve", "tensor_copy", vpb[:, :], vpg[:, :])
            pt = self.bankT()
            for h in range(8):
                k.I("pe", "transpose", pt[:, h * 128:(h + 1) * 128], kpb[:, h * 128:(h + 1) * 128], self.ident_b)
            ktp = k.nxt("ktp")
            self.copy(ktp[:, :, :], pt[:, :].rearrange("p (h s) -> p h s", s=128))
            pss = self.psA[pg % 2]
            for h in range(8):
                k.I("pe", "matmul", pss[:, h * T:(h + 1) * T], ktp[:, h, :], qT[:, h, 0:T], start=True, stop=False)
                k.I("pe", "matmul", pss[:, h * T:(h + 1) * T], self.onesf[0:8, :], cTm[0:8, h, 0:T], start=False, stop=True)
            e = k.nxt("e64")
            k.I("dve", "tensor_tensor", e[:, :].rearrange("p (h t) -> p h t", t=T), pss[:, 0:8 * T].rearrange("p (h t) -> p h t", t=T),
                tail[:, pg, :].unsqueeze(2).to_broadcast([128, 8, T]), ALU.add)
            pT = k.nxt("p64")
            k.I("act", "activation", out=pT[:, :], in_=e[:, :], func=AF.Exp)
            for h in range(8):
                k.I("pe", "matmul", pso[:, h * T:(h + 1) * T], vpb[:, h * 128:(h + 1) * 128], pT[:, h * T:(h + 1) * T], start=(pg == 0), stop=False)
                k.I("pe", "matmul", psd[:, h * T:(h + 1) * T], self.ones_b, pT[:, h * T:(h + 1) * T], start=(pg == 0), stop=False)
        kt_scr, v_scr = seq["kt_scr"], seq["v_scr"]
        ktn = k.nxt("ktp")
        vn = k.nxt("vpb")
        k.dma("sp", ktn[:, :, 0:T], kt_scr[j, :, :, 0:T])
        k.dma("sp", vn[:T, :], v_scr[j, 0:T, :])
        pss = self.psA[0]
        for h in range(8):
            k.I("pe", "matmul", pss[:T, h * T:(h + 1) * T], ktn[:, h, 0:T], qT[:, h, 0:T], start=True, stop=False)
            k.I("pe", "matmul", pss[:T, h * T:(h + 1) * T], self.onesf[0:8, 0:T], cTm[0:8, h, 0:T], start=False, stop=True)
        e = k.nxt("e64")
        k.I("dve", "tensor_tensor", e[:T, :].rearrange("p (h t) -> p h t", t=T), pss[:T, 0:8 * T].rearrange("p (h t) -> p h t", t=T),
            self.negc[:T, j, 0, :].unsqueeze(2).to_broadcast([T, 8, T]), ALU.add)
        pT = k.nxt("p64")
        k.I("act", "activation", out=pT[:T, :], in_=e[:T, :], func=AF.Exp)
        k.I("dve", "tensor_tensor", pT[:T, :].rearrange("p (h t) -> p h t", t=T), pT[:T, :].rearrange("p (h t) -> p h t", t=T),
            self.maskU_b[:T, 0:T].unsqueeze(1).to_broadcast([T, 8, T]), ALU.mult)
        for h in range(8):
            k.I("pe", "matmul", pso[:, h * T:(h + 1) * T], vn[:T, h * 128:(h + 1) * 128], pT[:T, h * T:(h + 1) * T], start=(NP == 0), stop=True)
            k.I("pe", "matmul", psd[:, h * T:(h + 1) * T], self.ones_b[:T, :], pT[:T, h * T:(h + 1) * T], start=(NP == 0), stop=True)
        rd = k.nxt("e64")
        k.I("dve", "reciprocal", rd[:, :], psd[:, 0:8 * T])
        k.I("dve", "tensor_tensor", ofT[:, :, :].rearrange("p h t -> p (h t)"), pso[:, 0:8 * T], rd[:, :], ALU.mult)

    def even_store_state(self, j, ssd_out, conv_out):
        k = self.k
        for m in range(16):
            ps = self.bankB()
            k.I("pe", "transpose", ps[:, 0:128], self.ssdH[:, j, m * 128:(m + 1) * 128], self.ident_f)
            o = k.nxt("stage")
            self.copy(o[:, 0:128], ps[:, 0:128])
            k.dma("sp", ssd_out[m * 128:(m + 1) * 128, :], o[:, 0:128])
        for m in range(24):
            k.dma("sp", conv_out[:, m * 128:(m + 1) * 128].rearrange("k p -> p k"), self.convst[:, j, m, :],
                  allow_slow_non_contiguous=True)

    def even_load_state(self, j, ssd_in, conv_in):
        k = self.k
        for m in range(16):
            stg = k.nxt("stage")
            k.dma("sp", stg[:, 0:128], ssd_in[m * 128:(m + 1) * 128, :])
            ps = self.bankB()
            k.I("pe", "transpose", ps[:, 0:128], stg[:, 0:128], self.ident_f)
            self.copy(self.ssdH[:, j, m * 128:(m + 1) * 128], ps[:, 0:128])
        for m in range(24):
            k.dma("sp", self.convst[:, j, m, :], conv_in[:, m * 128:(m + 1) * 128].rearrange("k p -> p k"),
                  allow_slow_non_contiguous=True)
        k.I("dve", "memset", self.foxtot[:, j, :], 0.0)

    def s5_load_state(self, j, re_in, im_in):
        k = self.k
        stg = k.nxt("stage")
        k.dma("sp", stg[0:64, 0:128], re_in)
        k.dma("sp", stg[0:64, 128:256], im_in)
        ps = self.bankB()
        k.I("pe", "transpose", ps[:, 0:64], stg[0:64, 0:128], self.ident_f[0:64, 0:64])
        k.I("pe", "transpose", ps[:, 64:128], stg[0:64, 128:256], self.ident_f[0:64, 0:64])
        k.I("act", "activation", out=self.s5st[:, j, 0, :], in_=ps[:, 0:64], func=AF.Copy)
        k.I("dve", "tensor_copy", self.s5vi[:, j, 0, :], ps[:, 0:64])
        k.I("dve", "tensor_copy", self.s5vi[:, j, 1, :], ps[:, 64:128])
        k.I("act", "activation", out=self.s5st[:, j, 1, :], in_=ps[:, 64:128], func=AF.Copy, scale=-1.0)

    def run_seq(self, seq):
        k = self.k
        T = seq["T"]
        TB = min(T, 128)
        nb = T // TB
        ntile = seq["SEQ"] // T
        for ti in range(ntile):
            for tb in range(nb):
                k.dma("sp", TA(self.x[:TB, tb, :], tb), seq["x_in"][ti * T + tb * TB: ti * T + (tb + 1) * TB, :])
            for i in range(self.depth):
                if "mixer" in self.parts:
                    if i % 2 == 1:
                        self.s5_mixer(i, T)
                    else:
                        self.even_mixer(i, T, ti * T, seq)
                if "cross" in self.parts:
                    self.cross(i, T, seq["mem_k"][i], seq["mem_v"][i])
                if "ffn" in self.parts:
                    self.ffn(i, T)
            for tb in range(nb):
                k.dma("sp", seq["y_out"][ti * T + tb * TB: ti * T + (tb + 1) * TB, :], TA(self.x[:TB, tb, :], tb))
        if "mixer" in self.parts:
            for j in range(self.nod):
                self.s5_store_state(j, seq["s5_re"][j], seq["s5_im"][j])
            for j in range(self.nev):
                self.even_store_state(j, seq["ssd"][j], seq["conv"][j])

    def build(self):
        k = self.k
        if "memkv" in self.parts:
            self.mem_kv_prompt()
        if "mixer" in self.parts:
            for j in range(self.nod):
                self.s5_setup(j)
                self.s5_init_state(j)
            for j in range(self.nev):
                self.even_init_state(j)
        pseq = dict(T=self.T, SEQ=self.SEQ, x_in=self.xp, y_out=self.y_p, mem_k=self.mem_k_p, mem_v=self.mem_v_p,
                    kt_scr=self.kt_scr, v_scr=self.v_scr, fox_k=self.fox_k_p, fox_v=self.fox_v_p, fox_lf=self.fox_logf_p,
                    ssd=self.ssd_p, conv=self.conv_p, past=False)
        if self.nod:
            pseq.update(s5_re=self.s5_re_p, s5_im=self.s5_im_p)
        self.run_seq(pseq)
        if self.do_sample:
            k.barrier()
            NP = self.NP
            pti = k.sb("pti", [128, NP], I32)
            ptf = k.sb("ptf", [128, NP], F32)
            self.pidx = k.sb("pidx", [128, NP], I32)
            k.dma("sp", pti[:], self.ptab[0:1, :].partition_broadcast(128))
            k.I("dve", "tensor_scalar", ptf[:], pti[:], 128.0, None, ALU.mult)
            k.I("dve", "tensor_tensor", self.pidx[:], ptf[:], self.cst[:, 384:384 + NP], ALU.add)
            if "mixer" in self.parts:
                for j in range(self.nod):
                    self.s5_load_state(j, self.s5re_in[j], self.s5im_in[j])
                for j in range(self.nev):
                    self.even_load_state(j, self.sssd[j], self.sconv[j])
            sseq = dict(T=8, SEQ=8, x_in=self.xs_in, y_out=self.y_s, mem_k=self.cmk, mem_v=self.cmv,
                        kt_scr=self.kt_scr_s, v_scr=self.v_scr_s, fox_k=self.fox_k_s, fox_v=self.fox_v_s, fox_lf=self.fox_logf_s,
                        ssd=self.ssd_s, conv=self.conv_s, past=True, n_pages=NP, cfk=self.cfk, cfv=self.cfv, cfl=self.cfl)
            if self.nod:
                sseq.update(s5_re=self.s5_re_s, s5_im=self.s5_im_s)
            self.run_seq(sseq)
        k.finish()
        return k.nc


def make_consts():
    c = np.zeros((128, 1024), np.float32)
    c[:, 0:128] = np.eye(128, dtype=np.float32)
    s = np.arange(128)[:, None]
    t = np.arange(128)[None, :]
    c[:, 128:256] = (s <= t)
    c[:, 256:384] = (s > t)
    c[:, 384:512] = np.arange(128, dtype=np.float32)[:, None]
    c[0, 512:1024] = np.arange(1, 513)
    es = np.zeros((32, 32, 128), np.float32)
    for h in range(32):
        es[h, h, :] = 1.0
    return c, es.reshape(32, 32 * 128)


def even_layout(ev):
    nev = ev['w_in_even'].shape[0]
    small = np.zeros((nev, 136), np.float32)
    small[:, 0:8] = ev['fox_b_forget']
    small[:, 8:40] = ev['ssd_dt_bias']
    small[:, 40:72] = ev['ssd_A_log']
    small[:, 72:104] = ev['ssd_D']
    cw = np.zeros((nev, 128, 24, 5), np.float32)
    for j in range(nev):
        cw[j, :, :, 0:4] = ev['ssd_conv_w'][j].T.reshape(24, 128, 4).transpose(1, 0, 2)
        cw[j, :, :, 4] = ev['ssd_conv_b'][j].reshape(24, 128).T
    return dict(w_in_even=np.ascontiguousarray(ev['w_in_even']), w_out_even=np.ascontiguousarray(ev['w_out_even']),
                fox_q_norm=np.ascontiguousarray(ev['fox_q_norm']), fox_k_norm=np.ascontiguousarray(ev['fox_k_norm']),
                ev_small=small, ssd_norm=np.ascontiguousarray(ev['ssd_norm']), conv_wT=cw.reshape(nev, 128, 120))


def s5_layout(A_re, A_im, log_dt, B_re, B_im, C_re, C_im, Dv):
    nod = A_re.shape[0]
    par = np.zeros((nod, 128, 3 * 64), np.float32)
    Bp = np.zeros((nod, 2, 64, 128, 128), np.float32)
    Cp = np.zeros((nod, 2, 64, 128, 128), np.float32)
    Dl = np.zeros((nod, 128, 16), np.float32)
    for j in range(nod):
        a_re = A_re[j].reshape(64, 2, 64).transpose(1, 2, 0).reshape(128, 64)
        a_im = A_im[j].reshape(64, 2, 64).transpose(1, 2, 0).reshape(128, 64)
        ld = np.repeat(log_dt[j].reshape(64, 2, 1), 64, axis=2).transpose(1, 2, 0).reshape(128, 64)
        par[j, :, 0:64] = a_re
        par[j, :, 64:128] = a_im
        par[j, :, 128:192] = ld
        for i in range(64):
            for gp in range(2):
                g = 2 * i + gp
                c0 = (i % 4) * 32 + gp * 16
                Bp[j, 0, i, gp * 64:(gp + 1) * 64, c0:c0 + 16] = B_re[j, g]
                Bp[j, 1, i, gp * 64:(gp + 1) * 64, c0:c0 + 16] = B_im[j, g]
                Cp[j, 0, i, gp * 64:(gp + 1) * 64, c0:c0 + 16] = C_re[j, g].T
                Cp[j, 1, i, gp * 64:(gp + 1) * 64, c0:c0 + 16] = C_im[j, g].T
        Dl[j] = Dv[j].reshape(16, 128).T
    return par, Bp, Cp, Dl


_CACHE = {}


def _f32(a):
    return np.ascontiguousarray(np.asarray(a), dtype=np.float32)


def make_in_maps(inp, ncores, depth):
    nev = (depth + 1) // 2
    nod = depth // 2
    c, _ = make_consts()
    sh = dict(consts=c)
    for kk in ("norm_mix", "norm_cross", "norm_mem", "norm_ffn", "w_mq", "w_mkv", "mem_q_norm", "mem_k_norm", "w_mo",
               "w_ffn_up", "w_ffn_down"):
        sh[kk] = _f32(inp[kk])
    sh.update(even_layout({kk: np.asarray(inp[kk]) for kk in ("w_in_even", "w_out_even", "fox_b_forget", "fox_q_norm", "fox_k_norm",
                                                              "ssd_conv_w", "ssd_conv_b", "ssd_dt_bias", "ssd_A_log", "ssd_D", "ssd_norm")}))
    if nod:
        par, Bp, Cp, Dl = s5_layout(np.asarray(inp["s5_A_re"]), np.asarray(inp["s5_A_im"]), np.asarray(inp["s5_log_dt"]),
                                    np.asarray(inp["s5_B_re"]), np.asarray(inp["s5_B_im"]), np.asarray(inp["s5_C_re"]),
                                    np.asarray(inp["s5_C_im"]), np.asarray(inp["s5_D"]))
        sh.update(w_in_odd=_f32(inp["w_in_odd"]), s5_w_glu=_f32(inp["s5_w_glu"]), s5_par=par, s5_Bp=Bp, s5_Cp=Cp, s5_D=Dl)
    cfk = _f32(inp["cache_fox_k"]).reshape(nev, -1, 1024)
    cfv = _f32(inp["cache_fox_v"]).reshape(nev, -1, 1024)
    cfl = _f32(inp["cache_fox_logf"]).reshape(nev, -1, 8)
    for jj in range(nev):
        sh["cfk%d" % jj] = cfk[jj]
        sh["cfv%d" % jj] = cfv[jj]
        sh["cfl%d" % jj] = cfl[jj]
    B = np.asarray(inp["x_prompt"]).shape[0]
    DB = np.asarray(inp["x_sample"]).shape[0]
    maps = []
    for core in range(ncores):
        b = core % B
        sb = core % DB
        m = dict(sh)
        m["xp"] = _f32(inp["x_prompt"][b])
        m["mem"] = _f32(inp["mem_prompt"][b])
        m["xs"] = _f32(inp["x_sample"][sb])
        m["cmk"] = _f32(np.asarray(inp["cache_mem_k"])[:, sb]).reshape(depth, 256, 512)
        m["cmv"] = _f32(np.asarray(inp["cache_mem_v"])[:, sb]).reshape(depth, 256, 512)
        m["sssd"] = _f32(np.asarray(inp["state_ssd"])[:, sb]).reshape(nev, 2048, 128)
        m["sconv"] = _f32(np.asarray(inp["state_conv"])[:, sb])
        if nod:
            m["s5re_in"] = _f32(np.asarray(inp["state_s5_re"])[:, sb]).reshape(nod, 64, 128)
            m["s5im_in"] = _f32(np.asarray(inp["state_s5_im"])[:, sb]).reshape(nod, 64, 128)
        m["ptab"] = np.ascontiguousarray(np.asarray(inp["page_table"])[sb:sb + 1].astype(np.int32))
        maps.append(m)
    return maps


def assemble(results, ncores, depth, SEQ, B=None, DB=None):
    nev = (depth + 1) // 2
    nod = depth // 2
    B = B or min(ncores, 4)
    DB = DB or ncores
    pr = [results[b] for b in range(B)]
    sr = [results[c] for c in range(DB)]

    def st(rs, key, shape_tail, lead):
        a = np.stack([np.asarray(r[key]) for r in rs], axis=1)
        return np.ascontiguousarray(a.reshape((lead, len(rs)) + shape_tail)).astype(np.float32)
    outs = [
        np.stack([np.asarray(r["y_p"]) for r in pr]).astype(np.float32),
        np.stack([np.asarray(r["y_s"]) for r in sr]).astype(np.float32),
        st(pr, "fox_k_p", (SEQ, 8, 128), nev), st(pr, "fox_v_p", (SEQ, 8, 128), nev), st(pr, "fox_logf_p", (SEQ, 8), nev),
        st(pr, "mem_k_p", (256, 4, 128), depth), st(pr, "mem_v_p", (256, 4, 128), depth),
        st(pr, "ssd_p", (32, 64, 128), nev), st(pr, "conv_p", (3, 3072), nev),
        st(pr, "s5_re_p", (128, 64), nod), st(pr, "s5_im_p", (128, 64), nod),
        st(sr, "fox_k_s", (8, 8, 128), nev), st(sr, "fox_v_s", (8, 8, 128), nev), st(sr, "fox_logf_s", (8, 8), nev),
        st(sr, "ssd_s", (32, 64, 128), nev), st(sr, "conv_s", (3, 3072), nev),
        st(sr, "s5_re_s", (128, 64), nod), st(sr, "s5_im_s", (128, 64), nod),
    ]
    return tuple(outs)


def kernel(**inp):
    depth = 4
    SEQ = int(np.asarray(inp["x_prompt"]).shape[1])
    n_pool = int(np.asarray(inp["cache_fox_k"]).shape[1])
    n_pages = int(np.asarray(inp["page_table"]).shape[1])
    cfg = dict(T=256, SEQ=SEQ, depth=depth, sample=True, n_pages=n_pages, n_pool=n_pool)
    key = (SEQ, n_pool, n_pages)
    if _CACHE.get("key") != key:
        _CACHE["nc"] = Prog(cfg).build()
        _CACHE["key"] = key
    nc = _CACHE["nc"]
    ncores = 8
    in_maps = make_in_maps(inp, ncores, depth)
    res = run_bass_kernel_spmd(nc, in_maps, core_ids=list(range(ncores)))
    return assemble(res.results, ncores, depth, SEQ, B=int(np.asarray(inp["x_prompt"]).shape[0]),
                    DB=int(np.asarray(inp["x_sample"]).shape[0]))
```
